# Optimizing a Trainium2 kernel written in Bass

```python
import math
import jax, jax.numpy as jnp
from jax import lax
import numpy as np

D_MODEL = 1024
BATCH = 16
SEQ = 2048
DEPTH = 2

CTX_LEN = 256
GRID_W = 64
HEAD_DIM = 64
GROUP_WIDTH = D_MODEL // 4
N_MOD = 6
EPS = 1e-6

ATT_HEADS = GROUP_WIDTH // HEAD_DIM
ATT_KV_HEADS = ATT_HEADS // 2
ATT_GROUP = ATT_HEADS // ATT_KV_HEADS
Q_BLOCK = 128
ROPE_BASE = 10000.0

S5_CH = GROUP_WIDTH
S5_GROUP_CH = 16
S5_GROUPS = S5_CH // S5_GROUP_CH
S5_STATE = 64
S5_DT_MIN = 1e-3
S5_DT_MAX = 1e-1
S5_MIN_DECAY = 1e-4

NA_HEADS = GROUP_WIDTH // HEAD_DIM
NA_ROWS = 8
NA_COLS = 16

FN_CH = GROUP_WIDTH
FN_GROUPS = 4
FN_GROUP_CH = FN_CH // FN_GROUPS

D_FF = 4 * D_MODEL

ATT_Q_W = ATT_HEADS * HEAD_DIM
ATT_KV_W = ATT_KV_HEADS * HEAD_DIM
NA_W = NA_HEADS * HEAD_DIM
OFF_ATT_Q = 0
OFF_ATT_K = OFF_ATT_Q + ATT_Q_W
OFF_ATT_V = OFF_ATT_K + ATT_KV_W
OFF_S5 = OFF_ATT_V + ATT_KV_W
OFF_NA_Q = OFF_S5 + S5_CH
OFF_NA_K = OFF_NA_Q + NA_W
OFF_NA_V = OFF_NA_K + NA_W
OFF_FN = OFF_NA_V + NA_W
IN_WIDTH = OFF_FN + FN_CH
MIX_OUT = ATT_Q_W + S5_CH + NA_W + FN_CH

kernel_name = 'hybrid_parallel_group_dit_block'


def rms_norm(x, g):
    x32 = x.astype(jnp.float32)
    y = x32 * lax.rsqrt(jnp.mean(x32 * x32, axis=-1, keepdims=True) + EPS)
    return (y * g.astype(jnp.float32)).astype(x.dtype)


def modulate(x, shift, scale):
    return x * (1 + scale) + shift


def heads(t, lo, hi, nh):
    return t[..., lo:hi].reshape(t.shape[0], t.shape[1], nh, HEAD_DIM)


def rope_axis(x, pos):
    half = x.shape[-1] // 2
    inv = ROPE_BASE ** (-jnp.arange(half, dtype=jnp.float32) / half)
    ang = pos.astype(jnp.float32)[:, None] * inv[None, :]
    cos = jnp.cos(ang)[None, :, None, :]
    sin = jnp.sin(ang)[None, :, None, :]
    x1, x2 = x[..., :half], x[..., half:]
    return jnp.concatenate([x1 * cos - x2 * sin, x1 * sin + x2 * cos], axis=-1)


def rope_2d(x, row, col):
    x32 = x.astype(jnp.float32)
    r = x.shape[-1] // 2
    out = jnp.concatenate([rope_axis(x32[..., :r], row), rope_axis(x32[..., r:], col)], axis=-1)
    return out.astype(x.dtype)


def gqa_mixer(z, zc, g_q, g_k, row, col, need_ctx):
    b, n, _ = z.shape
    lc = zc.shape[1]
    scale = HEAD_DIM ** -0.5
    q = rope_2d(rms_norm(heads(z, OFF_ATT_Q, OFF_ATT_K, ATT_HEADS), g_q), row, col) * scale
    q = q.reshape(b, n, ATT_KV_HEADS, ATT_GROUP, HEAD_DIM)
    k = rope_2d(rms_norm(heads(z, OFF_ATT_K, OFF_ATT_V, ATT_KV_HEADS), g_k), row, col)
    v = heads(z, OFF_ATT_V, OFF_S5, ATT_KV_HEADS)
    kc = rms_norm(heads(zc, OFF_ATT_K, OFF_ATT_V, ATT_KV_HEADS), g_k)
    vc = heads(zc, OFF_ATT_V, OFF_S5, ATT_KV_HEADS)
    k_all = jnp.concatenate([k, kc], axis=1)
    v_all = jnp.concatenate([v, vc], axis=1)

    def attend(qi, kk, vv):
        s = jnp.einsum('bqkgd,bskd->bkgqs', qi, kk).astype(jnp.float32)
        p = jax.nn.softmax(s, axis=-1).astype(vv.dtype)
        return jnp.einsum('bkgqs,bskd->bqkgd', p, vv)

    nblk = n // Q_BLOCK
    qb = q.reshape(b, nblk, Q_BLOCK, ATT_KV_HEADS, ATT_GROUP, HEAD_DIM).transpose(1, 0, 2, 3, 4, 5)
    o = lax.map(lambda qi: attend(qi, k_all, v_all), qb)
    o = o.transpose(1, 0, 2, 3, 4, 5).reshape(b, n, ATT_Q_W)
    oc = None
    if need_ctx:
        qc = rms_norm(heads(zc, OFF_ATT_Q, OFF_ATT_K, ATT_HEADS), g_q) * scale
        qc = qc.reshape(b, lc, ATT_KV_HEADS, ATT_GROUP, HEAD_DIM)
        oc = attend(qc, kc, vc).reshape(b, lc, ATT_Q_W)
    return o, oc


def linear_scan(bu, abar, h0, reverse):
    if h0 is not None:
        idx = bu.shape[1] - 1 if reverse else 0
        bu = bu.at[:, idx].add(abar * h0)
    a = jnp.broadcast_to(abar, bu.shape)

    def combine(e1, e2):
        a1, b1 = e1
        a2, b2 = e2
        return a1 * a2, a2 * b1 + b2

    _, h = lax.associative_scan(combine, (a, bu), reverse=reverse, axis=1)
    return h


def s5_mixer(z, zc, a_re, a_im, log_dt, b_re, b_im, c_re, c_im, d_skip, w_glu, need_ctx):
    f32 = jnp.float32
    b, n, _ = z.shape
    lc = zc.shape[1]
    u = z[..., OFF_S5:OFF_NA_Q].astype(f32)
    uc = zc[..., OFF_S5:OFF_NA_Q].astype(f32)
    u_g = u.reshape(b, n, S5_GROUPS, S5_GROUP_CH).astype(jnp.complex64)
    uc_g = uc.reshape(b, lc, S5_GROUPS, S5_GROUP_CH).astype(jnp.complex64)
    d32 = d_skip.astype(f32)
    y = d32 * u
    yc = d32 * uc if need_ctx else None
    for direction in range(2):
        reverse = direction == 1
        lam = lax.complex(jnp.minimum(a_re[direction].astype(f32), -S5_MIN_DECAY), a_im[direction].astype(f32))
        dt = jnp.exp(log_dt[direction].astype(f32))[:, None]
        abar = jnp.exp(lam * dt)
        bmat = lax.complex(b_re[direction].astype(f32), b_im[direction].astype(f32))
        bbar = ((abar - 1) / lam)[..., None] * bmat
        cmat = lax.complex(c_re[direction].astype(f32), c_im[direction].astype(f32))
        hc = linear_scan(jnp.einsum('blgh,gph->blgp', uc_g, bbar), abar, None, reverse)
        h0 = hc[:, 0] if reverse else hc[:, -1]
        h = linear_scan(jnp.einsum('blgh,gph->blgp', u_g, bbar), abar, h0, reverse)
        y = y + jnp.einsum('blgp,ghp->blgh', h, cmat).real.reshape(b, n, S5_CH)
        if need_ctx:
            yc = yc + jnp.einsum('blgp,ghp->blgh', hc, cmat).real.reshape(b, lc, S5_CH)

    def glu(t):
        t = jax.nn.gelu(t).astype(z.dtype)
        return t * jax.nn.sigmoid(t @ w_glu)

    return glu(y), (glu(yc) if need_ctx else None)


def na_mixer(z, zc, rel_bias, need_ctx):
    b, n, _ = z.shape
    rows = n // GRID_W
    k_r = min(NA_ROWS, rows)
    scale = HEAD_DIM ** -0.5
    q = heads(z, OFF_NA_Q, OFF_NA_K, NA_HEADS) * scale
    k = heads(z, OFF_NA_K, OFF_NA_V, NA_HEADS)
    v = heads(z, OFF_NA_V, OFF_FN, NA_HEADS)
    kc = heads(zc, OFF_NA_K, OFF_NA_V, NA_HEADS)
    vc = heads(zc, OFF_NA_V, OFF_FN, NA_HEADS)
    q_grid = q.reshape(b, rows, GRID_W, NA_HEADS, HEAD_DIM)
    k_grid = k.reshape(b, rows, GRID_W, NA_HEADS, HEAD_DIM)
    v_grid = v.reshape(b, rows, GRID_W, NA_HEADS, HEAD_DIM)
    cols = jnp.arange(GRID_W)
    col_start = jnp.clip(cols - NA_COLS // 2, 0, GRID_W - NA_COLS)
    col_idx = col_start[:, None] + jnp.arange(NA_COLS)[None, :]
    rel_c = col_idx - cols[:, None] + (NA_COLS - 1)
    bias_c = rel_bias[:, :, rel_c]
    n_loc = k_r * NA_COLS

    def row_block(r):
        rs = jnp.clip(r - k_r // 2, 0, rows - k_r)
        qr = lax.dynamic_index_in_dim(q_grid, r, axis=1, keepdims=False)
        kb = lax.dynamic_slice_in_dim(k_grid, rs, k_r, axis=1)[:, :, col_idx]
        vb = lax.dynamic_slice_in_dim(v_grid, rs, k_r, axis=1)[:, :, col_idx]
        rel_r = rs + jnp.arange(k_r) - r + (NA_ROWS - 1)
        bias = bias_c[:, rel_r].transpose(0, 2, 1, 3).astype(jnp.float32)
        s_loc = jnp.einsum('bjhd,bajkhd->bhjak', qr, kb).astype(jnp.float32) + bias[None]
        s_loc = s_loc.reshape(b, NA_HEADS, GRID_W, n_loc)
        s_ctx = jnp.einsum('bjhd,bshd->bhjs', qr, kc).astype(jnp.float32)
        p = jax.nn.softmax(jnp.concatenate([s_loc, s_ctx], axis=-1), axis=-1).astype(vb.dtype)
        p_loc = p[..., :n_loc].reshape(b, NA_HEADS, GRID_W, k_r, NA_COLS)
        p_ctx = p[..., n_loc:]
        return jnp.einsum('bhjak,bajkhd->bjhd', p_loc, vb) + jnp.einsum('bhjs,bshd->bjhd', p_ctx, vc)

    o = lax.map(row_block, jnp.arange(rows))
    o = o.transpose(1, 0, 2, 3, 4).reshape(b, n, NA_W)
    oc = None
    if need_ctx:
        qc = heads(zc, OFF_NA_Q, OFF_NA_K, NA_HEADS) * scale
        s = jnp.einsum('bqhd,bshd->bhqs', qc, kc).astype(jnp.float32)
        p = jax.nn.softmax(s, axis=-1).astype(vc.dtype)
        oc = jnp.einsum('bhqs,bshd->bqhd', p, vc).reshape(zc.shape[0], zc.shape[1], NA_W)
    return o, oc


def fourier_mixer(t, w_fnet, b_fnet):
    b, n, _ = t.shape
    u = t[..., OFF_FN:IN_WIDTH].astype(jnp.float32).reshape(b, n, FN_GROUPS, FN_GROUP_CH)
    f = jnp.fft.fft2(u, axes=(1, 3), norm='ortho').real.reshape(b, n, FN_CH).astype(t.dtype)
    return f @ w_fnet + b_fnet


def sq_relu_mlp(h, w1, w2):
    return jnp.square(jax.nn.relu(h @ w1)) @ w2


def trunk_layer(x, xc, c, c_ctx, row, col, w_ada, b_ada, g_pre_mix, g_post_mix, g_pre_mlp, g_post_mlp,
                w_in, g_q_attn, g_k_attn, s5_a_re, s5_a_im, s5_log_dt, s5_b_re, s5_b_im, s5_c_re, s5_c_im,
                s5_d, w_s5_glu, na_rel_bias, w_fnet, b_fnet, w_out, w_mlp1, w_mlp2, need_ctx):
    b = x.shape[0]
    mod = (jax.nn.silu(c) @ w_ada + b_ada).reshape(b, N_MOD, D_MODEL)
    sh1, sc1, g1, sh2, sc2, g2 = [mod[:, i, None, :] for i in range(N_MOD)]
    mod_c = (jax.nn.silu(c_ctx) @ w_ada + b_ada).reshape(N_MOD, D_MODEL)
    csh1, csc1, cg1, csh2, csc2, cg2 = [mod_c[i] for i in range(N_MOD)]

    z = modulate(rms_norm(x, g_pre_mix), sh1, sc1) @ w_in
    zc = modulate(rms_norm(xc, g_pre_mix), csh1, csc1) @ w_in
    oa, oac = gqa_mixer(z, zc, g_q_attn, g_k_attn, row, col, need_ctx)
    ob, obc = s5_mixer(z, zc, s5_a_re, s5_a_im, s5_log_dt, s5_b_re, s5_b_im, s5_c_re, s5_c_im,
                       s5_d, w_s5_glu, need_ctx)
    on, onc = na_mixer(z, zc, na_rel_bias, need_ctx)
    od = fourier_mixer(z, w_fnet, b_fnet)
    y = jnp.concatenate([oa, ob, on, od], axis=-1) @ w_out
    x = x + g1 * rms_norm(y, g_post_mix)
    h = modulate(rms_norm(x, g_pre_mlp), sh2, sc2)
    x = x + g2 * rms_norm(sq_relu_mlp(h, w_mlp1, w_mlp2), g_post_mlp)

    if need_ctx:
        odc = fourier_mixer(zc, w_fnet, b_fnet)
        yc = jnp.concatenate([oac, obc, onc, odc], axis=-1) @ w_out
        xc = xc + cg1 * rms_norm(yc, g_post_mix)
        hc = modulate(rms_norm(xc, g_pre_mlp), csh2, csc2)
        xc = xc + cg2 * rms_norm(sq_relu_mlp(hc, w_mlp1, w_mlp2), g_post_mlp)
    return x, xc


def setup_inputs(seed: int = 0) -> dict:
    key = jax.random.key(seed)
    ks = jax.random.split(key, 32)
    f32 = jnp.float32
    L = DEPTH

    def nrm(k, shape, scale):
        return jax.random.normal(k, shape, f32) * scale

    n_idx = jnp.arange(S5_STATE, dtype=f32)
    s5_shape = (L, 2, S5_GROUPS, S5_STATE)
    return {
        'x': nrm(ks[0], (BATCH, SEQ, D_MODEL), 1.0),
        'c': nrm(ks[1], (BATCH, D_MODEL), 1.0),
        'ctx': nrm(ks[2], (BATCH, CTX_LEN, D_MODEL), 1.0),
        'c_ctx': nrm(ks[3], (D_MODEL,), 1.0),
        'w_ada': nrm(ks[4], (L, D_MODEL, N_MOD * D_MODEL), 0.5 * D_MODEL ** -0.5),
        'b_ada': nrm(ks[5], (L, N_MOD * D_MODEL), 0.01),
        'g_pre_mix': 1.0 + nrm(ks[6], (L, D_MODEL), 0.02),
        'g_post_mix': 1.0 + nrm(ks[7], (L, D_MODEL), 0.02),
        'g_pre_mlp': 1.0 + nrm(ks[8], (L, D_MODEL), 0.02),
        'g_post_mlp': 1.0 + nrm(ks[9], (L, D_MODEL), 0.02),
        'w_in': nrm(ks[10], (L, D_MODEL, IN_WIDTH), D_MODEL ** -0.5),
        'g_q_attn': 1.0 + nrm(ks[11], (L, HEAD_DIM), 0.02),
        'g_k_attn': 1.0 + nrm(ks[12], (L, HEAD_DIM), 0.02),
        's5_a_re': -0.5 + nrm(ks[13], s5_shape, 0.01),
        's5_a_im': math.pi * n_idx + nrm(ks[14], s5_shape, 0.01),
        's5_log_dt': jax.random.uniform(ks[15], (L, 2, S5_GROUPS), f32, math.log(S5_DT_MIN), math.log(S5_DT_MAX)),
        's5_b_re': nrm(ks[16], (L, 2, S5_GROUPS, S5_STATE, S5_GROUP_CH), (2 * S5_GROUP_CH) ** -0.5),
        's5_b_im': nrm(ks[17], (L, 2, S5_GROUPS, S5_STATE, S5_GROUP_CH), (2 * S5_GROUP_CH) ** -0.5),
        's5_c_re': nrm(ks[18], (L, 2, S5_GROUPS, S5_GROUP_CH, S5_STATE), S5_STATE ** -0.5),
        's5_c_im': nrm(ks[19], (L, 2, S5_GROUPS, S5_GROUP_CH, S5_STATE), S5_STATE ** -0.5),
        's5_d': nrm(ks[20], (L, S5_CH), 1.0),
        'w_s5_glu': nrm(ks[21], (L, S5_CH, S5_CH), S5_CH ** -0.5),
        'na_rel_bias': nrm(ks[22], (L, NA_HEADS, 2 * NA_ROWS - 1, 2 * NA_COLS - 1), 0.02),
        'w_fnet': nrm(ks[23], (L, FN_CH, FN_CH), FN_CH ** -0.5),
        'b_fnet': nrm(ks[24], (L, FN_CH), 0.01),
        'w_out': nrm(ks[25], (L, MIX_OUT, D_MODEL), MIX_OUT ** -0.5),
        'w_mlp1': nrm(ks[26], (L, D_MODEL, D_FF), D_MODEL ** -0.5),
        'w_mlp2': nrm(ks[27], (L, D_FF, D_MODEL), D_FF ** -0.5),
    }


def reference(x, c, ctx, c_ctx, w_ada, b_ada, g_pre_mix, g_post_mix, g_pre_mlp, g_post_mlp, w_in,
              g_q_attn, g_k_attn, s5_a_re, s5_a_im, s5_log_dt, s5_b_re, s5_b_im, s5_c_re, s5_c_im,
              s5_d, w_s5_glu, na_rel_bias, w_fnet, b_fnet, w_out, w_mlp1, w_mlp2):
    n = x.shape[1]
    pos = jnp.arange(n, dtype=jnp.int32)
    row = pos // GRID_W
    col = pos % GRID_W
    xc = ctx
    for l in range(DEPTH):
        need_ctx = l < DEPTH - 1
        x, xc = trunk_layer(x, xc, c, c_ctx, row, col, w_ada[l], b_ada[l], g_pre_mix[l], g_post_mix[l],
                            g_pre_mlp[l], g_post_mlp[l], w_in[l], g_q_attn[l], g_k_attn[l],
                            s5_a_re[l], s5_a_im[l], s5_log_dt[l], s5_b_re[l], s5_b_im[l], s5_c_re[l], s5_c_im[l],
                            s5_d[l], w_s5_glu[l], na_rel_bias[l], w_fnet[l], b_fnet[l], w_out[l],
                            w_mlp1[l], w_mlp2[l], need_ctx)
    return x
```

```python
import math
import contextlib
import threading
import numpy as np
import concourse.bass as bass
import concourse.mybir as mybir
from concourse.bass_utils import run_bass_kernel_spmd

F32 = mybir.dt.float32
BF16 = mybir.dt.bfloat16
ALU = mybir.AluOpType
AF = mybir.ActivationFunctionType
AX = mybir.AxisListType


class Buf:
    __slots__ = ("name", "t", "lw", "rd")

    def __init__(self, name, t=None):
        self.name = name
        self.t = t
        self.lw = None
        self.rd = {}

    def __getitem__(self, key):
        return self.t[key]


class Alias:
    def __init__(self, parent, ap):
        self.parent, self.t, self.name = parent, ap, parent.name + "_alias"

    def __getitem__(self, key):
        return self.t[key]

    lw = property(lambda self: self.parent.lw, lambda self, v: setattr(self.parent, "lw", v))
    rd = property(lambda self: self.parent.rd, lambda self, v: setattr(self.parent, "rd", v))


class Side:
    def __init__(self, kb, fn, every=4):
        self.kb, self.fn, self.every = kb, fn, every
        self.go, self.back = threading.Event(), threading.Event()
        self.done, self.err, self.n = False, None, 0
        self.thread = threading.Thread(target=self._run, daemon=True)
        self.thread.start()

    def _run(self):
        self.go.wait()
        self.go.clear()
        try:
            self.fn()
        except BaseException as e:
            self.err = e
        finally:
            self.done = True
            self.back.set()

    def step(self):
        if self.done:
            return False
        prev = self.kb.coro
        self.kb.coro = self
        self.n = 0
        self.go.set()
        self.back.wait()
        self.back.clear()
        self.kb.coro = prev
        if self.err is not None:
            raise self.err
        return not self.done

    def tick(self):
        self.n += 1
        if self.n >= self.every:
            self.n = 0
            self.back.set()
            self.go.wait()
            self.go.clear()

    def gen(self):
        while self.step():
            yield

    def drain(self):
        while self.step():
            pass


class KB:
    ROT = 20000

    def __init__(self, nc):
        self.nc = nc
        self.es = contextlib.ExitStack()
        self.eng = {"pe": nc.tensor, "act": nc.scalar, "dve": nc.vector, "pool": nc.gpsimd, "sp": nc.sync}
        self.sems = {}
        self.cur = {}
        self.cnt = {}
        self.seen = {e: {} for e in self.eng}
        self.pend = {e: ([], []) for e in self.eng}
        self.nsem = 0
        for e in self.eng:
            self._newsem(e)
        self.dslots = []
        self.dpool = {"hw": [], "sw": []}
        self.dnext = {"hw": 0, "sw": 0}
        for kind, n in (("hw", 18), ("sw", 8)):
            for i in range(n):
                k = "dma%s%d" % (kind, i)
                self.sems[k] = self.es.enter_context(nc.semaphore(k))
                self.cnt[k] = 0
                self.dslots.append(k)
                self.dpool[kind].append(k)
        self.ninst = 0
        self.coro = None

    def _newsem(self, e):
        k = "%s_%d" % (e, self.nsem)
        self.nsem += 1
        self.sems[k] = self.es.enter_context(self.nc.semaphore(k))
        self.cnt[k] = 0
        self.cur[e] = k

    def sb(self, name, shape, dt=F32):
        self.nsb = getattr(self, "nsb", 0) + 1
        name = "%s_%d" % (name, self.nsb)
        t = self.es.enter_context(self.nc.sbuf_tensor(name, list(shape), dt))
        return Buf(name, t)

    def dram(self, name, shape, dt, kind="Internal"):
        t = self.nc.dram_tensor(name, list(shape), dt, kind=kind)
        return Buf(name, t)

    def _wait(self, e, dep):
        if dep is None:
            return
        k, v = dep
        if self.seen[e].get(k, 0) >= v:
            return
        self.eng[e].wait_ge(self.sems[k], v)
        self.seen[e][k] = v

    def _deps(self, e, r, w):
        for b in r:
            self._wait(e, b.lw)
        for b in w:
            self._wait(e, b.lw)
            for d in list(b.rd.items()):
                self._wait(e, d)

    def op(self, e, fn, r=(), w=(), inc=True):
        self._deps(e, r, w)
        ins = fn()
        self.ninst += 1
        pr, pw = self.pend[e]
        pr.extend(r)
        pw.extend(w)
        if not inc:
            return
        k = self.cur[e]
        self.cnt[k] += 1
        v = self.cnt[k]
        ins.then_inc(self.sems[k], 1)
        self.seen[e][k] = max(self.seen[e].get(k, 0), 0)
        tag = (k, v)
        for b in pr:
            b.rd[k] = v
        for b in pw:
            b.lw = tag
            b.rd = {}
        self.pend[e] = ([], [])
        if v >= self.ROT:
            self._newsem(e)
        if self.coro is not None and threading.current_thread() is self.coro.thread:
            self.coro.tick()

    def dma(self, q, out_ap, in_ap, r=(), w=(), **kw):
        self._deps(q, r, w)
        kind = "sw" if q == "pool" else "hw"
        k = self.dpool[kind][self.dnext[kind]]
        self.dnext[kind] = (self.dnext[kind] + 1) % len(self.dpool[kind])
        if self.cnt[k] > 0:
            self._wait(q, (k, self.cnt[k]))
        ins = self.eng[q].dma_start(out=out_ap, in_=in_ap, **kw)
        self.ninst += 1
        self.cnt[k] += 16
        ins.then_inc(self.sems[k], 16)
        tag = (k, self.cnt[k])
        for b in r:
            b.rd[k] = self.cnt[k]
        for b in w:
            b.lw = tag
            b.rd = {}
        if self.coro is not None and threading.current_thread() is self.coro.thread:
            self.coro.tick()
        return tag

    def finish(self, bufs):
        for b in bufs:
            self._wait("sp", b.lw)
            for d in list(b.rd.items()):
                self._wait("sp", d)

    @contextlib.contextmanager
    def scope(self):
        outer = self.es
        self.es = contextlib.ExitStack()
        try:
            yield
        finally:
            self.barrier()
            self.es.close()
            self.es = outer

    def barrier(self):
        tags = []
        for e in self.eng:
            for key in [kk for kk in self.sems if kk.startswith(e + "_")]:
                if self.cnt[key] > 0:
                    tags.append((key, self.cnt[key]))
        for key in self.dslots:
            if self.cnt[key] > 0:
                tags.append((key, self.cnt[key]))
        for e in self.eng:
            for t in tags:
                self._wait(e, t)


D = 1024
S = 2048
CTX = 256
T = CTX + S
NB = 2
DEPTH = 2
DFF = 4096
EPS = 1e-6
NFM = 1280
NTM = 896
TBLK = [(0, 256, 2)] + [(256 + 512 * i, 512, None) for i in range(4)]


class Prog:
    def __init__(self, dbg=None, nlayers=DEPTH, nbatch=NB, skip=()):
        self.dbg = dbg
        self.skip = set(skip)
        nc = bass.Bass("TRN2", target_bir_lowering=False)
        self.nc = nc
        k = KB(nc)
        self.k = k
        self.nlayers = nlayers
        self.nbatch = nbatch
        ein = lambda n, s, d=F32: Buf(n, nc.dram_tensor(n, list(s), d, kind="ExternalInput"))
        self.xin = ein("xin", [NB, D, T])
        self.cT = ein("cT", [128, 8, 3])
        self.w_ada = ein("w_ada", [DEPTH, D, 6 * D])
        self.bada = ein("bada", [DEPTH, 128, 48])
        self.gvec = ein("gvec", [DEPTH, 128, 4, 8])
        self.w_fm = ein("w_fm", [DEPTH, D, NFM])
        self.w_tm = ein("w_tm", [DEPTH, D, NTM])
        self.rope = ein("rope", [2, 128, T])
        self.gqk = ein("gqk", [DEPTH, 128, 4])
        self.cbf = ein("cbf", [128, 6, 128], BF16)
        self.dftn = ein("dftn", [2, S, S], BF16)
        self.dftc = ein("dftc", [2, CTX, CTX], BF16)
        self.nbias = ein("nbias", [DEPTH, 4, 20, 128, 512])
        self.s5p = ein("s5p", [DEPTH, 128, 32, 3])
        self.s5c = ein("s5c", [DEPTH, 128, 32, 2, 16])
        self.s5b = ein("s5b", [DEPTH, 128, 32, 2, 16])
        self.s5d = ein("s5d", [DEPTH, 128, 16])
        self.cf32 = ein("cf32", [128, 392])
        self.w_glu = ein("w_glu", [DEPTH, 256, 256])
        self.s5w = k.dram("s5w", [DEPTH, 32, 4, 128, 128], BF16)
        self.s5t = k.dram("s5t", [DEPTH, 16, 128, 128], BF16)
        self.s5tab = k.dram("s5tab", [DEPTH, 32, 2, 128, 288], F32)
        self.wfm_bf = k.dram("wfm_bf", [DEPTH, 128, 8, NFM], BF16)
        self.wtm_bf = k.dram("wtm_bf", [DEPTH, 128, 8, NTM], BF16)
        self.wout_bf = k.dram("wout_bf", [DEPTH, 128, 8, D], BF16)
        self.w_fnet = ein("w_fnet", [DEPTH, 256, 256])
        self.bfn = ein("bfn", [DEPTH, 128, 2])
        self.w_out = ein("w_out", [DEPTH, D, D])
        self.w_mlp1 = ein("w_mlp1", [DEPTH, D, DFF])
        self.w_mlp2 = ein("w_mlp2", [DEPTH, DFF, D])
        self.out = Buf("outT", nc.dram_tensor("outT", [NB, D, S], F32, kind="ExternalOutput"))
        self.gqT = k.dram("gqT", [384, T], BF16)
        self.nqk = k.dram("nqk", [512, T], BF16)
        self.ztm = k.dram("ztm", [T, NTM], BF16)
        self.mixT = k.dram("mixT", [D, T], BF16)
        self.xA = k.dram("xA", [D, T], F32)
        self.xB = k.dram("xB", [D, T], F32)
        self.yscr = k.dram("yscr", [D, T], F32)
        if dbg:
            self.dbgout = {n: Buf(n, nc.dram_tensor(n, list(s), d, kind="ExternalOutput")) for n, (s, d) in dbg.items()}
        pst = k.es.enter_context(nc.psum_tensor("ps", [128, 4096], F32))
        self.pst = pst
        self.psb = [Buf("psb%d" % i, pst) for i in range(8)]
        self.bank_rr = 0
        self.build()

    def bank(self):
        bs = getattr(self, "bank_set", None)
        if bs:
            self.bs_rr = getattr(self, "bs_rr", 0) + 1
            return bs[self.bs_rr % len(bs)]
        b = self.bank_rr
        self.bank_rr = (self.bank_rr + 1) % getattr(self, "bank_lim", 8)
        return b

    def pap(self, b, n=512, p0=0, p1=128):
        return self.pst[p0:p1, b * 512:b * 512 + n]

    def mm(self, out_ap, wbufs, pairs, rbufs):
        k, nc = self.k, self.nc
        n = len(pairs)
        for i, (l, r) in enumerate(pairs):
            k.op("pe", lambda l=l, r=r, i=i: nc.tensor.matmul(out_ap, lhsT=l, rhs=r, start=(i == 0), stop=(i == n - 1)),
                 r=rbufs, w=wbufs, inc=(i == n - 1))

    def mm_g(self, out_ap, wbufs, pairs, rbufs, every=4):
        k, nc = self.k, self.nc
        n = len(pairs)
        for i, (l, r) in enumerate(pairs):
            k.op("pe", lambda l=l, r=r, i=i: nc.tensor.matmul(out_ap, lhsT=l, rhs=r, start=(i == 0), stop=(i == n - 1)),
                 r=rbufs, w=wbufs, inc=(i == n - 1))
            if (i + 1) % every == 0 and i != n - 1:
                yield

    def build(self):
        k, nc = self.k, self.nc
        self.C = k.sb("cbf_sb", [128, 6, 128], BF16)
        k.dma("sp", self.C[:, :, :], self.cbf[:, :, :], r=[self.cbf], w=[self.C])
        self.IDENT = self.C[:, 0, :]
        self.JREV = self.C[:, 1, :]
        self.ONES = self.C[:, 2, :]
        self.BONES = self.C[:, 3, :]
        self.CC = self.C[:, 4, :]
        self.SCN = self.C[:, 5, :]
        self.mod = k.sb("mod", [128, DEPTH, 6, 8, 3])
        self.cols = k.sb("cols", [128, DEPTH, 6, 8, 3])
        self.gv_sb = k.sb("gvec_sb", [128, DEPTH, 4, 8])
        self.gqk_sb = k.sb("gqk_sb", [128, DEPTH, 4])
        self.bfn_sb = k.sb("bfn_sb", [128, DEPTH, 2])
        for l in range(DEPTH):
            k.dma("sp", self.gv_sb[:, l, :, :], self.gvec[l, :, :, :], r=[self.gvec], w=[self.gv_sb])
            k.dma("sp", self.gqk_sb[:, l, :], self.gqk[l, :, :], r=[self.gqk], w=[self.gqk_sb])
            k.dma("sp", self.bfn_sb[:, l, :], self.bfn[l, :, :], r=[self.bfn], w=[self.bfn_sb])
        self.CFb = k.sb("cf32_sb", [128, 392])
        k.dma("sp", self.CFb[:, :], self.cf32[:, :], r=[self.cf32], w=[self.CFb])
        self.CF = self.CFb
        self.IDF = self.CFb[:, 264:392]
        self.rdec = k.sb("rdec", [128, DEPTH, 32])
        s5on = "s5" not in self.skip
        self.ph_adaln(side=Side(k, lambda: self.ph_s5prep(0), every=6) if s5on else None)
        self.prep1_pending = s5on and self.nlayers > 1
        if self.prep1_pending and "gqa" in self.skip:
            self.ph_s5prep(1)
            self.prep1_pending = False
        for b in range(self.nbatch):
            for l in range(self.nlayers):
                self.layer(b, l)
        fin = [self.out] + (list(self.dbgout.values()) if self.dbg else [])
        k.barrier()
        k.finish(fin)

    def ph_castw(self):
        k = self.k
        if True:
            st = [k.sb("cast%d" % i, [128, 2, NFM], BF16) for i in range(2)]
            i = 0
            for l in range(self.nlayers):
                for src, dst, ncol in ((self.w_fm, self.wfm_bf, NFM), (self.w_tm, self.wtm_bf, NTM), (self.w_out, self.wout_bf, D)):
                    for k0 in range(0, 8, 2):
                        t = st[i % 2]
                        i += 1
                        k.dma("pool", t[:, :, :ncol], src[l, k0 * 128:(k0 + 2) * 128, :].rearrange("(kt p) c -> p kt c", p=128), r=[src], w=[t])
                        k.dma("sp", dst[l, :, k0:k0 + 2, :], t[:, :, :ncol], r=[t], w=[dst])
                        yield

    def ph_adaln(self, side=None):
        k, nc = self.k, self.nc
        with k.scope():
            c_sb = k.sb("c_sb", [128, 8, 3])
            cs = k.sb("cs", [128, 8, 3], BF16)
            sg = k.sb("sg", [128, 8, 3])
            bada_sb = k.sb("bada_sb", [128, DEPTH, 48])
            k.dma("sp", c_sb[:, :, :], self.cT[:, :, :], r=[self.cT], w=[c_sb])
            for l in range(DEPTH):
                k.dma("sp", bada_sb[:, l, :], self.bada[l, :, :], r=[self.bada], w=[bada_sb])
            k.op("act", lambda: nc.scalar.activation(out=sg[:, :, :], in_=c_sb[:, :, :], func=AF.Sigmoid), r=[c_sb], w=[sg])
            k.op("dve", lambda: nc.vector.tensor_tensor(out=cs[:, :, :], in0=c_sb[:, :, :], in1=sg[:, :, :], op=ALU.mult), r=[c_sb, sg], w=[cs])
            wb = [k.sb("wada%d" % i, [128, 8, 512], BF16) for i in range(2)]
            castg = self.ph_castw()
            ci = 0
            for l in range(DEPTH):
                for ch in range(12):
                    w = wb[ci % 2]
                    ci += 1
                    k.dma("pool", w[:, :, :], self.w_ada[l, :, ch * 512:(ch + 1) * 512].rearrange("(kt p) c -> p kt c", p=128),
                          r=[self.w_ada], w=[w])
                    next(castg, None)
                    for s in range(4):
                        ft = ch * 4 + s
                        bk = self.bank()
                        self.mm(self.pap(bk, 3), [self.psb[bk]],
                                [(w[:, kt, s * 128:(s + 1) * 128], cs[:, kt, :]) for kt in range(8)], [w, cs])
                        i6, t8 = ft // 8, ft % 8
                        k.op("dve", lambda bk=bk, l=l, i6=i6, t8=t8, ft=ft: nc.vector.tensor_scalar(
                            out=self.mod[:, l, i6, t8, :], in0=self.pap(bk, 3), scalar1=bada_sb[:, l, ft:ft + 1], scalar2=None, op0=ALU.add),
                            r=[self.psb[bk], bada_sb], w=[self.mod])
                        if side is not None:
                            side.step()
            for _ in castg:
                pass
            for l in range(DEPTH):
                def bc(j):
                    return self.gv_sb[:, l, j, :].unsqueeze(2).to_broadcast([128, 8, 3])
                md, co = self.mod, self.cols
                k.op("dve", lambda l=l: nc.vector.scalar_tensor_tensor(out=co[:, l, 0, :, :], in0=md[:, l, 1, :, :], scalar=1.0, in1=bc(0), op0=ALU.add, op1=ALU.mult), r=[md, self.gv_sb], w=[co])
                k.op("dve", lambda l=l: nc.vector.tensor_copy(out=co[:, l, 1, :, :], in_=md[:, l, 0, :, :]), r=[md], w=[co])
                k.op("dve", lambda l=l: nc.vector.tensor_tensor(out=co[:, l, 2, :, :], in0=md[:, l, 2, :, :], in1=bc(1), op=ALU.mult), r=[md, self.gv_sb], w=[co])
                k.op("dve", lambda l=l: nc.vector.scalar_tensor_tensor(out=co[:, l, 3, :, :], in0=md[:, l, 4, :, :], scalar=1.0, in1=bc(2), op0=ALU.add, op1=ALU.mult), r=[md, self.gv_sb], w=[co])
                k.op("dve", lambda l=l: nc.vector.tensor_copy(out=co[:, l, 4, :, :], in_=md[:, l, 3, :, :]), r=[md], w=[co])
                k.op("dve", lambda l=l: nc.vector.tensor_tensor(out=co[:, l, 5, :, :], in0=md[:, l, 5, :, :], in1=bc(3), op=ALU.mult), r=[md, self.gv_sb], w=[co])
            if side is not None:
                side.drain()

    def col(self, l, which, kt, j):
        return self.cols[:, l, which, kt, j:j + 1]

    def rstd_g(self, SQ, n, scale, out_rs, tmpbuf):
        k, nc = self.k, self.nc
        bk = self.bank()
        self.mm(self.pap(bk, n), [self.psb[bk]], [(self.ONES, SQ[:, kt, :n]) for kt in range(8)], [SQ, self.C])
        yield
        k.op("act", lambda: nc.scalar.activation(out=tmpbuf[:, :n], in_=self.pap(bk, n), func=AF.Ln, scale=scale, bias=self.CF[:, 5:6]),
             r=[self.psb[bk], self.CFb], w=[tmpbuf])
        k.op("act", lambda: nc.scalar.activation(out=out_rs[:, :n], in_=tmpbuf[:, :n], func=AF.Exp, scale=-0.5), r=[tmpbuf], w=[out_rs])
        yield

    def r_norm_g(self, X, n, l, wa, ws, j, Hout, hsl, W, split=False):
        k, nc = self.k, self.nc
        SQ, RS, TMP, XN = W["SQ"], W["RS"], W["TMP"], W["XN"]
        k.op("act", lambda: nc.scalar.activation(out=SQ[:, :, :n], in_=X[:, :, :n], func=AF.Square), r=[X], w=[SQ])
        yield
        yield from self.rstd_g(SQ, n, 1.0 / D, RS, TMP)
        k.op("dve", lambda: nc.vector.tensor_tensor(out=XN[:, :, :n], in0=X[:, :, :n], in1=RS[:, :n].unsqueeze(1).to_broadcast([128, 8, n]), op=ALU.mult),
             r=[X, RS], w=[XN])
        yield
        for kt in range(8):
            if kt % 2 == 0 or not split:
                k.op("act", lambda kt=kt: nc.scalar.activation(out=Hout[:, kt, hsl], in_=XN[:, kt, :n], func=AF.Identity,
                                                               scale=self.col(l, wa, kt, j), bias=self.col(l, ws, kt, j)),
                     r=[XN, self.cols], w=[Hout])
            else:
                k.op("dve", lambda kt=kt: nc.vector.tensor_scalar(out=Hout[:, kt, hsl], in0=XN[:, kt, :n], scalar1=self.col(l, wa, kt, j),
                                                                  scalar2=self.col(l, ws, kt, j), op0=ALU.mult, op1=ALU.add),
                     r=[XN, self.cols], w=[Hout])
            yield

    def r_post_g(self, Y, X, n, l, wg, j, W):
        k, nc = self.k, self.nc
        SQ, RS, TMP, XN = W["SQ"], W["RS"], W["TMP"], W["XN"]
        k.op("act", lambda: nc.scalar.activation(out=SQ[:, :, :n], in_=Y[:, :, :n], func=AF.Square), r=[Y], w=[SQ])
        yield
        yield from self.rstd_g(SQ, n, 1.0 / D, RS, TMP)
        k.op("dve", lambda: nc.vector.tensor_tensor(out=XN[:, :, :n], in0=Y[:, :, :n], in1=RS[:, :n].unsqueeze(1).to_broadcast([128, 8, n]), op=ALU.mult),
             r=[Y, RS], w=[XN])
        yield
        for kt in range(8):
            k.op("dve", lambda kt=kt: nc.vector.scalar_tensor_tensor(out=X[:, kt, :n], in0=XN[:, kt, :n], scalar=self.col(l, wg, kt, j),
                                                                      in1=X[:, kt, :n], op0=ALU.mult, op1=ALU.add),
                 r=[XN, X, self.cols], w=[X])
            yield

    def r_norm(self, *a):
        for _ in self.r_norm_g(*a):
            pass

    def r_post(self, *a):
        for _ in self.r_post_g(*a):
            pass

    @staticmethod
    def weave(main, side, ratio=1):
        for _ in main:
            if side is not None:
                for _r in range(ratio):
                    if next(side, "END") == "END":
                        side = None
                        break
        if side is not None:
            for _ in side:
                pass

    def layer(self, b, l):
        last = (l == DEPTH - 1)
        xsrc = (self.xin, b) if l == 0 else (self.xB, None)
        self.ph_proj(b, l, xsrc)
        merged = ("s5" not in self.skip and "na" not in self.skip)
        es_outer = contextlib.ExitStack()
        s5pre = None
        if merged and "gqa" not in self.skip and not self.prep1_pending:
            es_outer.enter_context(self.k.scope())
            s5pre = self.s5_inputs(b, l)
        for nm, fn_, r0 in (("gqa", self.ph_gqa, 0), ("s5", getattr(self, "ph_s5", None), 256), ("na", getattr(self, "ph_na", None), 512), ("fn", self.ph_fn, 768)):
            if merged and nm == "na":
                continue
            if merged and nm == "s5":
                with self.k.scope():
                    na_loads, run_na = self.na_setup(b, l)
                    self.ph_s5(b, l, after_loads=na_loads, pre=s5pre)
                    if getattr(self, "fn_deferred", False):
                        self.fn_deferred = False
                        run_na(side=self.fn_gen(b, l))
                    else:
                        run_na()
                es_outer.close()
                continue
            if nm in self.skip:
                self.zero_mix(r0, r0 + 256)
            elif nm == "fn" and "gqa" not in self.skip:
                pass
            else:
                fn_(b, l)
        if self.dbg and "d_mix" in self.dbg and b == 0 and l == 0:
            self.dump("d_mix", self.mixT, [D, T], BF16)
        self.ph_out_mlp(b, l, xsrc)

    def xsl(self, xsrc, t0, n):
        buf, bi = xsrc
        ap = buf[bi, :, t0:t0 + n] if bi is not None else buf[:, t0:t0 + n]
        return ap.rearrange("(kt p) t -> p kt t", p=128)

    def ph_proj(self, b, l, xsrc):
        k, nc = self.k, self.nc
        with k.scope():
            Wfm = k.sb("Wfm", [128, 8, NFM], BF16)
            Wtm = k.sb("Wtm", [128, 8, NTM], BF16)
            k.dma("sp", Wfm[:, 0:4, :], self.wfm_bf[l, :, 0:4, :], r=[self.wfm_bf], w=[Wfm])
            k.dma("act", Wfm[:, 4:8, :], self.wfm_bf[l, :, 4:8, :], r=[self.wfm_bf], w=[Wfm])
            k.dma("sp", Wtm[:, :, :], self.wtm_bf[l, :, :, :], r=[self.wtm_bf], w=[Wtm])
            Xs = [k.sb("X%d" % i, [128, 8, 512]) for i in range(2)]
            W = {"SQ": k.sb("SQ", [128, 8, 512], BF16), "RS": k.sb("RS", [128, 512]), "TMP": k.sb("TMP", [128, 512]),
                 "XN": k.sb("XN", [128, 8, 512])}
            Hs = [k.sb("H%d" % i, [128, 8, 512], BF16) for i in range(2)]
            ZQ = k.sb("ZQ", [128, 6, 512])
            ZB = [k.sb("ZB%d" % i, [128, 512], BF16) for i in range(2)]
            RC = [k.sb("RC%d" % i, [128, 2, 512]) for i in range(2)]
            SQh = [k.sb("SQh%d" % i, [128, 512], BF16) for i in range(3)]
            RSh = k.sb("RSh", [128, 512])
            TMh = k.sb("TMh", [128, 512])
            T1 = k.sb("T1", [128, 512])
            T2 = k.sb("T2", [128, 512])
            QO = [k.sb("QO%d" % i, [128, 512], BF16) for i in range(2)]
            ZT = [k.sb("ZT%d" % i, [128, 4, NTM], BF16) for i in range(2)]
            QK = [(0, 2, 0, 1, 0), (1, 3, 0, 1, 128), (4, 5, 2, 3, 256)]

            fuse = (l > 0)
            Ys = [k.sb("Yp%d" % i, [128, 8, 512]) for i in range(2)] if fuse else None

            def load_x(bi):
                if bi < len(TBLK):
                    t0, n, mj = TBLK[bi]
                    if fuse:
                        k.dma("sp", Xs[bi % 2][:, :, :n], self.xA[:, t0:t0 + n].rearrange("(kt p) t -> p kt t", p=128), r=[self.xA], w=[Xs[bi % 2]])
                        k.dma("sp", Ys[bi % 2][:, :, :n], self.yscr[:, t0:t0 + n].rearrange("(kt p) t -> p kt t", p=128), r=[self.yscr], w=[Ys[bi % 2]])
                    else:
                        k.dma("sp", Xs[bi % 2][:, :, :n], self.xsl(xsrc, t0, n), r=[xsrc[0]], w=[Xs[bi % 2]])

            def stage_n(bi):
                t0, n, mj = TBLK[bi]
                j = b if mj is None else mj
                X, rc = Xs[bi % 2], RC[bi % 2]
                load_x(bi + 1)
                k.dma("act", rc[:, :, :n], self.rope[:, :, t0:t0 + n].rearrange("c p t -> p c t"), r=[self.rope], w=[rc])
                yield
                yield
                if fuse:
                    yield from self.r_post_g(Ys[bi % 2], X, n, l - 1, 5, j, W)
                    k.dma("sp", self.xB[:, t0:t0 + n].rearrange("(kt p) t -> p kt t", p=128), X[:, :, :n], r=[X], w=[self.xB])
                yield from self.r_norm_g(X, n, l, 0, 1, j, Hs[bi % 2], slice(0, n), W)

            def stage_p(bi):
                t0, n, mj = TBLK[bi]
                H, rc = Hs[bi % 2], RC[bi % 2]
                for ot in range(10):
                    bk = self.bank()
                    self.mm(self.pap(bk, n), [self.psb[bk]],
                            [(Wfm[:, kt, ot * 128:(ot + 1) * 128], H[:, kt, :n]) for kt in range(8)], [Wfm, H])
                    if ot < 6:
                        k.op("act", lambda ot=ot, bk=bk: nc.scalar.copy(out=ZQ[:, ot, :n], in_=self.pap(bk, n)), r=[self.psb[bk]], w=[ZQ])
                        if ot in (0, 1, 4):
                            sq = SQh[(0, 1, None, None, 2)[ot]]
                            k.op("act", lambda ot=ot, sq=sq: nc.scalar.activation(out=sq[:, :n], in_=ZQ[:, ot, :n], func=AF.Square), r=[ZQ], w=[sq])
                    else:
                        zb = ZB[ot % 2]
                        sc = 0.125 if ot < 8 else 1.0
                        k.op("dve", lambda zb=zb, bk=bk, sc=sc: nc.vector.tensor_scalar(out=zb[:, :n], in0=self.pap(bk, n), scalar1=sc, scalar2=None, op0=ALU.mult),
                             r=[self.psb[bk]], w=[zb])
                        k.dma("sp", self.nqk[(ot - 6) * 128:(ot - 5) * 128, t0:t0 + n], zb[:, :n], r=[zb], w=[self.nqk])
                    yield
                zt_sb = ZT[bi % 2]
                ns = n // 128
                for s_ in range(ns):
                    for (c0, c1) in [(0, 384), (384, 896)]:
                        bk = self.bank()
                        self.mm(self.pap(bk, c1 - c0), [self.psb[bk]],
                                [(H[:, kt, s_ * 128:(s_ + 1) * 128], Wtm[:, kt, c0:c1]) for kt in range(8)], [Wtm, H])
                        if c0 == 0:
                            k.op("act", lambda bk=bk, s_=s_, c0=c0, c1=c1: nc.scalar.copy(out=zt_sb[:, s_, c0:c1], in_=self.pap(bk, c1 - c0)), r=[self.psb[bk]], w=[zt_sb])
                        else:
                            k.op("dve", lambda bk=bk, s_=s_, c0=c0, c1=c1: nc.vector.tensor_copy(out=zt_sb[:, s_, c0:c1], in_=self.pap(bk, c1 - c0)), r=[self.psb[bk]], w=[zt_sb])
                        yield
                k.dma("sp", self.ztm[t0:t0 + n, :].rearrange("(s p) c -> p s c", p=128), zt_sb[:, :ns, :], r=[zt_sb], w=[self.ztm])
                for qi, (zt, zs, gc, gs, row) in enumerate(QK):
                    sq = SQh[qi]
                    bk = self.bank()
                    self.mm(self.pap(bk, n), [self.psb[bk]], [(self.BONES, sq[:, :n])], [sq, self.C])
                    k.op("act", lambda bk=bk: nc.scalar.activation(out=TMh[:, :n], in_=self.pap(bk, n), func=AF.Ln, scale=1.0 / 64, bias=self.CF[:, 5:6]),
                         r=[self.psb[bk], self.CFb], w=[TMh])
                    k.op("act", lambda: nc.scalar.activation(out=RSh[:, :n], in_=TMh[:, :n], func=AF.Exp, scale=-0.5), r=[TMh], w=[RSh])
                    k.op("dve", lambda zt=zt, gc=gc: nc.vector.scalar_tensor_tensor(out=T1[:, :n], in0=ZQ[:, zt, :n], scalar=self.gqk_sb[:, l, gc:gc + 1],
                                                                              in1=rc[:, 0, :n], op0=ALU.mult, op1=ALU.mult), r=[ZQ, rc, self.gqk_sb], w=[T1])
                    k.op("dve", lambda zs=zs, gs=gs: nc.vector.scalar_tensor_tensor(out=T2[:, :n], in0=ZQ[:, zs, :n], scalar=self.gqk_sb[:, l, gs:gs + 1],
                                                                               in1=rc[:, 1, :n], op0=ALU.mult, op1=ALU.mult), r=[ZQ, rc, self.gqk_sb], w=[T2])
                    k.op("dve", lambda: nc.vector.tensor_tensor(out=T1[:, :n], in0=T1[:, :n], in1=T2[:, :n], op=ALU.add), r=[T1, T2], w=[T1])
                    qo = QO[qi % 2]
                    k.op("dve", lambda qo=qo: nc.vector.tensor_tensor(out=qo[:, :n], in0=T1[:, :n], in1=RSh[:, :n], op=ALU.mult), r=[T1, RSh], w=[qo])
                    k.dma("sp", self.gqT[row:row + 128, t0:t0 + n], qo[:, :n], r=[qo], w=[self.gqT])
                    yield

            load_x(0)
            for _ in stage_n(0):
                pass
            for bi in range(len(TBLK)):
                side = stage_n(bi + 1) if bi + 1 < len(TBLK) else None
                self.weave(stage_p(bi), side, 1)
            if self.dbg and "d_gqT" in self.dbg and b == 0 and l == 0:
                self.dump("d_gqT", self.gqT, [384, T], BF16)
                self.dump("d_ztm", self.ztm, [T, NTM], BF16)
                self.dump("d_nqk", self.nqk, [512, T], BF16)

    def dump(self, name, src, shape, dt):
        k = self.k
        o = self.dbgout[name]
        rows = shape[0]
        with k.scope():
            st = k.sb("dump_st", [128, shape[1]], dt)
            for r0 in range(0, rows, 128):
                rr = min(128, rows - r0)
                k.dma("sp", st[:rr, :], src[r0:r0 + rr, :], r=[src], w=[st])
                k.dma("sp", o[r0:r0 + rr, :], st[:rr, :], r=[st], w=[o])

    def zero_mix(self, r0, r1):
        k, nc = self.k, self.nc
        with k.scope():
            z = k.sb("zmix", [128, T], BF16)
            k.op("pool", lambda: nc.gpsimd.memset(z[:, :], 0.0), w=[z])
            for r in range(r0, r1, 128):
                k.dma("sp", self.mixT[r:r + 128, :], z[:, :], r=[z], w=[self.mixT])

    def attn_run(self, blocks, PT, ACC_R, OT, GS=2, sbanks=((0, 1), (2, 3)), abanks=(4, 5), side=None, side_from=14):
        k, nc = self.k, self.nc
        groups = []
        for bi, B in enumerate(blocks):
            kts = B["ktiles"]
            gl = [kts[i:i + GS] for i in range(0, len(kts), GS)]
            for gi, grp in enumerate(gl):
                groups.append((bi, B, gi, grp, gi == 0, gi == len(gl) - 1))

        def emit_s(i):
            bi, B, gi, grp, first, last = groups[i]
            if first and B.get("prep"):
                B["prep"]()
            kT, qT, q0, n, kp, extra = B["kT"], B["qT"], B["q0"], B["n"], B["kp"], B.get("extra")
            banks = sbanks[i % 2]
            pt = PT[i % len(PT)]
            for j, kt in enumerate(grp):
                bk = banks[j]
                pairs = [(kT[0:kp, kt * 128:(kt + 1) * 128], qT[0:kp, q0:q0 + n])]
                rb = [kT, qT]
                if extra is not None and kt in extra:
                    bb, bap = extra[kt]
                    pairs.append((self.IDENT, bap))
                    rb = rb + [bb, self.C]
                self.mm(self.pap(bk, n), [self.psb[bk]], pairs, rb)
            g = len(grp)
            b0 = banks[0]
            src = self.pst[:, b0 * 512:(b0 + g) * 512].rearrange("p (g c) -> p g c", c=512)[:, :, :n]
            k.op("act", lambda: nc.scalar.activation(out=pt[:, :g, :n], in_=src, func=AF.Exp, scale=B["scale"]),
                 r=[self.psb[b0 + j] for j in range(g)], w=[pt])

        def emit_pv(i):
            bi, B, gi, grp, first, last = groups[i]
            V, n, q0 = B["V"], B["n"], B["q0"]
            accb = abanks[bi % 2]
            pt = PT[i % len(PT)]
            g = len(grp)
            for j, kt in enumerate(grp):
                k.op("pe", lambda kt=kt, j=j: nc.tensor.matmul(self.pap(accb, n), lhsT=V[:, kt, :], rhs=pt[:, j, :n],
                                                               start=(first and j == 0), stop=(last and j == g - 1)),
                     r=[V, pt], w=[self.psb[accb]], inc=(j == g - 1))
            if last:
                k.op("dve", lambda: nc.vector.reciprocal(out=ACC_R[0:64, :n], in_=self.pap(accb, n, 64, 128)), r=[self.psb[accb]], w=[ACC_R])
                ot = OT[bi % 2]
                k.op("dve", lambda: nc.vector.tensor_tensor(out=ot[0:64, :n], in0=self.pap(accb, n, 0, 64), in1=ACC_R[0:64, :n], op=ALU.mult),
                     r=[self.psb[accb], ACC_R], w=[ot])
                k.dma("sp", self.mixT[B["row"]:B["row"] + 64, q0:q0 + n], ot[0:64, :n], r=[ot], w=[self.mixT])

        if side is not None:
            next(side, None)
        for i in range(len(groups) + 1):
            if i < len(groups):
                emit_s(i)
            if i >= 1:
                emit_pv(i - 1)
            if side is not None and i >= side_from:
                next(side, None)
        if side is not None:
            for _ in side:
                pass

    def ph_gqa(self, b, l):
        k, nc = self.k, self.nc
        need_ctx = (l < DEPTH - 1)
        with k.scope():
            kTs = [k.sb("kT%d" % i, [128, T], BF16) for i in range(2)]
            qTs = [k.sb("qT%d" % i, [128, T], BF16) for i in range(4)]
            for t_ in kTs + qTs:
                k.op("act", lambda t_=t_: nc.scalar.activation(out=t_[64:128, :], in_=self.CF[64:128, 0:1].to_broadcast([64, T]), func=AF.Copy, scale=0.0),
                     r=[self.CFb], w=[t_])
            Vs = [k.sb("Vaug%d" % i, [128, 18, 128], BF16) for i in range(2)]
            PT = [k.sb("PT%d" % i, [128, 2, 512], BF16) for i in range(3)]
            ACC_R = k.sb("ACC_R", [64, 512])
            OT = [k.sb("OT%d" % i, [64, 512], BF16) for i in range(2)]
            for v in Vs:
                k.op("dve", lambda v=v: nc.vector.memset(v[:, :, 64:128], 1.0), w=[v])
            blocks = []
            for kvh in range(2):
                kT, Vaug = kTs[kvh], Vs[kvh]
                k.dma("sp", kT[0:64, :], self.gqT[256 + 64 * kvh:256 + 64 * (kvh + 1), :], r=[self.gqT], w=[kT])
                k.dma("sp", Vaug[:, :, 0:64], self.ztm[:, 64 * kvh:64 * (kvh + 1)].rearrange("(kt p) c -> p kt c", p=128), r=[self.ztm], w=[Vaug])
                for hh in range(2):
                    h = 2 * kvh + hh
                    qT = qTs[h]
                    k.dma("act", qT[0:64, :], self.gqT[64 * h:64 * (h + 1), :], r=[self.gqT], w=[qT])
                    base = dict(kT=kT, qT=qT, V=Vaug, row=64 * h, scale=0.125, kp=128)
                    if need_ctx:
                        blocks.append(dict(base, q0=0, n=256, ktiles=[0, 1]))
                    for qb in range(4):
                        blocks.append(dict(base, q0=256 + 512 * qb, n=512, ktiles=list(range(18))))
            side = None
            fn_alone = False
            self.bank_set = [6, 7]
            if self.prep1_pending:
                side = Side(k, lambda: self.ph_s5prep(1), every=5).gen()
                self.prep1_pending = False
                fn_alone = "fn" not in self.skip
            elif "fn" not in self.skip:
                side = self.fn_gen(b, l)
            self.attn_run(blocks, PT, ACC_R, OT, side=side, side_from=(0 if fn_alone else 14))
            self.bank_set = None
        if fn_alone:
            if "s5" not in self.skip and "na" not in self.skip:
                self.fn_deferred = True
            else:
                self.ph_fn(b, l)

    def tt(self, eng, out, in0, in1, op, r, w):
        k, nc = self.k, self.nc
        e = nc.vector if eng == "dve" else nc.gpsimd
        k.op(eng, lambda: e.tensor_tensor(out=out, in0=in0, in1=in1, op=op), r=r, w=w)

    def cmul(self, X, Y, TX, TY, xo, yo, tx, ty, are, aim, bre, bim, rb):
        self.tt("dve", xo, are, bre, ALU.mult, rb, [X])
        self.tt("dve", tx, aim, bim, ALU.mult, rb, [TX])
        self.tt("dve", xo, xo, tx, ALU.subtract, [X, TX], [X])
        self.tt("pool", yo, are, bim, ALU.mult, rb, [Y])
        self.tt("pool", ty, aim, bre, ALU.mult, rb, [TY])
        self.tt("pool", yo, yo, ty, ALU.add, [Y, TY], [Y])

    def stack2(self, OUT, out, X, x, sa, Y, y, sb_):
        k, nc = self.k, self.nc
        k.op("act", lambda: nc.scalar.activation(out=out, in_=x, func=AF.Copy, scale=self.CF[:, sa:sa + 1]), r=[X, self.CFb], w=[OUT])
        k.op("dve", lambda: nc.vector.scalar_tensor_tensor(out=out, in0=y, scalar=self.CF[:, sb_:sb_ + 1], in1=out, op0=ALU.mult, op1=ALU.add),
             r=[Y, OUT, self.CFb], w=[OUT])

    def ph_s5prep(self, l):
        k, nc = self.k, self.nc
        Q = 32
        with k.scope():
            PB = k.sb("PB", [128, Q, 3]); CB = k.sb("CB", [128, Q, 2, 16]); BB = k.sb("BB", [128, Q, 2, 16]); DC = k.sb("DC", [128, 16])
            k.dma("sp", PB[:, :, :], self.s5p[l, :, :, :], r=[self.s5p], w=[PB])
            k.dma("sp", CB[:, :, :, :], self.s5c[l, :, :, :, :], r=[self.s5c], w=[CB])
            k.dma("sp", BB[:, :, :, :], self.s5b[l, :, :, :, :], r=[self.s5b], w=[BB])
            k.dma("sp", DC[:, :], self.s5d[l, :, :], r=[self.s5d], w=[DC])
            sm = lambda n, s=(128, Q): k.sb(n, list(s))
            LRE, DT, AR, TH = sm("LRE"), sm("DT"), sm("AR"), sm("TH")
            k.op("dve", lambda: nc.vector.tensor_scalar(out=LRE[:, :], in0=PB[:, :, 0], scalar1=-1e-4, scalar2=None, op0=ALU.min), r=[PB], w=[LRE])
            k.op("act", lambda: nc.scalar.activation(out=DT[:, :], in_=PB[:, :, 2], func=AF.Exp), r=[PB], w=[DT])
            self.tt("dve", AR[:, :], LRE[:, :], DT[:, :], ALU.mult, [LRE, DT], [AR])
            self.tt("dve", TH[:, :], PB[:, :, 1], DT[:, :], ALU.mult, [PB, DT], [TH])
            UC, US, T0, T1 = sm("UC"), sm("US"), sm("T0"), sm("T1")
            k.op("act", lambda: nc.scalar.activation(out=UC[:, :], in_=TH[:, :], func=AF.Sin, scale=1.0 / 16, bias=self.CF[:, 4:5]), r=[TH, self.CFb], w=[UC])
            k.op("act", lambda: nc.scalar.activation(out=US[:, :], in_=TH[:, :], func=AF.Sin, scale=1.0 / 16), r=[TH], w=[US])

            def csq(C_, S_):
                self.tt("dve", T0[:, :], C_[:, :], C_[:, :], ALU.mult, [C_], [T0])
                self.tt("dve", T1[:, :], S_[:, :], S_[:, :], ALU.mult, [S_], [T1])
                k.op("dve", lambda: nc.vector.scalar_tensor_tensor(out=S_[:, :], in0=C_[:, :], scalar=2.0, in1=S_[:, :], op0=ALU.mult, op1=ALU.mult), r=[C_, S_], w=[S_])
                self.tt("dve", C_[:, :], T0[:, :], T1[:, :], ALU.subtract, [T0, T1], [C_])
            for _ in range(4):
                csq(UC, US)
            PR_, PI_ = k.sb("POWre", [128, Q, 9]), k.sb("POWim", [128, Q, 9])
            NR_, NI_ = k.sb("NPOWre", [128, Q, 8]), k.sb("NPOWim", [128, Q, 8])
            UKr, UKi = k.sb("UKr", [128, Q, 9]), k.sb("UKi", [128, Q, 9])
            MG, NMG = k.sb("MG", [128, Q, 9]), k.sb("NMG", [128, Q, 8])
            k.op("dve", lambda: nc.vector.memset(UKr[:, :, 0], 1.0), w=[UKr])
            k.op("dve", lambda: nc.vector.memset(UKi[:, :, 0], 0.0), w=[UKi])
            for kk in range(1, 9):
                self.tt("dve", T0[:, :], UKr[:, :, kk - 1], UC[:, :], ALU.mult, [UKr, UC], [T0])
                self.tt("dve", T1[:, :], UKi[:, :, kk - 1], US[:, :], ALU.mult, [UKi, US], [T1])
                self.tt("dve", UKr[:, :, kk], T0[:, :], T1[:, :], ALU.subtract, [T0, T1], [UKr])
                self.tt("dve", T0[:, :], UKr[:, :, kk - 1], US[:, :], ALU.mult, [UKr, US], [T0])
                self.tt("dve", T1[:, :], UKi[:, :, kk - 1], UC[:, :], ALU.mult, [UKi, UC], [T1])
                self.tt("dve", UKi[:, :, kk], T0[:, :], T1[:, :], ALU.add, [T0, T1], [UKi])
            for kk in range(9):
                k.op("act", lambda kk=kk: nc.scalar.activation(out=MG[:, :, kk], in_=AR[:, :], func=AF.Exp, scale=float(kk)), r=[AR], w=[MG])
            for kk in range(8):
                k.op("act", lambda kk=kk: nc.scalar.activation(out=NMG[:, :, kk], in_=AR[:, :], func=AF.Exp, scale=float(-kk)), r=[AR], w=[NMG])
            self.tt("dve", PR_[:, :, :], UKr[:, :, :], MG[:, :, :], ALU.mult, [UKr, MG], [PR_])
            self.tt("dve", PI_[:, :, :], UKi[:, :, :], MG[:, :, :], ALU.mult, [UKi, MG], [PI_])
            self.tt("dve", NR_[:, :, :], UKr[:, :, 0:8], NMG[:, :, :], ALU.mult, [UKr, NMG], [NR_])
            k.op("dve", lambda: nc.vector.scalar_tensor_tensor(out=NI_[:, :, :], in0=UKi[:, :, 0:8], scalar=-1.0, in1=NMG[:, :, :], op0=ALU.mult, op1=ALU.mult), r=[UKi, NMG], w=[NI_])
            k.op("act", lambda: nc.scalar.activation(out=self.rdec[:, l, :], in_=AR[:, :], func=AF.Exp, scale=8.0), r=[AR], w=[self.rdec])
            with k.scope():
                WC_, WS_ = sm("WC_"), sm("WS_")
                k.op("dve", lambda: nc.vector.tensor_copy(out=WC_[:, :], in_=UKr[:, :, 8]), r=[UKr], w=[WC_])
                k.op("dve", lambda: nc.vector.tensor_copy(out=WS_[:, :], in_=UKi[:, :, 8]), r=[UKi], w=[WS_])
                EC, ES = k.sb("EC", [128, Q, 288]), k.sb("ES", [128, Q, 288])
                TA, TB = k.sb("TA", [128, Q, 128]), k.sb("TB", [128, Q, 128])
                k.op("dve", lambda: nc.vector.memset(EC[:, :, 0:1], 1.0), w=[EC])
                k.op("dve", lambda: nc.vector.memset(ES[:, :, 0:1], 0.0), w=[ES])
                s_ = 1
                while s_ < 288:
                    n = min(s_, 288 - s_)
                    wc = WC_[:, :].unsqueeze(2).to_broadcast([128, Q, n])
                    ws = WS_[:, :].unsqueeze(2).to_broadcast([128, Q, n])
                    self.tt("dve", TA[:, :, :n], EC[:, :, 0:n], wc, ALU.mult, [EC, WC_], [TA])
                    self.tt("dve", TB[:, :, :n], ES[:, :, 0:n], ws, ALU.mult, [ES, WS_], [TB])
                    self.tt("dve", EC[:, :, s_:s_ + n], TA[:, :, :n], TB[:, :, :n], ALU.subtract, [TA, TB], [EC])
                    self.tt("pool", TA[:, :, :n], EC[:, :, 0:n], ws, ALU.mult, [EC, WS_], [TA])
                    self.tt("pool", TB[:, :, :n], ES[:, :, 0:n], wc, ALU.mult, [ES, WC_], [TB])
                    self.tt("pool", ES[:, :, s_:s_ + n], TA[:, :, :n], TB[:, :, :n], ALU.add, [TA, TB], [ES])
                    csq(WC_, WS_)
                    s_ *= 2
                k.dma("sp", self.s5tab[l, :, 0, :, :].rearrange("q p c -> p q c"), EC[:, :, :], r=[EC], w=[self.s5tab])
                k.dma("sp", self.s5tab[l, :, 1, :, :].rearrange("q p c -> p q c"), ES[:, :, :], r=[ES], w=[self.s5tab])
            NRe, L2, CFr, CFi = sm("NRe"), sm("L2"), sm("CFr"), sm("CFi")
            k.op("dve", lambda: nc.vector.tensor_scalar(out=NRe[:, :], in0=PR_[:, :, 1], scalar1=-1.0, scalar2=None, op0=ALU.add), r=[PR_], w=[NRe])
            self.tt("dve", L2[:, :], LRE[:, :], LRE[:, :], ALU.mult, [LRE], [L2])
            self.tt("dve", T0[:, :], PB[:, :, 1], PB[:, :, 1], ALU.mult, [PB], [T0])
            self.tt("dve", L2[:, :], L2[:, :], T0[:, :], ALU.add, [L2, T0], [L2])
            k.op("dve", lambda: nc.vector.reciprocal(out=L2[:, :], in_=L2[:, :]), r=[L2], w=[L2])
            self.tt("dve", CFr[:, :], NRe[:, :], LRE[:, :], ALU.mult, [NRe, LRE], [CFr])
            self.tt("dve", T0[:, :], PI_[:, :, 1], PB[:, :, 1], ALU.mult, [PI_, PB], [T0])
            self.tt("dve", CFr[:, :], CFr[:, :], T0[:, :], ALU.add, [CFr, T0], [CFr])
            self.tt("dve", CFr[:, :], CFr[:, :], L2[:, :], ALU.mult, [CFr, L2], [CFr])
            self.tt("dve", CFi[:, :], PI_[:, :, 1], LRE[:, :], ALU.mult, [PI_, LRE], [CFi])
            self.tt("dve", T0[:, :], NRe[:, :], PB[:, :, 1], ALU.mult, [NRe, PB], [T0])
            self.tt("dve", CFi[:, :], CFi[:, :], T0[:, :], ALU.subtract, [CFi, T0], [CFi])
            self.tt("dve", CFi[:, :], CFi[:, :], L2[:, :], ALU.mult, [CFi, L2], [CFi])
            BBr, BBi = k.sb("BBr", [128, Q, 16]), k.sb("BBi", [128, Q, 16])
            TX16, TY16 = k.sb("TX16", [128, Q, 16]), k.sb("TY16", [128, Q, 16])
            b16 = lambda t_: t_[:, :].unsqueeze(2).to_broadcast([128, Q, 16])
            self.cmul(BBr, BBi, TX16, TY16, BBr[:, :, :], BBi[:, :, :], TX16[:, :, :], TY16[:, :, :],
                      b16(CFr), b16(CFi), BB[:, :, 0, :], BB[:, :, 1, :], [CFr, CFi, BB])
            def ptab(name, SRC_r, SRC_i, k0f, k0r):
                Pr, Pi = k.sb(name + "r", [128, Q, 8]), k.sb(name + "i", [128, Q, 8])
                for (dst, src, SB_) in ((Pr, SRC_r, SRC_r), (Pi, SRC_i, SRC_i)):
                    k.op("act", lambda dst=dst, src=src: nc.scalar.copy(out=dst[:, 0:16, :], in_=src[:, 0:16, k0f[0]:k0f[0] + 8] if k0f[1] > 0 else
                                                                          bass.AP(src.t, src[:, 0:16, k0f[0]:k0f[0] + 1].offset, [list(src[:, 0:16, 0:8].ap[0]), list(src[:, 0:16, 0:8].ap[1]), [-1, 8]])),
                         r=[SB_], w=[dst])
                    k.op("act", lambda dst=dst, src=src: nc.scalar.copy(out=dst[:, 16:32, :], in_=src[:, 16:32, k0r[0]:k0r[0] + 8] if k0r[1] > 0 else
                                                                          bass.AP(src.t, src[:, 16:32, k0r[0]:k0r[0] + 1].offset, [list(src[:, 16:32, 0:8].ap[0]), list(src[:, 16:32, 0:8].ap[1]), [-1, 8]])),
                         r=[SB_], w=[dst])
                return Pr, Pi
            PWCr, PWCi = ptab("PWC", PR_, PI_, (1, 1), (8, -1))
            PLr, PLi = ptab("PL", PR_, PI_, (0, 1), (7, -1))
            PRr, PRi = ptab("PRt", NR_, NI_, (0, 1), (7, -1))
            PW1r, PW1i = ptab("PW1", PR_, PI_, (7, -1), (0, 1))
            big = lambda n: k.sb(n, [128, Q, 8, 16])
            Xb, Yb, TXb, TYb = big("Xb"), big("Yb"), big("TXb"), big("TYb")
            ST1 = k.sb("ST1", [128, Q, 128], BF16)
            ST2 = k.sb("ST2", [128, Q, 128], BF16)
            SF1 = k.sb("SF1", [128, Q, 128])
            SF2 = k.sb("SF2", [128, Q, 128])
            bj = lambda t_, c_: (t_[:, :, c_, :] if c_ is not None else t_[:, :, :]).unsqueeze(2).to_broadcast([128, Q, 8, 16])
            bo = lambda t_: t_[:, :, :].unsqueeze(3).to_broadcast([128, Q, 8, 16])
            f4 = lambda t_: t_[:, :, :, :]
            f3 = lambda t_: t_[:, :, :].rearrange("p q (j o) -> p q j o", o=16)
            self.cmul(Xb, Yb, TXb, TYb, f4(Xb), f4(Yb), f4(TXb), f4(TYb), bo(PWCr), bo(PWCi), bj(CB, 0), bj(CB, 1), [PWCr, PWCi, CB])
            self.stack2(ST1, f3(ST1), Xb, f4(Xb), 0, Yb, f4(Yb), 3)
            self.stack2(ST2, f3(ST2), Yb, f4(Yb), 2, Xb, f4(Xb), 3)
            k.dma("sp", self.s5w[l, :, 2, :, :].rearrange("q p c -> p q c"), ST1[:, :, :], r=[ST1], w=[self.s5w])
            k.dma("sp", self.s5w[l, :, 3, :, :].rearrange("q p c -> p q c"), ST2[:, :, :], r=[ST2], w=[self.s5w])
            self.cmul(Xb, Yb, TXb, TYb, f4(Xb), f4(Yb), f4(TXb), f4(TYb), bo(PW1r), bo(PW1i), bj(BBr, None), bj(BBi, None), [PW1r, PW1i, BBr, BBi])
            f3f = lambda t_: t_[:, :, :].rearrange("p q (j o) -> p q j o", o=16)
            self.stack2(SF1, f3f(SF1), Xb, f4(Xb), 0, Yb, f4(Yb), 1)
            self.stack2(SF2, f3f(SF2), Yb, f4(Yb), 0, Xb, f4(Xb), 3)
            for which, SF in ((0, SF1), (1, SF2)):
                ST = ST1 if which == 0 else ST2
                for q in range(Q):
                    bk = self.bank()
                    self.mm(self.pap(bk, 128), [self.psb[bk]], [(SF[:, q, :], self.IDF)], [SF, self.CFb])
                    if q % 2 == 0:
                        k.op("act", lambda bk=bk, q=q, ST=ST: nc.scalar.copy(out=ST[:, q, :], in_=self.pap(bk, 128)), r=[self.psb[bk]], w=[ST])
                    else:
                        k.op("dve", lambda bk=bk, q=q, ST=ST: nc.vector.tensor_copy(out=ST[:, q, :], in_=self.pap(bk, 128)), r=[self.psb[bk]], w=[ST])
                k.dma("sp", self.s5w[l, :, which, :, :].rearrange("q p c -> p q c"), ST[:, :, :], r=[ST], w=[self.s5w])
            self.cmul(Xb, Yb, TXb, TYb, f4(Xb), f4(Yb), f4(TXb), f4(TYb), bo(PLr), bo(PLi), bj(CB, 0), bj(CB, 1), [PLr, PLi, CB])
            self.stack2(SF1, f3f(SF1), Xb, f4(Xb), 0, Yb, f4(Yb), 3)
            self.cmul(Xb, Yb, TXb, TYb, f4(Xb), f4(Yb), f4(TXb), f4(TYb), bo(PRr), bo(PRi), bj(BBr, None), bj(BBi, None), [PRr, PRi, BBr, BBi])
            self.stack2(SF2, f3f(SF2), Xb, f4(Xb), 0, Yb, f4(Yb), 1)
            TP = Alias(TYb, TYb[:, 0:16, :, :].rearrange("p g j o -> p g (j o)"))
            TPb = Alias(ST1, ST1[:, 0:16, :])
            for g in range(16):
                for d in range(2):
                    q = d * 16 + g
                    bk = self.bank()
                    self.mm(self.pap(bk, 128), [self.psb[bk]], [(SF2[:, q, :], SF1[:, q, :])], [SF1, SF2])
                    msk = self.CF[:, 8 + d * 128:8 + (d + 1) * 128]
                    if d == 0:
                        k.op("dve", lambda bk=bk, g=g, msk=msk: nc.vector.tensor_tensor(out=TP[:, g, :], in0=self.pap(bk, 128), in1=msk, op=ALU.mult),
                             r=[self.psb[bk], self.CFb], w=[TP])
                    else:
                        k.op("dve", lambda bk=bk, g=g, msk=msk: nc.vector.tensor_tensor(out=TXb[:, 0, :, :].rearrange("p j o -> p (j o)"), in0=self.pap(bk, 128), in1=msk, op=ALU.mult),
                             r=[self.psb[bk], self.CFb], w=[TXb])
                        self.tt("dve", TP[:, g, :], TP[:, g, :], TXb[:, 0, :, :].rearrange("p j o -> p (j o)"), ALU.add, [TP, TXb], [TP])
                k.op("dve", lambda g=g: nc.vector.scalar_tensor_tensor(out=TPb[:, g, :], in0=self.IDF, scalar=DC[:, g:g + 1], in1=TP[:, g, :], op0=ALU.mult, op1=ALU.add),
                     r=[TP, DC, self.CFb], w=[TPb])
            k.dma("sp", self.s5t[l, :, :, :].rearrange("g p c -> p g c"), TPb[:, :, :], r=[TPb], w=[self.s5t])

    S5NT = [(0, 32, 0), (32, 128, 256), (160, 128, 256 + 1024)]

    def s5_inputs(self, b, l):
        k, nc = self.k, self.nc
        NT = self.S5NT
        X = [k.sb("Xs%d" % i, [128, 8, 256], BF16) for i in range(3)]
        X2 = [k.sb("X2s%d" % i, [128, 16, 128], BF16) for i in range(3)]
        for i, (c0, ncn, t0) in enumerate(NT):
            k.dma("sp", X[i][0:ncn, :, :], self.ztm[t0:t0 + 8 * ncn, 384:640].rearrange("(c j) ch -> c j ch", j=8), r=[self.ztm], w=[X[i]])
            src = X[i][0:ncn, :, :].rearrange("c j (g i) -> c g j i", i=16)
            dst = X2[i][0:ncn, :, :].rearrange("c g (j i) -> c g j i", i=16)
            k.op("dve", lambda src=src, dst=dst: nc.vector.tensor_copy(out=dst, in_=src), r=[X[i]], w=[X2[i]])
        Wt = [k.sb("S5W%d" % i, [128, 2, 4, 128], BF16) for i in range(3)]
        Tp = [k.sb("S5T%d" % i, [128, 128], BF16) for i in range(3)]
        Tab = [k.sb("S5tab%d" % i, [128, 2, 2, 288]) for i in range(3)]

        def loads(g):
            if g >= 16:
                return
            wt, tp, tab = Wt[g % 3], Tp[g % 3], Tab[g % 3]
            for d in range(2):
                q = d * 16 + g
                k.dma("sp", wt[:, d, :, :], self.s5w[l, q, :, :, :].rearrange("w p c -> p w c"), r=[self.s5w], w=[wt])
                k.dma("act", tab[:, d, :, :], self.s5tab[l, q, :, :, :].rearrange("t p c -> p t c"), r=[self.s5tab], w=[tab])
            k.dma("sp", tp[:, :], self.s5t[l, g, :, :], r=[self.s5t], w=[tp])
        for g in range(3):
            loads(g)
        return dict(X2=X2, Wt=Wt, Tp=Tp, Tab=Tab, loads=loads)

    def ph_s5(self, b, l, after_loads=None, pre=None):
        k, nc = self.k, self.nc
        need_ctx = (l < DEPTH - 1)
        NT = self.S5NT
        with k.scope():
            tT = k.sb("tT", [128, 2, T], BF16)
            Tt = [k.sb("Tt%d" % i, [128, 8, 256], BF16) for i in range(3)]
            with k.scope():
                if pre is None:
                    pre = self.s5_inputs(b, l)
                X2, Wt, Tp, Tab, loads = pre["X2"], pre["Wt"], pre["Tp"], pre["Tab"], pre["loads"]
                if after_loads is not None:
                    after_loads()
                HN = [[[k.sb("HN%d%d%d" % (gp, d, cs), [128, 288], BF16) for cs in range(2)] for d in range(2)] for gp in range(2)]
                for gp in range(2):
                    for d in range(2):
                        for cs in range(2):
                            k.op("pool", lambda gp=gp, d=d, cs=cs: nc.gpsimd.memset(HN[gp][d][cs][:, :], 0.0), w=[HN[gp][d][cs]])
                U = [[k.sb("U%d%d" % (gp, d), [128, 288], BF16) for d in range(2)] for gp in range(2)]
                V1s = [k.sb("V1%d" % i, [128, 288]) for i in range(2)]
                V2s = [k.sb("V2%d" % i, [128, 288]) for i in range(2)]
                HTs = [k.sb("HT%d" % i, [128, 288]) for i in range(2)]
                def stage_a(g, d):
                    wt = Wt[g % 3]
                    u = U[g % 2][d]
                    bk = 0
                    order = [(0, 0), (1, 32), (2, 160)] if d == 0 else [(0, 0), (2, 32), (1, 160)]
                    for (ti, col) in order:
                        ncn = NT[ti][1]
                        rhs = (self.IDENT[0:ncn, 0:ncn] if d == 0 else self.JREV[0:ncn, 128 - ncn:128])
                        k.op("pe", lambda ti=ti, col=col, ncn=ncn, rhs=rhs: nc.tensor.matmul(
                            self.pst[:, bk * 512 + col:bk * 512 + col + ncn], lhsT=X2[ti][0:ncn, g, :], rhs=rhs, start=True, stop=True),
                            r=[X2[ti], self.C], w=[self.psb[bk]], inc=(col == 160))
                    k.op("act", lambda: nc.scalar.copy(out=u[:, :], in_=self.pap(bk, 288)), r=[self.psb[bk]], w=[u])
                    un = d
                    b1, b2 = (1, 2) if un == 0 else (3, 4)
                    self.mm(self.pap(b1, 288), [self.psb[b1]], [(wt[:, d, 0, :], u[:, :])], [wt, u])
                    self.mm(self.pap(b2, 288), [self.psb[b2]], [(wt[:, d, 1, :], u[:, :])], [wt, u])

                def stage_b(g, d):
                    tab = Tab[g % 3]
                    q = d * 16 + g
                    un = d
                    b1, b2 = (1, 2) if un == 0 else (3, 4)
                    V1, V2, HT = V1s[un], V2s[un], HTs[un]
                    self.tt("dve", V1[:, :], self.pap(b1, 288), tab[:, d, 0, :], ALU.mult, [self.psb[b1], tab], [V1])
                    yield
                    self.tt("dve", V2[:, :], self.pap(b2, 288), tab[:, d, 1, :], ALU.mult, [self.psb[b2], tab], [V2])
                    if g + 1 < 16:
                        stage_a(g + 1, d)
                    yield
                    self.tt("dve", V1[:, :], V1[:, :], V2[:, :], ALU.add, [V1, V2], [V1])
                    yield
                    k.op("dve", lambda: nc.vector.tensor_tensor_scan(out=HT[:, :], data0=self.rdec[:, l, q:q + 1].to_broadcast([128, 288]), data1=V1[:, :],
                                                                     initial=0.0, op0=ALU.mult, op1=ALU.add), r=[self.rdec, V1], w=[HT])
                    yield
                    for cs in range(2):
                        hn = HN[g % 2][d][cs]
                        eng = "dve" if cs == 0 else "pool"
                        if d == 0:
                            self.tt(eng, hn[:, 1:288], tab[:, d, cs, 0:287], HT[:, 0:287], ALU.mult, [tab, HT], [hn])
                        else:
                            rev = lambda t_, ap0, start, cnt: bass.AP(t_.t, ap0[:, start:start + 1].offset, [list(ap0[:, 0:cnt].ap[0]), [-1, cnt]])
                            tb = tab[:, d, cs, :]
                            self.tt(eng, hn[:, 32:288], rev(tab, tb, 286, 256), rev(HT, HT[:, :], 286, 256), ALU.mult, [tab, HT], [hn])
                            self.tt(eng, hn[:, 0:31], rev(tab, tb, 30, 31), rev(HT, HT[:, :], 30, 31), ALU.mult, [tab, HT], [hn])

                def stage_c(g):
                    wt, tp = Wt[g % 3], Tp[g % 3]
                    quad, gl = g // 4, g % 4
                    hn = HN[g % 2]
                    for ti, (c0, ncn, t0) in enumerate(NT):
                        if ti == 0 and not need_ctx:
                            continue
                        ob = 5 + ti
                        pairs = []
                        for d in range(2):
                            pairs.append((hn[d][0][:, c0:c0 + ncn], wt[:, d, 2, :]))
                            pairs.append((hn[d][1][:, c0:c0 + ncn], wt[:, d, 3, :]))
                        pairs.append((U[g % 2][0][:, c0:c0 + ncn], tp[:, :]))
                        self.mm(self.pst[0:ncn, ob * 512 + gl * 128:ob * 512 + (gl + 1) * 128], [self.psb[ob]], pairs,
                                [hn[0][0], hn[0][1], hn[1][0], hn[1][1], wt, U[g % 2][0], tp])
                    if gl == 3:
                        for ti, (c0, ncn, t0) in enumerate(NT):
                            if ti == 0 and not need_ctx:
                                continue
                            ob = 5 + ti
                            src = self.pst[0:ncn, ob * 512:(ob + 1) * 512].rearrange("c (g j o) -> c g j o", j=8, o=16)
                            dst = Tt[ti][0:ncn, :, quad * 64:(quad + 1) * 64].rearrange("c j (g o) -> c g j o", o=16)
                            k.op("act", lambda src=src, dst=dst: nc.scalar.activation(out=dst, in_=src, func=AF.Gelu_apprx_tanh), r=[self.psb[ob]], w=[Tt[ti]])

                stage_a(0, 0)
                stage_a(0, 1)
                for g in range(16):
                    self.weave(stage_b(g, 0), stage_b(g, 1), 1)
                    stage_c(g)
                    loads(g + 3)
                self.bank_lim = 8
            for ti, (c0, ncn, t0) in enumerate(NT):
                if ti == 0 and not need_ctx:
                    continue
                for ct in range(2):
                    for jh in range(2):
                        bk = self.bank()
                        for j4 in range(4):
                            j = jh * 4 + j4
                            k.op("pe", lambda j=j, j4=j4, bk=bk, ti=ti, ct=ct, ncn=ncn: nc.tensor.matmul(
                                self.pst[:, bk * 512 + j4 * 128:bk * 512 + j4 * 128 + ncn], lhsT=Tt[ti][0:ncn, j, ct * 128:(ct + 1) * 128],
                                rhs=self.IDENT[0:ncn, 0:ncn], start=True, stop=True), r=[Tt[ti], self.C], w=[self.psb[bk]], inc=(j4 == 3))
                        src = self.pst[:, bk * 512:(bk + 1) * 512].rearrange("p (j c) -> p j c", c=128)[:, :, 0:ncn]
                        dst = tT[:, ct, t0:t0 + 8 * ncn].rearrange("p (c j) -> p j c", j=8)[:, jh * 4:(jh + 1) * 4, :]
                        if (ct + jh) % 2 == 0:
                            k.op("act", lambda src=src, dst=dst: nc.scalar.copy(out=dst, in_=src), r=[self.psb[bk]], w=[tT])
                        else:
                            k.op("dve", lambda src=src, dst=dst: nc.vector.tensor_copy(out=dst, in_=src), r=[self.psb[bk]], w=[tT])
            Wg = k.sb("Wglu", [128, 2, 256], BF16)
            k.dma("pool", Wg[:, :, :], self.w_glu[l, :, :].rearrange("(ct p) o -> p ct o", p=128), r=[self.w_glu], w=[Wg])
            SG = [k.sb("SG%d" % i, [128, 512]) for i in range(2)]
            OB = [k.sb("OB%d" % i, [128, 512], BF16) for i in range(2)]
            blks = TBLK if need_ctx else TBLK[1:]
            ii = 0
            for (t0, n, mj) in blks:
                for ot in range(2):
                    bk = self.bank()
                    self.mm(self.pap(bk, n), [self.psb[bk]], [(Wg[:, ct, ot * 128:(ot + 1) * 128], tT[:, ct, t0:t0 + n]) for ct in range(2)], [Wg, tT])
                    sg, obuf = SG[ii % 2], OB[ii % 2]
                    ii += 1
                    k.op("act", lambda bk=bk, sg=sg, n=n: nc.scalar.activation(out=sg[:, :n], in_=self.pap(bk, n), func=AF.Sigmoid), r=[self.psb[bk]], w=[sg])
                    self.tt("dve", obuf[:, :n], sg[:, :n], tT[:, ot, t0:t0 + n], ALU.mult, [sg, tT], [obuf])
                    k.dma("sp", self.mixT[256 + ot * 128:256 + (ot + 1) * 128, t0:t0 + n], obuf[:, :n], r=[obuf], w=[self.mixT])
            if not need_ctx:
                pass

    NA_TILES = {0: list(range(0, 6)), 1: list(range(2, 10)), 2: list(range(6, 14)), 3: list(range(10, 16))}

    @staticmethod
    def na_tidx(qb, m):
        if qb == 0:
            return m
        if qb == 3:
            return 14 + (m - 10)
        return 6 + (m - (4 * qb - 2))

    def ph_na(self, b, l):
        with self.k.scope():
            issue, run = self.na_setup(b, l)
            issue()
            run()

    def na_setup(self, b, l):
        k, nc = self.k, self.nc
        need_ctx = (l < DEPTH - 1)
        if True:
            kTs = [k.sb("nkT%d" % i, [128, T], BF16) for i in range(4)]
            qTs = [k.sb("nqT%d" % i, [128, T], BF16) for i in range(4)]
            Vs = [k.sb("nV%d" % i, [128, 18, 128], BF16) for i in range(4)]
            BTs = [k.sb("BT%d" % i, [128, 20, 512], BF16) for i in range(2)]
            PT = [k.sb("PT%d" % i, [128, 3, 512], BF16) for i in range(3)]
            ACC_R = k.sb("ACC_R", [64, 512])
            OT = [k.sb("OT%d" % i, [64, 512], BF16) for i in range(2)]
            for v in Vs:
                k.op("act", lambda v=v: nc.scalar.activation(out=v[:, :, 64:128], in_=self.CF[:, 0:1].unsqueeze(2).to_broadcast([128, 18, 64]), func=AF.Identity, scale=0.0, bias=self.CF[:, 6:7]),
                     r=[self.CFb], w=[v])
            for t_ in kTs + qTs:
                k.op("act", lambda t_=t_: nc.scalar.activation(out=t_[64:128, :], in_=self.CF[64:128, 0:1].to_broadcast([64, T]), func=AF.Copy, scale=0.0),
                     r=[self.CFb], w=[t_])

            def load_bt(h):
                BT = BTs[h % 2]
                for t5 in range(4):
                    k.dma("pool", BT[:, t5 * 5:(t5 + 1) * 5, :], self.nbias[l, h, t5 * 5:(t5 + 1) * 5, :, :].rearrange("t p q -> p t q"), r=[self.nbias], w=[BT])
            blocks = []

            def issue_loads():
                for h in range(4):
                    kT, qT, Vaug = kTs[h], qTs[h], Vs[h]
                    k.dma("sp", kT[0:64, :], self.nqk[256 + 64 * h:256 + 64 * (h + 1), :], r=[self.nqk], w=[kT])
                    k.dma("act", qT[0:64, :], self.nqk[64 * h:64 * (h + 1), :], r=[self.nqk], w=[qT])
                    k.dma("sp", Vaug[:, :, 0:64], self.ztm[:, 128 + 64 * h:128 + 64 * (h + 1)].rearrange("(kt p) c -> p kt c", p=128), r=[self.ztm], w=[Vaug])
                    if h < 2:
                        load_bt(h)
            for h in range(4):
                kT, qT, Vaug, BT = kTs[h], qTs[h], Vs[h], BTs[h % 2]
                base = dict(kT=kT, qT=qT, V=Vaug, row=512 + 64 * h, scale=1.0, kp=128)
                hb = []
                if need_ctx:
                    hb.append(dict(base, q0=0, n=256, ktiles=[0, 1]))
                for qb in range(4):
                    ms = self.NA_TILES[qb]
                    extra = {2 + m: (BT, BT[:, self.na_tidx(qb, m), :]) for m in ms}
                    hb.append(dict(base, q0=256 + 512 * qb, n=512, ktiles=[0, 1] + [2 + m for m in ms], extra=extra))
                if 1 <= h < 3:
                    hb[0]["prep"] = (lambda hh=h + 1: load_bt(hh))
                blocks.extend(hb)
            def run(side=None):
                if side is None:
                    self.attn_run(blocks, PT, ACC_R, OT, GS=3, sbanks=((0, 1, 2), (3, 4, 5)), abanks=(6, 7))
                else:
                    self.bank_set = [6, 7]
                    self.attn_run(blocks, PT, ACC_R, OT, GS=2, sbanks=((0, 1), (2, 3)), abanks=(4, 5), side=side, side_from=8)
                    self.bank_set = None
            return issue_loads, run

    def ph_fn(self, b, l):
        with self.k.scope():
            for _ in self.fn_gen(b, l):
                pass

    def fn_gen(self, b, l):
        k, nc = self.k, self.nc
        need_ctx = (l < DEPTH - 1)
        if True:
            U = k.sb("U", [128, 18, 256], BF16)
            k.dma("sp", U[:, :, :], self.ztm[:, 640:896].rearrange("(kt p) c -> p kt c", p=128), r=[self.ztm], w=[U])
            Wfn = k.sb("Wfn", [128, 2, 256], BF16)
            k.dma("pool", Wfn[:, :, :], self.w_fnet[l, :, :].rearrange("(ct p) o -> p ct o", p=128), r=[self.w_fnet], w=[Wfn])
            CS = [k.sb("CS%d" % i, [128, 2, 16, 512], BF16) for i in range(2)]
            AB = k.sb("AB", [128, 4, 512], BF16)
            Fs = k.sb("Fs", [128, 2, 512], BF16)
            OD = [k.sb("OD%d" % i, [128, 512], BF16) for i in range(2)]
            jobs = []
            if need_ctx:
                jobs.append((0, 256, 0, 2, self.dftc, 0, 1.0 / math.sqrt(256 * 64)))
            for nb in range(4):
                jobs.append((256 + nb * 512, 512, 2, 16, self.dftn, nb * 512, 1.0 / math.sqrt(2048 * 64)))
            def load_cs(ji):
                if ji < len(jobs):
                    t0, n, kt0, nkt, tab, c0, fscale = jobs[ji]
                    for cs_i in range(2):
                        k.dma("sp", CS[ji % 2][:, cs_i, :nkt, :n], tab[cs_i, :, c0:c0 + n].rearrange("(kt p) c -> p kt c", p=128), r=[tab], w=[CS[ji % 2]])
            load_cs(0)
            load_cs(1)
            yield
            for ji, (t0, n, kt0, nkt, tab, c0, fscale) in enumerate(jobs):
                cs = CS[ji % 2]
                if ji >= 1:
                    load_cs(ji + 1)
                for cs_i in range(2):
                    for ct in range(2):
                        bk = self.bank()
                        yield from self.mm_g(self.pap(bk, n), [self.psb[bk]],
                                             [(U[:, kt0 + kt, ct * 128:(ct + 1) * 128], cs[:, cs_i, kt, :n]) for kt in range(nkt)], [U, cs], every=4)
                        idx = cs_i * 2 + ct
                        if False:
                            pass
                        else:
                            k.op("dve", lambda bk=bk, idx=idx: nc.vector.tensor_copy(out=AB[:, idx, :n], in_=self.pap(bk, n)), r=[self.psb[bk]], w=[AB])
                        yield
                for ct in range(2):
                    bk = self.bank()
                    self.mm(self.pap(bk, n), [self.psb[bk]], [(self.CC, AB[:, ct, :n]), (self.SCN, AB[:, 2 + ct, :n])], [AB, self.C])
                    k.op("dve", lambda bk=bk, ct=ct: nc.vector.tensor_scalar(out=Fs[:, ct, :n], in0=self.pap(bk, n), scalar1=fscale, scalar2=None, op0=ALU.mult), r=[self.psb[bk]], w=[Fs])
                for ot in range(2):
                    bk = self.bank()
                    self.mm(self.pap(bk, n), [self.psb[bk]], [(Wfn[:, ct, ot * 128:(ot + 1) * 128], Fs[:, ct, :n]) for ct in range(2)], [Wfn, Fs])
                    od = OD[ot]
                    k.op("dve", lambda bk=bk, ot=ot, od=od: nc.vector.tensor_scalar(out=od[:, :n], in0=self.pap(bk, n), scalar1=self.bfn_sb[:, l, ot:ot + 1], scalar2=None, op0=ALU.add),
                         r=[self.psb[bk], self.bfn_sb], w=[od])
                    k.dma("sp", self.mixT[768 + ot * 128:768 + (ot + 1) * 128, t0:t0 + n], od[:, :n], r=[od], w=[self.mixT])
                yield

    def ph_out_mlp(self, b, l, xsrc):
        k, nc = self.k, self.nc
        need_ctx = (l < DEPTH - 1)
        last = not need_ctx
        blks = TBLK if need_ctx else TBLK[1:]
        with k.scope():
            H2 = k.sb("H2", [128, 8, T], BF16)
            WB = k.sb("WB", [128, 8192], BF16)
            W1 = [Buf("W1v%d" % i, WB[:, i * 2048:(i + 1) * 2048].rearrange("p (kt c) -> p kt c", c=256)) for i in range(4)]
            W2 = [Buf("W2v%d" % i, WB[:, i * 4096:(i + 1) * 4096].rearrange("p (ft c) -> p ft c", c=128)) for i in range(2)]

            def load_w1(fp):
                if fp < 16:
                    k.dma("pool", W1[fp % 4][:, :, :], self.w_mlp1[l, :, fp * 256:(fp + 1) * 256].rearrange("(kt p) c -> p kt c", p=128), r=[self.w_mlp1], w=[W1[fp % 4]])
            with k.scope():
                Wout = k.sb("Wout", [128, 8, D], BF16)
                k.dma("sp", Wout[:, 0:4, :], self.wout_bf[l, :, 0:4, :], r=[self.wout_bf], w=[Wout])
                k.dma("act", Wout[:, 4:8, :], self.wout_bf[l, :, 4:8, :], r=[self.wout_bf], w=[Wout])
                for fp in range(4):
                    load_w1(fp)
                Xs = [k.sb("X%d" % i, [128, 8, 512]) for i in range(3)]
                Ms = [k.sb("M%d" % i, [128, 8, 512], BF16) for i in range(3)]
                Ys = [k.sb("Y%d" % i, [128, 8, 512]) for i in range(2)]
                W = {"SQ": k.sb("SQ", [128, 8, 512], BF16), "RS": k.sb("RS", [128, 512]), "TMP": k.sb("TMP", [128, 512]),
                     "XN": k.sb("XN", [128, 8, 512])}

                def loads3(bi):
                    if bi < len(blks):
                        t0, n, mj = blks[bi]
                        k.dma("sp", Xs[bi % 3][:, :, :n], self.xsl(xsrc, t0, n), r=[xsrc[0]], w=[Xs[bi % 3]])
                        k.dma("act", Ms[bi % 3][:, :, :n], self.mixT[:, t0:t0 + n].rearrange("(kt p) t -> p kt t", p=128), r=[self.mixT], w=[Ms[bi % 3]])

                def stage_mm(bi):
                    t0, n, mj = blks[bi]
                    X, M, Y = Xs[bi % 3], Ms[bi % 3], Ys[bi % 2]
                    loads3(bi + 1)
                    for ot in range(8):
                        bk = self.bank()
                        self.mm(self.pap(bk, n), [self.psb[bk]],
                                [(Wout[:, kt, ot * 128:(ot + 1) * 128], M[:, kt, :n]) for kt in range(8)], [Wout, M])
                        if ot % 2 == 0:
                            k.op("act", lambda bk=bk, ot=ot: nc.scalar.copy(out=Y[:, ot, :n], in_=self.pap(bk, n)), r=[self.psb[bk]], w=[Y])
                        else:
                            k.op("dve", lambda bk=bk, ot=ot: nc.vector.tensor_copy(out=Y[:, ot, :n], in_=self.pap(bk, n)), r=[self.psb[bk]], w=[Y])
                        yield

                def stage_post(bi):
                    t0, n, mj = blks[bi]
                    j = b if mj is None else mj
                    X, Y = Xs[bi % 3], Ys[bi % 2]
                    yield from self.r_post_g(Y, X, n, l, 2, j, W)
                    k.dma("sp", self.xA[:, t0:t0 + n].rearrange("(kt p) t -> p kt t", p=128), X[:, :, :n], r=[X], w=[self.xA])
                    yield from self.r_norm_g(X, n, l, 3, 4, j, H2, slice(t0, t0 + n), W, split=True)

                loads3(0)
                for _ in stage_mm(0):
                    pass
                for bi in range(len(blks)):
                    nxt = stage_mm(bi + 1) if bi + 1 < len(blks) else None
                    if nxt is None:
                        for _ in stage_post(bi):
                            pass
                    else:
                        self.weave(nxt, stage_post(bi), 4)
            with k.scope():
                HID = k.sb("HID", [128, 32, T], BF16)
                with k.scope():
                    RT = [k.sb("RT%d" % i, [128, 512]) for i in range(2)]
                    ri = 0
                    for fp in range(16):
                        w1 = W1[fp % 4]
                        if fp >= 1:
                            load_w1(fp + 3)
                        for fs in range(2):
                            f = fp * 2 + fs
                            for (t0, n, mj) in blks:
                                bk = self.bank()
                                self.mm(self.pap(bk, n), [self.psb[bk]],
                                        [(w1[:, kt, fs * 128:(fs + 1) * 128], H2[:, kt, t0:t0 + n]) for kt in range(8)], [w1, H2])
                                rt = RT[ri % 2]
                                ri += 1
                                k.op("act", lambda bk=bk, rt=rt, n=n: nc.scalar.activation(out=rt[:, :n], in_=self.pap(bk, n), func=AF.Relu), r=[self.psb[bk]], w=[rt])
                                k.op("dve", lambda rt=rt, f=f, t0=t0, n=n: nc.vector.tensor_tensor(out=HID[:, f, t0:t0 + n], in0=rt[:, :n], in1=rt[:, :n], op=ALU.mult), r=[rt], w=[HID])
                with k.scope():
                    YS = [k.sb("YS%d" % i, [128, 512]) for i in range(2)]
                    yi = 0
                    for ot in range(8):
                        w2 = W2[ot % 2]
                        k.dma("pool", w2[:, :, :], self.w_mlp2[l, :, ot * 128:(ot + 1) * 128].rearrange("(ft p) c -> p ft c", p=128), r=[self.w_mlp2], w=[w2])
                        for (t0, n, mj) in blks:
                            bk = self.bank()
                            self.mm(self.pap(bk, n), [self.psb[bk]], [(w2[:, f, :], HID[:, f, t0:t0 + n]) for f in range(32)], [w2, HID])
                            ys = YS[yi % 2]
                            if yi % 2 == 0:
                                k.op("act", lambda bk=bk, ys=ys, n=n: nc.scalar.copy(out=ys[:, :n], in_=self.pap(bk, n)), r=[self.psb[bk]], w=[ys])
                            else:
                                k.op("dve", lambda bk=bk, ys=ys, n=n: nc.vector.tensor_copy(out=ys[:, :n], in_=self.pap(bk, n)), r=[self.psb[bk]], w=[ys])
                            yi += 1
                            k.dma("sp", self.yscr[ot * 128:(ot + 1) * 128, t0:t0 + n], ys[:, :n], r=[ys], w=[self.yscr])
        if not last:
            return
        with k.scope():
            nb_ = len(blks)
            Xs = [k.sb("X%d" % i, [128, 8, 512]) for i in range(nb_)]
            Ys = [k.sb("Y%d" % i, [128, 8, 512]) for i in range(nb_)]
            Ws = [{"SQ": k.sb("SQ", [128, 8, 512], BF16), "RS": k.sb("RS", [128, 512]), "TMP": k.sb("TMP", [128, 512]),
                   "XN": k.sb("XN", [128, 8, 512])} for _ in range(2)]
            for bi, (t0, n, mj) in enumerate(blks):
                k.dma("sp", Xs[bi][:, :, :n], self.xA[:, t0:t0 + n].rearrange("(kt p) t -> p kt t", p=128), r=[self.xA], w=[Xs[bi]])
                k.dma("act", Ys[bi][:, :, :n], self.yscr[:, t0:t0 + n].rearrange("(kt p) t -> p kt t", p=128), r=[self.yscr], w=[Ys[bi]])

            def chain(bi):
                t0, n, mj = blks[bi]
                j = b if mj is None else mj
                yield from self.r_post_g(Ys[bi], Xs[bi], n, l, 5, j, Ws[bi % 2])
                dst = self.out[b, :, t0 - CTX:t0 - CTX + n].rearrange("(kt p) t -> p kt t", p=128)
                k.dma("sp", dst, Xs[bi][:, :, :n], r=[Xs[bi]], w=[self.out])
            for bi in range(0, nb_, 2):
                self.weave(chain(bi), chain(bi + 1) if bi + 1 < nb_ else None, 1)
            if self.dbg and "d_x1" in self.dbg and b == 0 and l == 0:
                self.dump("d_x1", self.xB, [D, T], F32)


OFF = dict(q=0, k=256, v=384, s5=512, nq=768, nk=1024, nv=1280, fn=1536)
SWAP64 = np.concatenate([np.arange(16, 32), np.arange(0, 16), np.arange(48, 64), np.arange(32, 48)])


def _bf16(a):
    import ml_dtypes
    return np.asarray(a, dtype=np.float32).astype(ml_dtypes.bfloat16)


def host_consts():
    c = {}
    pos = np.arange(S)
    row, colp = pos // 64, pos % 64
    inv = 10000.0 ** (-np.arange(16, dtype=np.float32) / 16.0)
    cos = np.ones((64, T), np.float32)
    sin = np.zeros((64, T), np.float32)
    for d in range(64):
        p = row if d < 32 else colp
        ang = p.astype(np.float32) * inv[d % 16]
        cos[d, CTX:] = np.cos(ang)
        sgn = -1.0 if (d % 32) < 16 else 1.0
        sin[d, CTX:] = sgn * np.sin(ang)
    c["rope"] = np.stack([np.concatenate([cos, cos], 0), np.concatenate([sin, sin], 0)]).astype(np.float32)
    eye = np.eye(128, dtype=np.float32)
    bones = np.kron(np.eye(2, dtype=np.float32), np.ones((64, 64), np.float32))
    cc = np.arange(64)
    ang = 2 * np.pi * ((cc[:, None] * cc[None, :]) % 64) / 64.0
    Cc = np.kron(np.eye(2), np.cos(ang))
    Sc = np.kron(np.eye(2), np.sin(ang))
    c["cbf"] = _bf16(np.stack([eye, eye[::-1], np.ones((128, 128)), bones, Cc, -Sc], 1))
    for nm, N in (("dftn", S), ("dftc", CTX)):
        n = np.arange(N, dtype=np.int64)
        a = 2 * np.pi * ((n[:, None] * n[None, :]) % N).astype(np.float64) / N
        c[nm] = _bf16(np.stack([np.cos(a), np.sin(a)]))
    return c


def host_prep(inp, core):
    f = lambda a: np.ascontiguousarray(np.asarray(a, dtype=np.float32))
    m = {}
    bs = slice(core * NB, (core + 1) * NB)
    x, ctx = f(inp["x"])[bs], f(inp["ctx"])[bs]
    m["xin"] = np.ascontiguousarray(np.concatenate([ctx, x], axis=1).transpose(0, 2, 1))
    cv = np.concatenate([f(inp["c"])[bs], f(inp["c_ctx"])[None]], 0)
    m["cT"] = np.ascontiguousarray(cv.reshape(3, 8, 128).transpose(2, 1, 0))
    return m


def host_nbias(rel_bias):
    rb = np.asarray(rel_bias, np.float32)
    out = np.empty((DEPTH, 4, 20, 128, 512), np.float32)
    a_l = np.arange(2)[:, None, None, None]
    kk = np.arange(64)[None, :, None, None]
    r_l = np.arange(8)[None, None, :, None]
    jj = np.arange(64)[None, None, None, :]
    cs = np.clip(jj - 8, 0, 48)
    for qb in range(4):
        for m in Prog.NA_TILES[qb]:
            if qb == 2:
                continue
            a = 2 * m + a_l
            r = 8 * qb + r_l
            rs = np.clip(r - 4, 0, 24)
            valid = (a >= rs) & (a < rs + 8) & (kk >= cs) & (kk < cs + 16)
            ia = np.clip(a - r + 7, 0, 14) + 0 * kk + 0 * jj
            ik = np.clip(kk - jj + 15, 0, 30) + 0 * a_l + 0 * r_l
            valid = np.broadcast_to(valid, (2, 64, 8, 64))
            g = rb[:, :, np.broadcast_to(ia, (2, 64, 8, 64)), np.broadcast_to(ik, (2, 64, 8, 64))]
            g = np.where(valid[None, None], g, np.float32(-30000.0))
            out[:, :, Prog.na_tidx(qb, m)] = g.reshape(DEPTH, 4, 128, 512)
    return out


def host_s5(inp):
    f = lambda a: np.asarray(a, dtype=np.float32)
    L = DEPTH
    m = {}
    are, aim, ldt = f(inp["s5_a_re"]), f(inp["s5_a_im"]), f(inp["s5_log_dt"])
    p3 = np.stack([are, aim, np.broadcast_to(ldt[..., None], are.shape)], -1)
    p3 = p3.reshape(L, 32, 64, 3).transpose(0, 2, 1, 3)
    m["s5p"] = np.ascontiguousarray(np.concatenate([p3, p3], 1))
    cre, cim = f(inp["s5_c_re"]), f(inp["s5_c_im"])
    cc = np.stack([cre, cim], 3).reshape(L, 32, 2, 16, 64).transpose(0, 4, 1, 2, 3)
    m["s5c"] = np.ascontiguousarray(np.concatenate([cc, cc], 1))
    bre, bim = f(inp["s5_b_re"]), f(inp["s5_b_im"])
    bb = np.stack([bre, bim], 4).reshape(L, 32, 64, 2, 16).transpose(0, 2, 1, 3, 4)
    m["s5b"] = np.ascontiguousarray(np.concatenate([bb, bb], 1))
    dsk = f(inp["s5_d"]).reshape(L, 16, 16)
    m["s5d"] = np.ascontiguousarray(np.tile(dsk.transpose(0, 2, 1), (1, 8, 1)))
    cf = np.zeros((128, 392), np.float32)
    cf[:64, 0] = 1.0
    cf[64:, 1] = 1.0
    cf[:64, 2] = -1.0
    cf[64:, 3] = -1.0
    cf[:, 4] = np.pi / 2
    cf[:, 5] = EPS
    cf[:, 6] = 1.0
    jj = np.arange(128) // 16
    cf[:, 8:136] = (jj[:, None] <= jj[None, :])
    cf[:, 136:264] = (jj[:, None] >= jj[None, :])
    cf[:, 264:392] = np.eye(128)
    m["cf32"] = cf
    m["w_glu"] = np.ascontiguousarray(f(inp["w_s5_glu"]))
    return m

def host_shared(inp):
    f = lambda a: np.ascontiguousarray(np.asarray(a, dtype=np.float32))
    m = dict(host_consts())
    m["w_ada"] = f(inp["w_ada"])
    m["bada"] = np.ascontiguousarray(f(inp["b_ada"]).reshape(DEPTH, 48, 128).transpose(0, 2, 1))
    g4 = np.stack([f(inp[n]) for n in ("g_pre_mix", "g_post_mix", "g_pre_mlp", "g_post_mlp")], 1)
    m["gvec"] = np.ascontiguousarray(g4.reshape(DEPTH, 4, 8, 128).transpose(0, 3, 1, 2))
    w_in = f(inp["w_in"])
    sw4 = np.concatenate([SWAP64 + 64 * h for h in range(4)])
    sw2 = np.concatenate([SWAP64 + 64 * h for h in range(2)])
    cols_fm = np.concatenate([OFF["q"] + np.arange(256), OFF["q"] + sw4, OFF["k"] + np.arange(128), OFF["k"] + sw2,
                              OFF["nq"] + np.arange(256), OFF["nk"] + np.arange(256)])
    cols_tm = np.concatenate([OFF["v"] + np.arange(128), OFF["nv"] + np.arange(256), OFF["s5"] + np.arange(256), OFF["fn"] + np.arange(256)])
    m["w_fm"] = np.ascontiguousarray(w_in[:, :, cols_fm])
    m["w_tm"] = np.ascontiguousarray(w_in[:, :, cols_tm])
    gq, gk = f(inp["g_q_attn"]), f(inp["g_k_attn"])
    m["gqk"] = np.ascontiguousarray(np.stack([np.tile(gq, (1, 2)), np.tile(gq[:, SWAP64], (1, 2)),
                                              np.tile(gk, (1, 2)), np.tile(gk[:, SWAP64], (1, 2))], 2))
    m["nbias"] = host_nbias(inp["na_rel_bias"])
    m.update(host_s5(inp))
    m["w_fnet"] = f(inp["w_fnet"])
    m["bfn"] = np.ascontiguousarray(f(inp["b_fnet"]).reshape(DEPTH, 2, 128).transpose(0, 2, 1))
    m["w_out"] = f(inp["w_out"])
    m["w_mlp1"] = f(inp["w_mlp1"])
    m["w_mlp2"] = f(inp["w_mlp2"])
    return m


def run_prog(prog, inp, cores):
    shared = host_shared(inp)
    in_maps = []
    for c in cores:
        m = dict(shared)
        m.update(host_prep(inp, c))
        in_maps.append(m)
    return run_bass_kernel_spmd(prog.nc, in_maps, core_ids=list(range(len(cores))))


def kernel(**inputs):
    prog = Prog()
    res = run_prog(prog, inputs, list(range(8)))
    out = np.empty((8 * NB, S, D), np.float32)
    for c in range(8):
        o = res.results[c]["outT"]
        for b in range(NB):
            out[c * NB + b] = o[b].T
    return out
```

```python
import math
import contextlib
import threading
import numpy as np
import concourse.bass as bass
import concourse.mybir as mybir
from concourse.bass_utils import run_bass_kernel_spmd

F32 = mybir.dt.float32
BF16 = mybir.dt.bfloat16
ALU = mybir.AluOpType
AF = mybir.ActivationFunctionType
AX = mybir.AxisListType


class Buf:
    __slots__ = ("name", "t", "lw", "rd")

    def __init__(self, name, t=None):
        self.name = name
        self.t = t
        self.lw = None
        self.rd = {}

    def __getitem__(self, key):
        return self.t[key]


class Alias:
    def __init__(self, parent, ap):
        self.parent, self.t, self.name = parent, ap, parent.name + "_alias"

    def __getitem__(self, key):
        return self.t[key]

    lw = property(lambda self: self.parent.lw, lambda self, v: setattr(self.parent, "lw", v))
    rd = property(lambda self: self.parent.rd, lambda self, v: setattr(self.parent, "rd", v))


class Side:
    def __init__(self, kb, fn, every=4):
        self.kb, self.fn, self.every = kb, fn, every
        self.go, self.back = threading.Event(), threading.Event()
        self.done, self.err, self.n = False, None, 0
        self.thread = threading.Thread(target=self._run, daemon=True)
        self.thread.start()

    def _run(self):
        self.go.wait()
        self.go.clear()
        try:
            self.fn()
        except BaseException as e:
            self.err = e
        finally:
            self.done = True
            self.back.set()

    def step(self):
        if self.done:
            return False
        prev = self.kb.coro
        self.kb.coro = self
        self.n = 0
        self.go.set()
        self.back.wait()
        self.back.clear()
        self.kb.coro = prev
        if self.err is not None:
            raise self.err
        return not self.done

    def tick(self):
        self.n += 1
        if self.n >= self.every:
            self.n = 0
            self.back.set()
            self.go.wait()
            self.go.clear()

    def gen(self):
        while self.step():
            yield

    def drain(self):
        while self.step():
            pass


class KB:
    ROT = 20000

    def __init__(self, nc):
        self.nc = nc
        self.es = contextlib.ExitStack()
        self.eng = {"pe": nc.tensor, "act": nc.scalar, "dve": nc.vector, "pool": nc.gpsimd, "sp": nc.sync}
        self.sems = {}
        self.cur = {}
        self.cnt = {}
        self.seen = {e: {} for e in self.eng}
        self.pend = {e: ([], []) for e in self.eng}
        self.nsem = 0
        for e in self.eng:
            self._newsem(e)
        self.dslots = []
        self.dpool = {"hw": [], "sw": []}
        self.dnext = {"hw": 0, "sw": 0}
        for kind, n in (("hw", 18), ("sw", 8)):
            for i in range(n):
                k = "dma%s%d" % (kind, i)
                self.sems[k] = self.es.enter_context(nc.semaphore(k))
                self.cnt[k] = 0
                self.dslots.append(k)
                self.dpool[kind].append(k)
        self.ninst = 0
        self.coro = None

    def _newsem(self, e):
        k = "%s_%d" % (e, self.nsem)
        self.nsem += 1
        self.sems[k] = self.es.enter_context(self.nc.semaphore(k))
        self.cnt[k] = 0
        self.cur[e] = k

    def sb(self, name, shape, dt=F32):
        self.nsb = getattr(self, "nsb", 0) + 1
        name = "%s_%d" % (name, self.nsb)
        t = self.es.enter_context(self.nc.sbuf_tensor(name, list(shape), dt))
        return Buf(name, t)

    def dram(self, name, shape, dt, kind="Internal"):
        t = self.nc.dram_tensor(name, list(shape), dt, kind=kind)
        return Buf(name, t)

    def _wait(self, e, dep):
        if dep is None:
            return
        k, v = dep
        if self.seen[e].get(k, 0) >= v:
            return
        self.eng[e].wait_ge(self.sems[k], v)
        self.seen[e][k] = v

    def _deps(self, e, r, w):
        for b in r:
            self._wait(e, b.lw)
        for b in w:
            self._wait(e, b.lw)
            for d in list(b.rd.items()):
                self._wait(e, d)

    def op(self, e, fn, r=(), w=(), inc=True):
        self._deps(e, r, w)
        ins = fn()
        self.ninst += 1
        pr, pw = self.pend[e]
        pr.extend(r)
        pw.extend(w)
        if not inc:
            return
        k = self.cur[e]
        self.cnt[k] += 1
        v = self.cnt[k]
        ins.then_inc(self.sems[k], 1)
        self.seen[e][k] = max(self.seen[e].get(k, 0), 0)
        tag = (k, v)
        for b in pr:
            b.rd[k] = v
        for b in pw:
            b.lw = tag
            b.rd = {}
        self.pend[e] = ([], [])
        if v >= self.ROT:
            self._newsem(e)
        if self.coro is not None and threading.current_thread() is self.coro.thread:
            self.coro.tick()

    def dma(self, q, out_ap, in_ap, r=(), w=(), **kw):
        self._deps(q, r, w)
        kind = "sw" if q == "pool" else "hw"
        k = self.dpool[kind][self.dnext[kind]]
        self.dnext[kind] = (self.dnext[kind] + 1) % len(self.dpool[kind])
        if self.cnt[k] > 0:
            self._wait(q, (k, self.cnt[k]))
        ins = self.eng[q].dma_start(out=out_ap, in_=in_ap, **kw)
        self.ninst += 1
        self.cnt[k] += 16
        ins.then_inc(self.sems[k], 16)
        tag = (k, self.cnt[k])
        for b in r:
            b.rd[k] = self.cnt[k]
        for b in w:
            b.lw = tag
            b.rd = {}
        if self.coro is not None and threading.current_thread() is self.coro.thread:
            self.coro.tick()
        return tag

    def finish(self, bufs):
        for b in bufs:
            self._wait("sp", b.lw)
            for d in list(b.rd.items()):
                self._wait("sp", d)

    @contextlib.contextmanager
    def scope(self):
        outer = self.es
        self.es = contextlib.ExitStack()
        try:
            yield
        finally:
            self.barrier()
            self.es.close()
            self.es = outer

    def barrier(self):
        tags = []
        for e in self.eng:
            for key in [kk for kk in self.sems if kk.startswith(e + "_")]:
                if self.cnt[key] > 0:
                    tags.append((key, self.cnt[key]))
        for key in self.dslots:
            if self.cnt[key] > 0:
                tags.append((key, self.cnt[key]))
        for e in self.eng:
            for t in tags:
                self._wait(e, t)


D = 1024
S = 2048
CTX = 256
T = CTX + S
NB = 2
DEPTH = 2
DFF = 4096
EPS = 1e-6
NFM = 1280
NTM = 896
TBLK = [(0, 256, 2)] + [(256 + 512 * i, 512, None) for i in range(4)]


class Prog:
    def __init__(self, dbg=None, nlayers=DEPTH, nbatch=NB, skip=()):
        self.dbg = dbg
        self.skip = set(skip)
        nc = bass.Bass("TRN2", target_bir_lowering=False)
        self.nc = nc
        k = KB(nc)
        self.k = k
        self.nlayers = nlayers
        self.nbatch = nbatch
        ein = lambda n, s, d=F32: Buf(n, nc.dram_tensor(n, list(s), d, kind="ExternalInput"))
        self.xin = ein("xin", [NB, D, T])
        self.cT = ein("cT", [128, 8, 3])
        self.w_ada = ein("w_ada", [DEPTH, D, 6 * D])
        self.bada = ein("bada", [DEPTH, 128, 48])
        self.gvec = ein("gvec", [DEPTH, 128, 4, 8])
        self.w_fm = ein("w_fm", [DEPTH, D, NFM])
        self.w_tm = ein("w_tm", [DEPTH, D, NTM])
        self.rope = ein("rope", [2, 128, T])
        self.gqk = ein("gqk", [DEPTH, 128, 4])
        self.cbf = ein("cbf", [128, 6, 128], BF16)
        self.dftn = ein("dftn", [2, S, S], BF16)
        self.dftc = ein("dftc", [2, CTX, CTX], BF16)
        self.nbias = ein("nbias", [DEPTH, 4, 20, 128, 512])
        self.s5p = ein("s5p", [DEPTH, 128, 32, 3])
        self.s5c = ein("s5c", [DEPTH, 128, 32, 2, 16])
        self.s5b = ein("s5b", [DEPTH, 128, 32, 2, 16])
        self.s5d = ein("s5d", [DEPTH, 128, 16])
        self.cf32 = ein("cf32", [128, 392])
        self.w_glu = ein("w_glu", [DEPTH, 256, 256])
        self.s5w = k.dram("s5w", [DEPTH, 32, 4, 128, 128], BF16)
        self.s5t = k.dram("s5t", [DEPTH, 16, 128, 128], BF16)
        self.s5tab = k.dram("s5tab", [DEPTH, 32, 2, 128, 288], F32)
        self.wfm_bf = k.dram("wfm_bf", [DEPTH, 128, 8, NFM], BF16)
        self.wtm_bf = k.dram("wtm_bf", [DEPTH, 128, 8, NTM], BF16)
        self.wout_bf = k.dram("wout_bf", [DEPTH, 128, 8, D], BF16)
        self.w_fnet = ein("w_fnet", [DEPTH, 256, 256])
        self.bfn = ein("bfn", [DEPTH, 128, 2])
        self.w_out = ein("w_out", [DEPTH, D, D])
        self.w_mlp1 = ein("w_mlp1", [DEPTH, D, DFF])
        self.w_mlp2 = ein("w_mlp2", [DEPTH, DFF, D])
        self.out = Buf("outT", nc.dram_tensor("outT", [NB, D, S], F32, kind="ExternalOutput"))
        self.gqT = k.dram("gqT", [384, T], BF16)
        self.nqk = k.dram("nqk", [512, T], BF16)
        self.ztm = k.dram("ztm", [T, NTM], BF16)
        self.mixT = k.dram("mixT", [D, T], BF16)
        self.xA = k.dram("xA", [D, T], F32)
        self.xB = k.dram("xB", [D, T], F32)
        self.yscr = k.dram("yscr", [D, T], F32)
        if dbg:
            self.dbgout = {n: Buf(n, nc.dram_tensor(n, list(s), d, kind="ExternalOutput")) for n, (s, d) in dbg.items()}
        pst = k.es.enter_context(nc.psum_tensor("ps", [128, 4096], F32))
        self.pst = pst
        self.psb = [Buf("psb%d" % i, pst) for i in range(8)]
        self.bank_rr = 0
        self.build()

    def bank(self):
        bs = getattr(self, "bank_set", None)
        if bs:
            self.bs_rr = getattr(self, "bs_rr", 0) + 1
            return bs[self.bs_rr % len(bs)]
        b = self.bank_rr
        self.bank_rr = (self.bank_rr + 1) % getattr(self, "bank_lim", 8)
        return b

    def pap(self, b, n=512, p0=0, p1=128):
        return self.pst[p0:p1, b * 512:b * 512 + n]

    def mm(self, out_ap, wbufs, pairs, rbufs):
        k, nc = self.k, self.nc
        n = len(pairs)
        for i, (l, r) in enumerate(pairs):
            k.op("pe", lambda l=l, r=r, i=i: nc.tensor.matmul(out_ap, lhsT=l, rhs=r, start=(i == 0), stop=(i == n - 1)),
                 r=rbufs, w=wbufs, inc=(i == n - 1))

    def mm_g(self, out_ap, wbufs, pairs, rbufs, every=4):
        k, nc = self.k, self.nc
        n = len(pairs)
        for i, (l, r) in enumerate(pairs):
            k.op("pe", lambda l=l, r=r, i=i: nc.tensor.matmul(out_ap, lhsT=l, rhs=r, start=(i == 0), stop=(i == n - 1)),
                 r=rbufs, w=wbufs, inc=(i == n - 1))
            if (i + 1) % every == 0 and i != n - 1:
                yield

    def build(self):
        k, nc = self.k, self.nc
        self.C = k.sb("cbf_sb", [128, 6, 128], BF16)
        k.dma("sp", self.C[:, :, :], self.cbf[:, :, :], r=[self.cbf], w=[self.C])
        self.IDENT = self.C[:, 0, :]
        self.JREV = self.C[:, 1, :]
        self.ONES = self.C[:, 2, :]
        self.BONES = self.C[:, 3, :]
        self.CC = self.C[:, 4, :]
        self.SCN = self.C[:, 5, :]
        self.mod = k.sb("mod", [128, DEPTH, 6, 8, 3])
        self.cols = k.sb("cols", [128, DEPTH, 6, 8, 3])
        self.gv_sb = k.sb("gvec_sb", [128, DEPTH, 4, 8])
        self.gqk_sb = k.sb("gqk_sb", [128, DEPTH, 4])
        self.bfn_sb = k.sb("bfn_sb", [128, DEPTH, 2])
        for l in range(DEPTH):
            k.dma("sp", self.gv_sb[:, l, :, :], self.gvec[l, :, :, :], r=[self.gvec], w=[self.gv_sb])
            k.dma("sp", self.gqk_sb[:, l, :], self.gqk[l, :, :], r=[self.gqk], w=[self.gqk_sb])
            k.dma("sp", self.bfn_sb[:, l, :], self.bfn[l, :, :], r=[self.bfn], w=[self.bfn_sb])
        self.CFb = k.sb("cf32_sb", [128, 392])
        k.dma("sp", self.CFb[:, :], self.cf32[:, :], r=[self.cf32], w=[self.CFb])
        self.CF = self.CFb
        self.IDF = self.CFb[:, 264:392]
        self.rdec = k.sb("rdec", [128, DEPTH, 32])
        s5on = "s5" not in self.skip
        self.ph_adaln(side=Side(k, lambda: self.ph_s5prep(0), every=6) if s5on else None)
        self.prep1_pending = s5on and self.nlayers > 1
        if self.prep1_pending and "gqa" in self.skip:
            self.ph_s5prep(1)
            self.prep1_pending = False
        for b in range(self.nbatch):
            for l in range(self.nlayers):
                self.layer(b, l)
        fin = [self.out] + (list(self.dbgout.values()) if self.dbg else [])
        k.barrier()
        k.finish(fin)

    def ph_castw(self):
        k = self.k
        if True:
            st = [k.sb("cast%d" % i, [128, 2, NFM], BF16) for i in range(2)]
            i = 0
            for l in range(self.nlayers):
                for src, dst, ncol in ((self.w_fm, self.wfm_bf, NFM), (self.w_tm, self.wtm_bf, NTM), (self.w_out, self.wout_bf, D)):
                    for k0 in range(0, 8, 2):
                        t = st[i % 2]
                        i += 1
                        k.dma("pool", t[:, :, :ncol], src[l, k0 * 128:(k0 + 2) * 128, :].rearrange("(kt p) c -> p kt c", p=128), r=[src], w=[t])
                        k.dma("sp", dst[l, :, k0:k0 + 2, :], t[:, :, :ncol], r=[t], w=[dst])
                        yield

    def ph_adaln(self, side=None):
        k, nc = self.k, self.nc
        with k.scope():
            c_sb = k.sb("c_sb", [128, 8, 3])
            cs = k.sb("cs", [128, 8, 3], BF16)
            sg = k.sb("sg", [128, 8, 3])
            bada_sb = k.sb("bada_sb", [128, DEPTH, 48])
            k.dma("sp", c_sb[:, :, :], self.cT[:, :, :], r=[self.cT], w=[c_sb])
            for l in range(DEPTH):
                k.dma("sp", bada_sb[:, l, :], self.bada[l, :, :], r=[self.bada], w=[bada_sb])
            k.op("act", lambda: nc.scalar.activation(out=sg[:, :, :], in_=c_sb[:, :, :], func=AF.Sigmoid), r=[c_sb], w=[sg])
            k.op("dve", lambda: nc.vector.tensor_tensor(out=cs[:, :, :], in0=c_sb[:, :, :], in1=sg[:, :, :], op=ALU.mult), r=[c_sb, sg], w=[cs])
            wb = [k.sb("wada%d" % i, [128, 8, 512], BF16) for i in range(2)]
            castg = self.ph_castw()
            ci = 0
            for l in range(DEPTH):
                for ch in range(12):
                    w = wb[ci % 2]
                    ci += 1
                    k.dma("pool", w[:, :, :], self.w_ada[l, :, ch * 512:(ch + 1) * 512].rearrange("(kt p) c -> p kt c", p=128),
                          r=[self.w_ada], w=[w])
                    next(castg, None)
                    for s in range(4):
                        ft = ch * 4 + s
                        bk = self.bank()
                        self.mm(self.pap(bk, 3), [self.psb[bk]],
                                [(w[:, kt, s * 128:(s + 1) * 128], cs[:, kt, :]) for kt in range(8)], [w, cs])
                        i6, t8 = ft // 8, ft % 8
                        k.op("dve", lambda bk=bk, l=l, i6=i6, t8=t8, ft=ft: nc.vector.tensor_scalar(
                            out=self.mod[:, l, i6, t8, :], in0=self.pap(bk, 3), scalar1=bada_sb[:, l, ft:ft + 1], scalar2=None, op0=ALU.add),
                            r=[self.psb[bk], bada_sb], w=[self.mod])
                        if side is not None:
                            side.step()
            for _ in castg:
                pass
            for l in range(DEPTH):
                def bc(j):
                    return self.gv_sb[:, l, j, :].unsqueeze(2).to_broadcast([128, 8, 3])
                md, co = self.mod, self.cols
                k.op("dve", lambda l=l: nc.vector.scalar_tensor_tensor(out=co[:, l, 0, :, :], in0=md[:, l, 1, :, :], scalar=1.0, in1=bc(0), op0=ALU.add, op1=ALU.mult), r=[md, self.gv_sb], w=[co])
                k.op("dve", lambda l=l: nc.vector.tensor_copy(out=co[:, l, 1, :, :], in_=md[:, l, 0, :, :]), r=[md], w=[co])
                k.op("dve", lambda l=l: nc.vector.tensor_tensor(out=co[:, l, 2, :, :], in0=md[:, l, 2, :, :], in1=bc(1), op=ALU.mult), r=[md, self.gv_sb], w=[co])
                k.op("dve", lambda l=l: nc.vector.scalar_tensor_tensor(out=co[:, l, 3, :, :], in0=md[:, l, 4, :, :], scalar=1.0, in1=bc(2), op0=ALU.add, op1=ALU.mult), r=[md, self.gv_sb], w=[co])
                k.op("dve", lambda l=l: nc.vector.tensor_copy(out=co[:, l, 4, :, :], in_=md[:, l, 3, :, :]), r=[md], w=[co])
                k.op("dve", lambda l=l: nc.vector.tensor_tensor(out=co[:, l, 5, :, :], in0=md[:, l, 5, :, :], in1=bc(3), op=ALU.mult), r=[md, self.gv_sb], w=[co])
            if side is not None:
                side.drain()

    def col(self, l, which, kt, j):
        return self.cols[:, l, which, kt, j:j + 1]

    def rstd_g(self, SQ, n, scale, out_rs, tmpbuf):
        k, nc = self.k, self.nc
        bk = self.bank()
        self.mm(self.pap(bk, n), [self.psb[bk]], [(self.ONES, SQ[:, kt, :n]) for kt in range(8)], [SQ, self.C])
        yield
        k.op("act", lambda: nc.scalar.activation(out=tmpbuf[:, :n], in_=self.pap(bk, n), func=AF.Ln, scale=scale, bias=self.CF[:, 5:6]),
             r=[self.psb[bk], self.CFb], w=[tmpbuf])
        k.op("act", lambda: nc.scalar.activation(out=out_rs[:, :n], in_=tmpbuf[:, :n], func=AF.Exp, scale=-0.5), r=[tmpbuf], w=[out_rs])
        yield

    def r_norm_g(self, X, n, l, wa, ws, j, Hout, hsl, W, split=False):
        k, nc = self.k, self.nc
        SQ, RS, TMP, XN = W["SQ"], W["RS"], W["TMP"], W["XN"]
        k.op("act", lambda: nc.scalar.activation(out=SQ[:, :, :n], in_=X[:, :, :n], func=AF.Square), r=[X], w=[SQ])
        yield
        yield from self.rstd_g(SQ, n, 1.0 / D, RS, TMP)
        k.op("dve", lambda: nc.vector.tensor_tensor(out=XN[:, :, :n], in0=X[:, :, :n], in1=RS[:, :n].unsqueeze(1).to_broadcast([128, 8, n]), op=ALU.mult),
             r=[X, RS], w=[XN])
        yield
        for kt in range(8):
            if kt % 2 == 0 or not split:
                k.op("act", lambda kt=kt: nc.scalar.activation(out=Hout[:, kt, hsl], in_=XN[:, kt, :n], func=AF.Identity,
                                                               scale=self.col(l, wa, kt, j), bias=self.col(l, ws, kt, j)),
                     r=[XN, self.cols], w=[Hout])
            else:
                k.op("dve", lambda kt=kt: nc.vector.tensor_scalar(out=Hout[:, kt, hsl], in0=XN[:, kt, :n], scalar1=self.col(l, wa, kt, j),
                                                                  scalar2=self.col(l, ws, kt, j), op0=ALU.mult, op1=ALU.add),
                     r=[XN, self.cols], w=[Hout])
            yield

    def r_post_g(self, Y, X, n, l, wg, j, W):
        k, nc = self.k, self.nc
        SQ, RS, TMP, XN = W["SQ"], W["RS"], W["TMP"], W["XN"]
        k.op("act", lambda: nc.scalar.activation(out=SQ[:, :, :n], in_=Y[:, :, :n], func=AF.Square), r=[Y], w=[SQ])
        yield
        yield from self.rstd_g(SQ, n, 1.0 / D, RS, TMP)
        k.op("dve", lambda: nc.vector.tensor_tensor(out=XN[:, :, :n], in0=Y[:, :, :n], in1=RS[:, :n].unsqueeze(1).to_broadcast([128, 8, n]), op=ALU.mult),
             r=[Y, RS], w=[XN])
        yield
        for kt in range(8):
            k.op("dve", lambda kt=kt: nc.vector.scalar_tensor_tensor(out=X[:, kt, :n], in0=XN[:, kt, :n], scalar=self.col(l, wg, kt, j),
                                                                      in1=X[:, kt, :n], op0=ALU.mult, op1=ALU.add),
                 r=[XN, X, self.cols], w=[X])
            yield

    def r_norm(self, *a):
        for _ in self.r_norm_g(*a):
            pass

    def r_post(self, *a):
        for _ in self.r_post_g(*a):
            pass

    @staticmethod
    def weave(main, side, ratio=1):
        for _ in main:
            if side is not None:
                for _r in range(ratio):
                    if next(side, "END") == "END":
                        side = None
                        break
        if side is not None:
            for _ in side:
                pass

    def layer(self, b, l):
        last = (l == DEPTH - 1)
        xsrc = (self.xin, b) if l == 0 else (self.xB, None)
        self.ph_proj(b, l, xsrc)
        merged = ("s5" not in self.skip and "na" not in self.skip)
        es_outer = contextlib.ExitStack()
        s5pre = None
        if merged and "gqa" not in self.skip and not self.prep1_pending:
            es_outer.enter_context(self.k.scope())
            s5pre = self.s5_inputs(b, l)
        for nm, fn_, r0 in (("gqa", self.ph_gqa, 0), ("s5", getattr(self, "ph_s5", None), 256), ("na", getattr(self, "ph_na", None), 512), ("fn", self.ph_fn, 768)):
            if merged and nm == "na":
                continue
            if merged and nm == "s5":
                with self.k.scope():
                    na_loads, run_na = self.na_setup(b, l)
                    self.ph_s5(b, l, after_loads=na_loads, pre=s5pre)
                    if getattr(self, "fn_deferred", False):
                        self.fn_deferred = False
                        run_na(side=self.fn_gen(b, l))
                    else:
                        run_na()
                es_outer.close()
                continue
            if nm in self.skip:
                self.zero_mix(r0, r0 + 256)
            elif nm == "fn" and "gqa" not in self.skip:
                pass
            else:
                fn_(b, l)
        if self.dbg and "d_mix" in self.dbg and b == 0 and l == 0:
            self.dump("d_mix", self.mixT, [D, T], BF16)
        self.ph_out_mlp(b, l, xsrc)

    def xsl(self, xsrc, t0, n):
        buf, bi = xsrc
        ap = buf[bi, :, t0:t0 + n] if bi is not None else buf[:, t0:t0 + n]
        return ap.rearrange("(kt p) t -> p kt t", p=128)

    def ph_proj(self, b, l, xsrc):
        k, nc = self.k, self.nc
        with k.scope():
            Wfm = k.sb("Wfm", [128, 8, NFM], BF16)
            Wtm = k.sb("Wtm", [128, 8, NTM], BF16)
            k.dma("sp", Wfm[:, 0:4, :], self.wfm_bf[l, :, 0:4, :], r=[self.wfm_bf], w=[Wfm])
            k.dma("act", Wfm[:, 4:8, :], self.wfm_bf[l, :, 4:8, :], r=[self.wfm_bf], w=[Wfm])
            k.dma("sp", Wtm[:, :, :], self.wtm_bf[l, :, :, :], r=[self.wtm_bf], w=[Wtm])
            Xs = [k.sb("X%d" % i, [128, 8, 512]) for i in range(2)]
            W = {"SQ": k.sb("SQ", [128, 8, 512], BF16), "RS": k.sb("RS", [128, 512]), "TMP": k.sb("TMP", [128, 512]),
                 "XN": k.sb("XN", [128, 8, 512])}
            Hs = [k.sb("H%d" % i, [128, 8, 512], BF16) for i in range(2)]
            ZQ = k.sb("ZQ", [128, 6, 512])
            ZB = [k.sb("ZB%d" % i, [128, 512], BF16) for i in range(2)]
            RC = [k.sb("RC%d" % i, [128, 2, 512]) for i in range(2)]
            SQh = [k.sb("SQh%d" % i, [128, 512], BF16) for i in range(3)]
            RShs = [k.sb("RSh%d" % i, [128, 512]) for i in range(2)]
            TMhs = [k.sb("TMh%d" % i, [128, 512]) for i in range(2)]
            T1s = [k.sb("T1%d" % i, [128, 512]) for i in range(2)]
            T2s = [k.sb("T2%d" % i, [128, 512]) for i in range(2)]
            QO = [k.sb("QO%d" % i, [128, 512], BF16) for i in range(2)]
            ZT = [k.sb("ZT%d" % i, [128, 4, NTM], BF16) for i in range(2)]
            QK = [(0, 2, 0, 1, 0), (1, 3, 0, 1, 128), (4, 5, 2, 3, 256)]

            fuse = (l > 0)
            Ys = [k.sb("Yp%d" % i, [128, 8, 512]) for i in range(2)] if fuse else None

            def load_x(bi):
                if bi < len(TBLK):
                    t0, n, mj = TBLK[bi]
                    if fuse:
                        k.dma("sp", Xs[bi % 2][:, :, :n], self.xA[:, t0:t0 + n].rearrange("(kt p) t -> p kt t", p=128), r=[self.xA], w=[Xs[bi % 2]])
                        k.dma("sp", Ys[bi % 2][:, :, :n], self.yscr[:, t0:t0 + n].rearrange("(kt p) t -> p kt t", p=128), r=[self.yscr], w=[Ys[bi % 2]])
                    else:
                        k.dma("sp", Xs[bi % 2][:, :, :n], self.xsl(xsrc, t0, n), r=[xsrc[0]], w=[Xs[bi % 2]])

            def stage_n(bi):
                t0, n, mj = TBLK[bi]
                j = b if mj is None else mj
                X, rc = Xs[bi % 2], RC[bi % 2]
                load_x(bi + 1)
                k.dma("act", rc[:, :, :n], self.rope[:, :, t0:t0 + n].rearrange("c p t -> p c t"), r=[self.rope], w=[rc])
                yield
                yield
                if fuse:
                    yield from self.r_post_g(Ys[bi % 2], X, n, l - 1, 5, j, W)
                    k.dma("sp", self.xB[:, t0:t0 + n].rearrange("(kt p) t -> p kt t", p=128), X[:, :, :n], r=[X], w=[self.xB])
                yield from self.r_norm_g(X, n, l, 0, 1, j, Hs[bi % 2], slice(0, n), W)

            def stage_p(bi):
                t0, n, mj = TBLK[bi]
                H, rc = Hs[bi % 2], RC[bi % 2]
                for ot in range(10):
                    bk = self.bank()
                    self.mm(self.pap(bk, n), [self.psb[bk]],
                            [(Wfm[:, kt, ot * 128:(ot + 1) * 128], H[:, kt, :n]) for kt in range(8)], [Wfm, H])
                    if ot < 6:
                        k.op("act", lambda ot=ot, bk=bk: nc.scalar.copy(out=ZQ[:, ot, :n], in_=self.pap(bk, n)), r=[self.psb[bk]], w=[ZQ])
                        if ot in (0, 1, 4):
                            sq = SQh[(0, 1, None, None, 2)[ot]]
                            k.op("act", lambda ot=ot, sq=sq: nc.scalar.activation(out=sq[:, :n], in_=ZQ[:, ot, :n], func=AF.Square), r=[ZQ], w=[sq])
                    else:
                        zb = ZB[ot % 2]
                        sc = 0.125 if ot < 8 else 1.0
                        k.op("dve", lambda zb=zb, bk=bk, sc=sc: nc.vector.tensor_scalar(out=zb[:, :n], in0=self.pap(bk, n), scalar1=sc, scalar2=None, op0=ALU.mult),
                             r=[self.psb[bk]], w=[zb])
                        k.dma("sp", self.nqk[(ot - 6) * 128:(ot - 5) * 128, t0:t0 + n], zb[:, :n], r=[zb], w=[self.nqk])
                    yield
                zt_sb = ZT[bi % 2]
                ns = n // 128
                for s_ in range(ns):
                    for (c0, c1) in [(0, 384), (384, 896)]:
                        bk = self.bank()
                        self.mm(self.pap(bk, c1 - c0), [self.psb[bk]],
                                [(H[:, kt, s_ * 128:(s_ + 1) * 128], Wtm[:, kt, c0:c1]) for kt in range(8)], [Wtm, H])
                        if c0 == 0:
                            k.op("act", lambda bk=bk, s_=s_, c0=c0, c1=c1: nc.scalar.copy(out=zt_sb[:, s_, c0:c1], in_=self.pap(bk, c1 - c0)), r=[self.psb[bk]], w=[zt_sb])
                        else:
                            k.op("dve", lambda bk=bk, s_=s_, c0=c0, c1=c1: nc.vector.tensor_copy(out=zt_sb[:, s_, c0:c1], in_=self.pap(bk, c1 - c0)), r=[self.psb[bk]], w=[zt_sb])
                        yield
                k.dma("sp", self.ztm[t0:t0 + n, :].rearrange("(s p) c -> p s c", p=128), zt_sb[:, :ns, :], r=[zt_sb], w=[self.ztm])
                for qi, (zt, zs, gc, gs, row) in enumerate(QK):
                    sq = SQh[qi]
                    RSh, TMh, T1, T2 = RShs[qi % 2], TMhs[qi % 2], T1s[qi % 2], T2s[qi % 2]
                    bk = self.bank()
                    self.mm(self.pap(bk, n), [self.psb[bk]], [(self.BONES, sq[:, :n])], [sq, self.C])
                    k.op("act", lambda bk=bk: nc.scalar.activation(out=TMh[:, :n], in_=self.pap(bk, n), func=AF.Ln, scale=1.0 / 64, bias=self.CF[:, 5:6]),
                         r=[self.psb[bk], self.CFb], w=[TMh])
                    k.op("act", lambda: nc.scalar.activation(out=RSh[:, :n], in_=TMh[:, :n], func=AF.Exp, scale=-0.5), r=[TMh], w=[RSh])
                    k.op("dve", lambda zt=zt, gc=gc: nc.vector.scalar_tensor_tensor(out=T1[:, :n], in0=ZQ[:, zt, :n], scalar=self.gqk_sb[:, l, gc:gc + 1],
                                                                              in1=rc[:, 0, :n], op0=ALU.mult, op1=ALU.mult), r=[ZQ, rc, self.gqk_sb], w=[T1])
                    k.op("dve", lambda zs=zs, gs=gs: nc.vector.scalar_tensor_tensor(out=T2[:, :n], in0=ZQ[:, zs, :n], scalar=self.gqk_sb[:, l, gs:gs + 1],
                                                                               in1=rc[:, 1, :n], op0=ALU.mult, op1=ALU.mult), r=[ZQ, rc, self.gqk_sb], w=[T2])
                    k.op("dve", lambda: nc.vector.tensor_tensor(out=T1[:, :n], in0=T1[:, :n], in1=T2[:, :n], op=ALU.add), r=[T1, T2], w=[T1])
                    qo = QO[qi % 2]
                    k.op("dve", lambda qo=qo: nc.vector.tensor_tensor(out=qo[:, :n], in0=T1[:, :n], in1=RSh[:, :n], op=ALU.mult), r=[T1, RSh], w=[qo])
                    k.dma("sp", self.gqT[row:row + 128, t0:t0 + n], qo[:, :n], r=[qo], w=[self.gqT])
                    yield

            load_x(0)
            for _ in stage_n(0):
                pass
            for bi in range(len(TBLK)):
                side = stage_n(bi + 1) if bi + 1 < len(TBLK) else None
                self.weave(stage_p(bi), side, 1)
            if self.dbg and "d_gqT" in self.dbg and b == 0 and l == 0:
                self.dump("d_gqT", self.gqT, [384, T], BF16)
                self.dump("d_ztm", self.ztm, [T, NTM], BF16)
                self.dump("d_nqk", self.nqk, [512, T], BF16)

    def dump(self, name, src, shape, dt):
        k = self.k
        o = self.dbgout[name]
        rows = shape[0]
        with k.scope():
            st = k.sb("dump_st", [128, shape[1]], dt)
            for r0 in range(0, rows, 128):
                rr = min(128, rows - r0)
                k.dma("sp", st[:rr, :], src[r0:r0 + rr, :], r=[src], w=[st])
                k.dma("sp", o[r0:r0 + rr, :], st[:rr, :], r=[st], w=[o])

    def zero_mix(self, r0, r1):
        k, nc = self.k, self.nc
        with k.scope():
            z = k.sb("zmix", [128, T], BF16)
            k.op("pool", lambda: nc.gpsimd.memset(z[:, :], 0.0), w=[z])
            for r in range(r0, r1, 128):
                k.dma("sp", self.mixT[r:r + 128, :], z[:, :], r=[z], w=[self.mixT])

    def attn_run(self, blocks, PT, ACC_R, OT, GS=2, sbanks=((0, 1), (2, 3)), abanks=(4, 5), side=None, side_from=14):
        k, nc = self.k, self.nc
        groups = []
        for bi, B in enumerate(blocks):
            kts = B["ktiles"]
            gl = [kts[i:i + GS] for i in range(0, len(kts), GS)]
            for gi, grp in enumerate(gl):
                groups.append((bi, B, gi, grp, gi == 0, gi == len(gl) - 1))

        def emit_s(i):
            bi, B, gi, grp, first, last = groups[i]
            if first and B.get("prep"):
                B["prep"]()
            kT, qT, q0, n, kp, extra = B["kT"], B["qT"], B["q0"], B["n"], B["kp"], B.get("extra")
            banks = sbanks[i % 2]
            pt = PT[i % len(PT)]
            for j, kt in enumerate(grp):
                bk = banks[j]
                pairs = [(kT[0:kp, kt * 128:(kt + 1) * 128], qT[0:kp, q0:q0 + n])]
                rb = [kT, qT]
                if extra is not None and kt in extra:
                    bb, bap = extra[kt]
                    pairs.append((self.IDENT, bap))
                    rb = rb + [bb, self.C]
                self.mm(self.pap(bk, n), [self.psb[bk]], pairs, rb)
            g = len(grp)
            b0 = banks[0]
            src = self.pst[:, b0 * 512:(b0 + g) * 512].rearrange("p (g c) -> p g c", c=512)[:, :, :n]
            k.op("act", lambda: nc.scalar.activation(out=pt[:, :g, :n], in_=src, func=AF.Exp, scale=B["scale"]),
                 r=[self.psb[b0 + j] for j in range(g)], w=[pt])

        def emit_pv(i):
            bi, B, gi, grp, first, last = groups[i]
            V, n, q0 = B["V"], B["n"], B["q0"]
            accb = abanks[bi % 2]
            pt = PT[i % len(PT)]
            g = len(grp)
            for j, kt in enumerate(grp):
                k.op("pe", lambda kt=kt, j=j: nc.tensor.matmul(self.pap(accb, n), lhsT=V[:, kt, :], rhs=pt[:, j, :n],
                                                               start=(first and j == 0), stop=(last and j == g - 1)),
                     r=[V, pt], w=[self.psb[accb]], inc=(j == g - 1))
            if last:
                k.op("dve", lambda: nc.vector.reciprocal(out=ACC_R[0:64, :n], in_=self.pap(accb, n, 64, 128)), r=[self.psb[accb]], w=[ACC_R])
                ot = OT[bi % 2]
                k.op("dve", lambda: nc.vector.tensor_tensor(out=ot[0:64, :n], in0=self.pap(accb, n, 0, 64), in1=ACC_R[0:64, :n], op=ALU.mult),
                     r=[self.psb[accb], ACC_R], w=[ot])
                k.dma("sp", self.mixT[B["row"]:B["row"] + 64, q0:q0 + n], ot[0:64, :n], r=[ot], w=[self.mixT])

        if side is not None:
            next(side, None)
        for i in range(len(groups) + 1):
            if i < len(groups):
                emit_s(i)
            if i >= 1:
                emit_pv(i - 1)
            if side is not None and i >= side_from:
                next(side, None)
        if side is not None:
            for _ in side:
                pass

    def ph_gqa(self, b, l):
        k, nc = self.k, self.nc
        need_ctx = (l < DEPTH - 1)
        with k.scope():
            kTs = [k.sb("kT%d" % i, [128, T], BF16) for i in range(2)]
            qTs = [k.sb("qT%d" % i, [128, T], BF16) for i in range(4)]
            for t_ in kTs + qTs:
                k.op("act", lambda t_=t_: nc.scalar.activation(out=t_[64:128, :], in_=self.CF[64:128, 0:1].to_broadcast([64, T]), func=AF.Copy, scale=0.0),
                     r=[self.CFb], w=[t_])
            Vs = [k.sb("Vaug%d" % i, [128, 18, 128], BF16) for i in range(2)]
            PT = [k.sb("PT%d" % i, [128, 2, 512], BF16) for i in range(3)]
            ACC_R = k.sb("ACC_R", [64, 512])
            OT = [k.sb("OT%d" % i, [64, 512], BF16) for i in range(2)]
            for v in Vs:
                k.op("dve", lambda v=v: nc.vector.memset(v[:, :, 64:128], 1.0), w=[v])
            blocks = []
            for kvh in range(2):
                kT, Vaug = kTs[kvh], Vs[kvh]
                k.dma("sp", kT[0:64, :], self.gqT[256 + 64 * kvh:256 + 64 * (kvh + 1), :], r=[self.gqT], w=[kT])
                k.dma("sp", Vaug[:, :, 0:64], self.ztm[:, 64 * kvh:64 * (kvh + 1)].rearrange("(kt p) c -> p kt c", p=128), r=[self.ztm], w=[Vaug])
                for hh in range(2):
                    h = 2 * kvh + hh
                    qT = qTs[h]
                    k.dma("act", qT[0:64, :], self.gqT[64 * h:64 * (h + 1), :], r=[self.gqT], w=[qT])
                    base = dict(kT=kT, qT=qT, V=Vaug, row=64 * h, scale=0.125, kp=128)
                    if need_ctx:
                        blocks.append(dict(base, q0=0, n=256, ktiles=[0, 1]))
                    for qb in range(4):
                        blocks.append(dict(base, q0=256 + 512 * qb, n=512, ktiles=list(range(18))))
            side = None
            fn_alone = False
            self.bank_set = [6, 7]
            if self.prep1_pending:
                side = Side(k, lambda: self.ph_s5prep(1), every=5).gen()
                self.prep1_pending = False
                fn_alone = "fn" not in self.skip
            elif "fn" not in self.skip:
                side = self.fn_gen(b, l)
            self.attn_run(blocks, PT, ACC_R, OT, side=side, side_from=(0 if fn_alone else 14))
            self.bank_set = None
        if fn_alone:
            if "s5" not in self.skip and "na" not in self.skip:
                self.fn_deferred = True
            else:
                self.ph_fn(b, l)

    def tt(self, eng, out, in0, in1, op, r, w):
        k, nc = self.k, self.nc
        e = nc.vector if eng == "dve" else nc.gpsimd
        k.op(eng, lambda: e.tensor_tensor(out=out, in0=in0, in1=in1, op=op), r=r, w=w)

    def cmul(self, X, Y, TX, TY, xo, yo, tx, ty, are, aim, bre, bim, rb):
        self.tt("dve", xo, are, bre, ALU.mult, rb, [X])
        self.tt("dve", tx, aim, bim, ALU.mult, rb, [TX])
        self.tt("dve", xo, xo, tx, ALU.subtract, [X, TX], [X])
        self.tt("pool", yo, are, bim, ALU.mult, rb, [Y])
        self.tt("pool", ty, aim, bre, ALU.mult, rb, [TY])
        self.tt("pool", yo, yo, ty, ALU.add, [Y, TY], [Y])

    def stack2(self, OUT, out, X, x, sa, Y, y, sb_):
        k, nc = self.k, self.nc
        k.op("act", lambda: nc.scalar.activation(out=out, in_=x, func=AF.Copy, scale=self.CF[:, sa:sa + 1]), r=[X, self.CFb], w=[OUT])
        k.op("dve", lambda: nc.vector.scalar_tensor_tensor(out=out, in0=y, scalar=self.CF[:, sb_:sb_ + 1], in1=out, op0=ALU.mult, op1=ALU.add),
             r=[Y, OUT, self.CFb], w=[OUT])

    def ph_s5prep(self, l):
        k, nc = self.k, self.nc
        Q = 32
        with k.scope():
            PB = k.sb("PB", [128, Q, 3]); CB = k.sb("CB", [128, Q, 2, 16]); BB = k.sb("BB", [128, Q, 2, 16]); DC = k.sb("DC", [128, 16])
            k.dma("sp", PB[:, :, :], self.s5p[l, :, :, :], r=[self.s5p], w=[PB])
            k.dma("sp", CB[:, :, :, :], self.s5c[l, :, :, :, :], r=[self.s5c], w=[CB])
            k.dma("sp", BB[:, :, :, :], self.s5b[l, :, :, :, :], r=[self.s5b], w=[BB])
            k.dma("sp", DC[:, :], self.s5d[l, :, :], r=[self.s5d], w=[DC])
            sm = lambda n, s=(128, Q): k.sb(n, list(s))
            LRE, DT, AR, TH = sm("LRE"), sm("DT"), sm("AR"), sm("TH")
            k.op("dve", lambda: nc.vector.tensor_scalar(out=LRE[:, :], in0=PB[:, :, 0], scalar1=-1e-4, scalar2=None, op0=ALU.min), r=[PB], w=[LRE])
            k.op("act", lambda: nc.scalar.activation(out=DT[:, :], in_=PB[:, :, 2], func=AF.Exp), r=[PB], w=[DT])
            self.tt("dve", AR[:, :], LRE[:, :], DT[:, :], ALU.mult, [LRE, DT], [AR])
            self.tt("dve", TH[:, :], PB[:, :, 1], DT[:, :], ALU.mult, [PB, DT], [TH])
            UC, US, T0, T1 = sm("UC"), sm("US"), sm("T0"), sm("T1")
            k.op("act", lambda: nc.scalar.activation(out=UC[:, :], in_=TH[:, :], func=AF.Sin, scale=1.0 / 16, bias=self.CF[:, 4:5]), r=[TH, self.CFb], w=[UC])
            k.op("act", lambda: nc.scalar.activation(out=US[:, :], in_=TH[:, :], func=AF.Sin, scale=1.0 / 16), r=[TH], w=[US])

            def csq(C_, S_):
                self.tt("dve", T0[:, :], C_[:, :], C_[:, :], ALU.mult, [C_], [T0])
                self.tt("dve", T1[:, :], S_[:, :], S_[:, :], ALU.mult, [S_], [T1])
                k.op("dve", lambda: nc.vector.scalar_tensor_tensor(out=S_[:, :], in0=C_[:, :], scalar=2.0, in1=S_[:, :], op0=ALU.mult, op1=ALU.mult), r=[C_, S_], w=[S_])
                self.tt("dve", C_[:, :], T0[:, :], T1[:, :], ALU.subtract, [T0, T1], [C_])
            for _ in range(4):
                csq(UC, US)
            PR_, PI_ = k.sb("POWre", [128, Q, 9]), k.sb("POWim", [128, Q, 9])
            NR_, NI_ = k.sb("NPOWre", [128, Q, 8]), k.sb("NPOWim", [128, Q, 8])
            UKr, UKi = k.sb("UKr", [128, Q, 9]), k.sb("UKi", [128, Q, 9])
            MG, NMG = k.sb("MG", [128, Q, 9]), k.sb("NMG", [128, Q, 8])
            k.op("dve", lambda: nc.vector.memset(UKr[:, :, 0], 1.0), w=[UKr])
            k.op("dve", lambda: nc.vector.memset(UKi[:, :, 0], 0.0), w=[UKi])
            for kk in range(1, 9):
                self.tt("dve", T0[:, :], UKr[:, :, kk - 1], UC[:, :], ALU.mult, [UKr, UC], [T0])
                self.tt("dve", T1[:, :], UKi[:, :, kk - 1], US[:, :], ALU.mult, [UKi, US], [T1])
                self.tt("dve", UKr[:, :, kk], T0[:, :], T1[:, :], ALU.subtract, [T0, T1], [UKr])
                self.tt("dve", T0[:, :], UKr[:, :, kk - 1], US[:, :], ALU.mult, [UKr, US], [T0])
                self.tt("dve", T1[:, :], UKi[:, :, kk - 1], UC[:, :], ALU.mult, [UKi, UC], [T1])
                self.tt("dve", UKi[:, :, kk], T0[:, :], T1[:, :], ALU.add, [T0, T1], [UKi])
            for kk in range(9):
                k.op("act", lambda kk=kk: nc.scalar.activation(out=MG[:, :, kk], in_=AR[:, :], func=AF.Exp, scale=float(kk)), r=[AR], w=[MG])
            for kk in range(8):
                k.op("act", lambda kk=kk: nc.scalar.activation(out=NMG[:, :, kk], in_=AR[:, :], func=AF.Exp, scale=float(-kk)), r=[AR], w=[NMG])
            self.tt("dve", PR_[:, :, :], UKr[:, :, :], MG[:, :, :], ALU.mult, [UKr, MG], [PR_])
            self.tt("dve", PI_[:, :, :], UKi[:, :, :], MG[:, :, :], ALU.mult, [UKi, MG], [PI_])
            self.tt("dve", NR_[:, :, :], UKr[:, :, 0:8], NMG[:, :, :], ALU.mult, [UKr, NMG], [NR_])
            k.op("dve", lambda: nc.vector.scalar_tensor_tensor(out=NI_[:, :, :], in0=UKi[:, :, 0:8], scalar=-1.0, in1=NMG[:, :, :], op0=ALU.mult, op1=ALU.mult), r=[UKi, NMG], w=[NI_])
            k.op("act", lambda: nc.scalar.activation(out=self.rdec[:, l, :], in_=AR[:, :], func=AF.Exp, scale=8.0), r=[AR], w=[self.rdec])
            with k.scope():
                WC_, WS_ = sm("WC_"), sm("WS_")
                k.op("dve", lambda: nc.vector.tensor_copy(out=WC_[:, :], in_=UKr[:, :, 8]), r=[UKr], w=[WC_])
                k.op("dve", lambda: nc.vector.tensor_copy(out=WS_[:, :], in_=UKi[:, :, 8]), r=[UKi], w=[WS_])
                EC, ES = k.sb("EC", [128, Q, 288]), k.sb("ES", [128, Q, 288])
                TA, TB = k.sb("TA", [128, Q, 128]), k.sb("TB", [128, Q, 128])
                k.op("dve", lambda: nc.vector.memset(EC[:, :, 0:1], 1.0), w=[EC])
                k.op("dve", lambda: nc.vector.memset(ES[:, :, 0:1], 0.0), w=[ES])
                s_ = 1
                while s_ < 288:
                    n = min(s_, 288 - s_)
                    wc = WC_[:, :].unsqueeze(2).to_broadcast([128, Q, n])
                    ws = WS_[:, :].unsqueeze(2).to_broadcast([128, Q, n])
                    self.tt("dve", TA[:, :, :n], EC[:, :, 0:n], wc, ALU.mult, [EC, WC_], [TA])
                    self.tt("dve", TB[:, :, :n], ES[:, :, 0:n], ws, ALU.mult, [ES, WS_], [TB])
                    self.tt("dve", EC[:, :, s_:s_ + n], TA[:, :, :n], TB[:, :, :n], ALU.subtract, [TA, TB], [EC])
                    self.tt("pool", TA[:, :, :n], EC[:, :, 0:n], ws, ALU.mult, [EC, WS_], [TA])
                    self.tt("pool", TB[:, :, :n], ES[:, :, 0:n], wc, ALU.mult, [ES, WC_], [TB])
                    self.tt("pool", ES[:, :, s_:s_ + n], TA[:, :, :n], TB[:, :, :n], ALU.add, [TA, TB], [ES])
                    csq(WC_, WS_)
                    s_ *= 2
                k.dma("sp", self.s5tab[l, :, 0, :, :].rearrange("q p c -> p q c"), EC[:, :, :], r=[EC], w=[self.s5tab])
                k.dma("sp", self.s5tab[l, :, 1, :, :].rearrange("q p c -> p q c"), ES[:, :, :], r=[ES], w=[self.s5tab])
            NRe, L2, CFr, CFi = sm("NRe"), sm("L2"), sm("CFr"), sm("CFi")
            k.op("dve", lambda: nc.vector.tensor_scalar(out=NRe[:, :], in0=PR_[:, :, 1], scalar1=-1.0, scalar2=None, op0=ALU.add), r=[PR_], w=[NRe])
            self.tt("dve", L2[:, :], LRE[:, :], LRE[:, :], ALU.mult, [LRE], [L2])
            self.tt("dve", T0[:, :], PB[:, :, 1], PB[:, :, 1], ALU.mult, [PB], [T0])
            self.tt("dve", L2[:, :], L2[:, :], T0[:, :], ALU.add, [L2, T0], [L2])
            k.op("dve", lambda: nc.vector.reciprocal(out=L2[:, :], in_=L2[:, :]), r=[L2], w=[L2])
            self.tt("dve", CFr[:, :], NRe[:, :], LRE[:, :], ALU.mult, [NRe, LRE], [CFr])
            self.tt("dve", T0[:, :], PI_[:, :, 1], PB[:, :, 1], ALU.mult, [PI_, PB], [T0])
            self.tt("dve", CFr[:, :], CFr[:, :], T0[:, :], ALU.add, [CFr, T0], [CFr])
            self.tt("dve", CFr[:, :], CFr[:, :], L2[:, :], ALU.mult, [CFr, L2], [CFr])
            self.tt("dve", CFi[:, :], PI_[:, :, 1], LRE[:, :], ALU.mult, [PI_, LRE], [CFi])
            self.tt("dve", T0[:, :], NRe[:, :], PB[:, :, 1], ALU.mult, [NRe, PB], [T0])
            self.tt("dve", CFi[:, :], CFi[:, :], T0[:, :], ALU.subtract, [CFi, T0], [CFi])
            self.tt("dve", CFi[:, :], CFi[:, :], L2[:, :], ALU.mult, [CFi, L2], [CFi])
            BBr, BBi = k.sb("BBr", [128, Q, 16]), k.sb("BBi", [128, Q, 16])
            TX16, TY16 = k.sb("TX16", [128, Q, 16]), k.sb("TY16", [128, Q, 16])
            b16 = lambda t_: t_[:, :].unsqueeze(2).to_broadcast([128, Q, 16])
            self.cmul(BBr, BBi, TX16, TY16, BBr[:, :, :], BBi[:, :, :], TX16[:, :, :], TY16[:, :, :],
                      b16(CFr), b16(CFi), BB[:, :, 0, :], BB[:, :, 1, :], [CFr, CFi, BB])
            def ptab(name, SRC_r, SRC_i, k0f, k0r):
                Pr, Pi = k.sb(name + "r", [128, Q, 8]), k.sb(name + "i", [128, Q, 8])
                for (dst, src, SB_) in ((Pr, SRC_r, SRC_r), (Pi, SRC_i, SRC_i)):
                    k.op("act", lambda dst=dst, src=src: nc.scalar.copy(out=dst[:, 0:16, :], in_=src[:, 0:16, k0f[0]:k0f[0] + 8] if k0f[1] > 0 else
                                                                          bass.AP(src.t, src[:, 0:16, k0f[0]:k0f[0] + 1].offset, [list(src[:, 0:16, 0:8].ap[0]), list(src[:, 0:16, 0:8].ap[1]), [-1, 8]])),
                         r=[SB_], w=[dst])
                    k.op("act", lambda dst=dst, src=src: nc.scalar.copy(out=dst[:, 16:32, :], in_=src[:, 16:32, k0r[0]:k0r[0] + 8] if k0r[1] > 0 else
                                                                          bass.AP(src.t, src[:, 16:32, k0r[0]:k0r[0] + 1].offset, [list(src[:, 16:32, 0:8].ap[0]), list(src[:, 16:32, 0:8].ap[1]), [-1, 8]])),
                         r=[SB_], w=[dst])
                return Pr, Pi
            PWCr, PWCi = ptab("PWC", PR_, PI_, (1, 1), (8, -1))
            PLr, PLi = ptab("PL", PR_, PI_, (0, 1), (7, -1))
            PRr, PRi = ptab("PRt", NR_, NI_, (0, 1), (7, -1))
            PW1r, PW1i = ptab("PW1", PR_, PI_, (7, -1), (0, 1))
            big = lambda n: k.sb(n, [128, Q, 8, 16])
            Xb, Yb, TXb, TYb = big("Xb"), big("Yb"), big("TXb"), big("TYb")
            ST1 = k.sb("ST1", [128, Q, 128], BF16)
            ST2 = k.sb("ST2", [128, Q, 128], BF16)
            SF1 = k.sb("SF1", [128, Q, 128])
            SF2 = k.sb("SF2", [128, Q, 128])
            bj = lambda t_, c_: (t_[:, :, c_, :] if c_ is not None else t_[:, :, :]).unsqueeze(2).to_broadcast([128, Q, 8, 16])
            bo = lambda t_: t_[:, :, :].unsqueeze(3).to_broadcast([128, Q, 8, 16])
            f4 = lambda t_: t_[:, :, :, :]
            f3 = lambda t_: t_[:, :, :].rearrange("p q (j o) -> p q j o", o=16)
            self.cmul(Xb, Yb, TXb, TYb, f4(Xb), f4(Yb), f4(TXb), f4(TYb), bo(PWCr), bo(PWCi), bj(CB, 0), bj(CB, 1), [PWCr, PWCi, CB])
            self.stack2(ST1, f3(ST1), Xb, f4(Xb), 0, Yb, f4(Yb), 3)
            self.stack2(ST2, f3(ST2), Yb, f4(Yb), 2, Xb, f4(Xb), 3)
            k.dma("sp", self.s5w[l, :, 2, :, :].rearrange("q p c -> p q c"), ST1[:, :, :], r=[ST1], w=[self.s5w])
            k.dma("sp", self.s5w[l, :, 3, :, :].rearrange("q p c -> p q c"), ST2[:, :, :], r=[ST2], w=[self.s5w])
            self.cmul(Xb, Yb, TXb, TYb, f4(Xb), f4(Yb), f4(TXb), f4(TYb), bo(PW1r), bo(PW1i), bj(BBr, None), bj(BBi, None), [PW1r, PW1i, BBr, BBi])
            f3f = lambda t_: t_[:, :, :].rearrange("p q (j o) -> p q j o", o=16)
            self.stack2(SF1, f3f(SF1), Xb, f4(Xb), 0, Yb, f4(Yb), 1)
            self.stack2(SF2, f3f(SF2), Yb, f4(Yb), 0, Xb, f4(Xb), 3)
            for which, SF in ((0, SF1), (1, SF2)):
                ST = ST1 if which == 0 else ST2
                for q in range(Q):
                    bk = self.bank()
                    self.mm(self.pap(bk, 128), [self.psb[bk]], [(SF[:, q, :], self.IDF)], [SF, self.CFb])
                    if q % 2 == 0:
                        k.op("act", lambda bk=bk, q=q, ST=ST: nc.scalar.copy(out=ST[:, q, :], in_=self.pap(bk, 128)), r=[self.psb[bk]], w=[ST])
                    else:
                        k.op("dve", lambda bk=bk, q=q, ST=ST: nc.vector.tensor_copy(out=ST[:, q, :], in_=self.pap(bk, 128)), r=[self.psb[bk]], w=[ST])
                k.dma("sp", self.s5w[l, :, which, :, :].rearrange("q p c -> p q c"), ST[:, :, :], r=[ST], w=[self.s5w])
            self.cmul(Xb, Yb, TXb, TYb, f4(Xb), f4(Yb), f4(TXb), f4(TYb), bo(PLr), bo(PLi), bj(CB, 0), bj(CB, 1), [PLr, PLi, CB])
            self.stack2(SF1, f3f(SF1), Xb, f4(Xb), 0, Yb, f4(Yb), 3)
            self.cmul(Xb, Yb, TXb, TYb, f4(Xb), f4(Yb), f4(TXb), f4(TYb), bo(PRr), bo(PRi), bj(BBr, None), bj(BBi, None), [PRr, PRi, BBr, BBi])
            self.stack2(SF2, f3f(SF2), Xb, f4(Xb), 0, Yb, f4(Yb), 1)
            TP = Alias(TYb, TYb[:, 0:16, :, :].rearrange("p g j o -> p g (j o)"))
            TPb = Alias(ST1, ST1[:, 0:16, :])
            for g in range(16):
                for d in range(2):
                    q = d * 16 + g
                    bk = self.bank()
                    self.mm(self.pap(bk, 128), [self.psb[bk]], [(SF2[:, q, :], SF1[:, q, :])], [SF1, SF2])
                    msk = self.CF[:, 8 + d * 128:8 + (d + 1) * 128]
                    if d == 0:
                        k.op("dve", lambda bk=bk, g=g, msk=msk: nc.vector.tensor_tensor(out=TP[:, g, :], in0=self.pap(bk, 128), in1=msk, op=ALU.mult),
                             r=[self.psb[bk], self.CFb], w=[TP])
                    else:
                        k.op("dve", lambda bk=bk, g=g, msk=msk: nc.vector.tensor_tensor(out=TXb[:, 0, :, :].rearrange("p j o -> p (j o)"), in0=self.pap(bk, 128), in1=msk, op=ALU.mult),
                             r=[self.psb[bk], self.CFb], w=[TXb])
                        self.tt("dve", TP[:, g, :], TP[:, g, :], TXb[:, 0, :, :].rearrange("p j o -> p (j o)"), ALU.add, [TP, TXb], [TP])
                k.op("dve", lambda g=g: nc.vector.scalar_tensor_tensor(out=TPb[:, g, :], in0=self.IDF, scalar=DC[:, g:g + 1], in1=TP[:, g, :], op0=ALU.mult, op1=ALU.add),
                     r=[TP, DC, self.CFb], w=[TPb])
            k.dma("sp", self.s5t[l, :, :, :].rearrange("g p c -> p g c"), TPb[:, :, :], r=[TPb], w=[self.s5t])

    S5NT = [(0, 32, 0), (32, 128, 256), (160, 128, 256 + 1024)]

    def s5_inputs(self, b, l):
        k, nc = self.k, self.nc
        NT = self.S5NT
        X = [k.sb("Xs%d" % i, [128, 8, 256], BF16) for i in range(3)]
        X2 = [k.sb("X2s%d" % i, [128, 16, 128], BF16) for i in range(3)]
        for i, (c0, ncn, t0) in enumerate(NT):
            k.dma("sp", X[i][0:ncn, :, :], self.ztm[t0:t0 + 8 * ncn, 384:640].rearrange("(c j) ch -> c j ch", j=8), r=[self.ztm], w=[X[i]])
            src = X[i][0:ncn, :, :].rearrange("c j (g i) -> c g j i", i=16)
            dst = X2[i][0:ncn, :, :].rearrange("c g (j i) -> c g j i", i=16)
            k.op("dve", lambda src=src, dst=dst: nc.vector.tensor_copy(out=dst, in_=src), r=[X[i]], w=[X2[i]])
        Wt = [k.sb("S5W%d" % i, [128, 2, 4, 128], BF16) for i in range(3)]
        Tp = [k.sb("S5T%d" % i, [128, 128], BF16) for i in range(3)]
        Tab = [k.sb("S5tab%d" % i, [128, 2, 2, 288]) for i in range(3)]

        def loads(g):
            if g >= 16:
                return
            wt, tp, tab = Wt[g % 3], Tp[g % 3], Tab[g % 3]
            for d in range(2):
                q = d * 16 + g
                k.dma("sp", wt[:, d, :, :], self.s5w[l, q, :, :, :].rearrange("w p c -> p w c"), r=[self.s5w], w=[wt])
                k.dma("act", tab[:, d, :, :], self.s5tab[l, q, :, :, :].rearrange("t p c -> p t c"), r=[self.s5tab], w=[tab])
            k.dma("sp", tp[:, :], self.s5t[l, g, :, :], r=[self.s5t], w=[tp])
        for g in range(3):
            loads(g)
        return dict(X2=X2, Wt=Wt, Tp=Tp, Tab=Tab, loads=loads)

    def ph_s5(self, b, l, after_loads=None, pre=None):
        k, nc = self.k, self.nc
        need_ctx = (l < DEPTH - 1)
        NT = self.S5NT
        with k.scope():
            tT = k.sb("tT", [128, 2, T], BF16)
            Tt = [k.sb("Tt%d" % i, [128, 8, 256], BF16) for i in range(3)]
            with k.scope():
                if pre is None:
                    pre = self.s5_inputs(b, l)
                X2, Wt, Tp, Tab, loads = pre["X2"], pre["Wt"], pre["Tp"], pre["Tab"], pre["loads"]
                if after_loads is not None:
                    after_loads()
                HN = [[[k.sb("HN%d%d%d" % (gp, d, cs), [128, 288], BF16) for cs in range(2)] for d in range(2)] for gp in range(2)]
                for gp in range(2):
                    for d in range(2):
                        for cs in range(2):
                            k.op("pool", lambda gp=gp, d=d, cs=cs: nc.gpsimd.memset(HN[gp][d][cs][:, :], 0.0), w=[HN[gp][d][cs]])
                U = [[k.sb("U%d%d" % (gp, d), [128, 288], BF16) for d in range(2)] for gp in range(2)]
                V1s = [k.sb("V1%d" % i, [128, 288]) for i in range(2)]
                V2s = [k.sb("V2%d" % i, [128, 288]) for i in range(2)]
                HTs = [k.sb("HT%d" % i, [128, 288]) for i in range(2)]
                def stage_a(g, d):
                    wt = Wt[g % 3]
                    u = U[g % 2][d]
                    bk = 0
                    order = [(0, 0), (1, 32), (2, 160)] if d == 0 else [(0, 0), (2, 32), (1, 160)]
                    for (ti, col) in order:
                        ncn = NT[ti][1]
                        rhs = (self.IDENT[0:ncn, 0:ncn] if d == 0 else self.JREV[0:ncn, 128 - ncn:128])
                        k.op("pe", lambda ti=ti, col=col, ncn=ncn, rhs=rhs: nc.tensor.matmul(
                            self.pst[:, bk * 512 + col:bk * 512 + col + ncn], lhsT=X2[ti][0:ncn, g, :], rhs=rhs, start=True, stop=True),
                            r=[X2[ti], self.C], w=[self.psb[bk]], inc=(col == 160))
                    k.op("act", lambda: nc.scalar.copy(out=u[:, :], in_=self.pap(bk, 288)), r=[self.psb[bk]], w=[u])
                    un = d
                    b1, b2 = (1, 2) if un == 0 else (3, 4)
                    self.mm(self.pap(b1, 288), [self.psb[b1]], [(wt[:, d, 0, :], u[:, :])], [wt, u])
                    self.mm(self.pap(b2, 288), [self.psb[b2]], [(wt[:, d, 1, :], u[:, :])], [wt, u])

                def stage_b(g, d):
                    tab = Tab[g % 3]
                    q = d * 16 + g
                    un = d
                    b1, b2 = (1, 2) if un == 0 else (3, 4)
                    V1, V2, HT = V1s[un], V2s[un], HTs[un]
                    self.tt("dve", V1[:, :], self.pap(b1, 288), tab[:, d, 0, :], ALU.mult, [self.psb[b1], tab], [V1])
                    yield
                    self.tt("dve", V2[:, :], self.pap(b2, 288), tab[:, d, 1, :], ALU.mult, [self.psb[b2], tab], [V2])
                    if g + 1 < 16:
                        stage_a(g + 1, d)
                    yield
                    self.tt("dve", V1[:, :], V1[:, :], V2[:, :], ALU.add, [V1, V2], [V1])
                    yield
                    k.op("dve", lambda: nc.vector.tensor_tensor_scan(out=HT[:, :], data0=self.rdec[:, l, q:q + 1].to_broadcast([128, 288]), data1=V1[:, :],
                                                                     initial=0.0, op0=ALU.mult, op1=ALU.add), r=[self.rdec, V1], w=[HT])
                    yield
                    for cs in range(2):
                        hn = HN[g % 2][d][cs]
                        eng = "dve" if cs == 0 else "pool"
                        if d == 0:
                            self.tt(eng, hn[:, 1:288], tab[:, d, cs, 0:287], HT[:, 0:287], ALU.mult, [tab, HT], [hn])
                        else:
                            rev = lambda t_, ap0, start, cnt: bass.AP(t_.t, ap0[:, start:start + 1].offset, [list(ap0[:, 0:cnt].ap[0]), [-1, cnt]])
                            tb = tab[:, d, cs, :]
                            self.tt(eng, hn[:, 32:288], rev(tab, tb, 286, 256), rev(HT, HT[:, :], 286, 256), ALU.mult, [tab, HT], [hn])
                            self.tt(eng, hn[:, 0:31], rev(tab, tb, 30, 31), rev(HT, HT[:, :], 30, 31), ALU.mult, [tab, HT], [hn])

                def stage_c(g):
                    wt, tp = Wt[g % 3], Tp[g % 3]
                    quad, gl = g // 4, g % 4
                    hn = HN[g % 2]
                    for ti, (c0, ncn, t0) in enumerate(NT):
                        if ti == 0 and not need_ctx:
                            continue
                        ob = 5 + ti
                        pairs = []
                        for d in range(2):
                            pairs.append((hn[d][0][:, c0:c0 + ncn], wt[:, d, 2, :]))
                            pairs.append((hn[d][1][:, c0:c0 + ncn], wt[:, d, 3, :]))
                        pairs.append((U[g % 2][0][:, c0:c0 + ncn], tp[:, :]))
                        self.mm(self.pst[0:ncn, ob * 512 + gl * 128:ob * 512 + (gl + 1) * 128], [self.psb[ob]], pairs,
                                [hn[0][0], hn[0][1], hn[1][0], hn[1][1], wt, U[g % 2][0], tp])
                    if gl == 3:
                        for ti, (c0, ncn, t0) in enumerate(NT):
                            if ti == 0 and not need_ctx:
                                continue
                            ob = 5 + ti
                            src = self.pst[0:ncn, ob * 512:(ob + 1) * 512].rearrange("c (g j o) -> c g j o", j=8, o=16)
                            dst = Tt[ti][0:ncn, :, quad * 64:(quad + 1) * 64].rearrange("c j (g o) -> c g j o", o=16)
                            k.op("act", lambda src=src, dst=dst: nc.scalar.activation(out=dst, in_=src, func=AF.Gelu_apprx_tanh), r=[self.psb[ob]], w=[Tt[ti]])

                stage_a(0, 0)
                stage_a(0, 1)
                for g in range(16):
                    self.weave(stage_b(g, 0), stage_b(g, 1), 1)
                    stage_c(g)
                    loads(g + 3)
                self.bank_lim = 8
            for ti, (c0, ncn, t0) in enumerate(NT):
                if ti == 0 and not need_ctx:
                    continue
                for ct in range(2):
                    for jh in range(2):
                        bk = self.bank()
                        for j4 in range(4):
                            j = jh * 4 + j4
                            k.op("pe", lambda j=j, j4=j4, bk=bk, ti=ti, ct=ct, ncn=ncn: nc.tensor.matmul(
                                self.pst[:, bk * 512 + j4 * 128:bk * 512 + j4 * 128 + ncn], lhsT=Tt[ti][0:ncn, j, ct * 128:(ct + 1) * 128],
                                rhs=self.IDENT[0:ncn, 0:ncn], start=True, stop=True), r=[Tt[ti], self.C], w=[self.psb[bk]], inc=(j4 == 3))
                        src = self.pst[:, bk * 512:(bk + 1) * 512].rearrange("p (j c) -> p j c", c=128)[:, :, 0:ncn]
                        dst = tT[:, ct, t0:t0 + 8 * ncn].rearrange("p (c j) -> p j c", j=8)[:, jh * 4:(jh + 1) * 4, :]
                        if (ct + jh) % 2 == 0:
                            k.op("act", lambda src=src, dst=dst: nc.scalar.copy(out=dst, in_=src), r=[self.psb[bk]], w=[tT])
                        else:
                            k.op("dve", lambda src=src, dst=dst: nc.vector.tensor_copy(out=dst, in_=src), r=[self.psb[bk]], w=[tT])
            Wg = k.sb("Wglu", [128, 2, 256], BF16)
            k.dma("pool", Wg[:, :, :], self.w_glu[l, :, :].rearrange("(ct p) o -> p ct o", p=128), r=[self.w_glu], w=[Wg])
            SG = [k.sb("SG%d" % i, [128, 512]) for i in range(2)]
            OB = [k.sb("OB%d" % i, [128, 512], BF16) for i in range(2)]
            blks = TBLK if need_ctx else TBLK[1:]
            ii = 0
            for (t0, n, mj) in blks:
                for ot in range(2):
                    bk = self.bank()
                    self.mm(self.pap(bk, n), [self.psb[bk]], [(Wg[:, ct, ot * 128:(ot + 1) * 128], tT[:, ct, t0:t0 + n]) for ct in range(2)], [Wg, tT])
                    sg, obuf = SG[ii % 2], OB[ii % 2]
                    ii += 1
                    k.op("act", lambda bk=bk, sg=sg, n=n: nc.scalar.activation(out=sg[:, :n], in_=self.pap(bk, n), func=AF.Sigmoid), r=[self.psb[bk]], w=[sg])
                    self.tt("dve", obuf[:, :n], sg[:, :n], tT[:, ot, t0:t0 + n], ALU.mult, [sg, tT], [obuf])
                    k.dma("sp", self.mixT[256 + ot * 128:256 + (ot + 1) * 128, t0:t0 + n], obuf[:, :n], r=[obuf], w=[self.mixT])
            if not need_ctx:
                pass

    NA_TILES = {0: list(range(0, 6)), 1: list(range(2, 10)), 2: list(range(6, 14)), 3: list(range(10, 16))}

    @staticmethod
    def na_tidx(qb, m):
        if qb == 0:
            return m
        if qb == 3:
            return 14 + (m - 10)
        return 6 + (m - (4 * qb - 2))

    def ph_na(self, b, l):
        with self.k.scope():
            issue, run = self.na_setup(b, l)
            issue()
            run()

    def na_setup(self, b, l):
        k, nc = self.k, self.nc
        need_ctx = (l < DEPTH - 1)
        if True:
            kTs = [k.sb("nkT%d" % i, [128, T], BF16) for i in range(4)]
            qTs = [k.sb("nqT%d" % i, [128, T], BF16) for i in range(4)]
            Vs = [k.sb("nV%d" % i, [128, 18, 128], BF16) for i in range(4)]
            BTs = [k.sb("BT%d" % i, [128, 20, 512], BF16) for i in range(2)]
            PT = [k.sb("PT%d" % i, [128, 3, 512], BF16) for i in range(3)]
            ACC_R = k.sb("ACC_R", [64, 512])
            OT = [k.sb("OT%d" % i, [64, 512], BF16) for i in range(2)]
            for v in Vs:
                k.op("act", lambda v=v: nc.scalar.activation(out=v[:, :, 64:128], in_=self.CF[:, 0:1].unsqueeze(2).to_broadcast([128, 18, 64]), func=AF.Identity, scale=0.0, bias=self.CF[:, 6:7]),
                     r=[self.CFb], w=[v])
            for t_ in kTs + qTs:
                k.op("act", lambda t_=t_: nc.scalar.activation(out=t_[64:128, :], in_=self.CF[64:128, 0:1].to_broadcast([64, T]), func=AF.Copy, scale=0.0),
                     r=[self.CFb], w=[t_])

            def load_bt(h):
                BT = BTs[h % 2]
                for t5 in range(4):
                    k.dma("pool", BT[:, t5 * 5:(t5 + 1) * 5, :], self.nbias[l, h, t5 * 5:(t5 + 1) * 5, :, :].rearrange("t p q -> p t q"), r=[self.nbias], w=[BT])
            blocks = []

            def issue_loads():
                for h in range(4):
                    kT, qT, Vaug = kTs[h], qTs[h], Vs[h]
                    k.dma("sp", kT[0:64, :], self.nqk[256 + 64 * h:256 + 64 * (h + 1), :], r=[self.nqk], w=[kT])
                    k.dma("act", qT[0:64, :], self.nqk[64 * h:64 * (h + 1), :], r=[self.nqk], w=[qT])
                    k.dma("sp", Vaug[:, :, 0:64], self.ztm[:, 128 + 64 * h:128 + 64 * (h + 1)].rearrange("(kt p) c -> p kt c", p=128), r=[self.ztm], w=[Vaug])
                    if h < 2:
                        load_bt(h)
            for h in range(4):
                kT, qT, Vaug, BT = kTs[h], qTs[h], Vs[h], BTs[h % 2]
                base = dict(kT=kT, qT=qT, V=Vaug, row=512 + 64 * h, scale=1.0, kp=128)
                hb = []
                if need_ctx:
                    hb.append(dict(base, q0=0, n=256, ktiles=[0, 1]))
                for qb in range(4):
                    ms = self.NA_TILES[qb]
                    extra = {2 + m: (BT, BT[:, self.na_tidx(qb, m), :]) for m in ms}
                    hb.append(dict(base, q0=256 + 512 * qb, n=512, ktiles=[0, 1] + [2 + m for m in ms], extra=extra))
                if 1 <= h < 3:
                    hb[0]["prep"] = (lambda hh=h + 1: load_bt(hh))
                blocks.extend(hb)
            def run(side=None):
                if side is None:
                    self.attn_run(blocks, PT, ACC_R, OT, GS=3, sbanks=((0, 1, 2), (3, 4, 5)), abanks=(6, 7))
                else:
                    self.bank_set = [6, 7]
                    self.attn_run(blocks, PT, ACC_R, OT, GS=2, sbanks=((0, 1), (2, 3)), abanks=(4, 5), side=side, side_from=8)
                    self.bank_set = None
            return issue_loads, run

    def ph_fn(self, b, l):
        with self.k.scope():
            for _ in self.fn_gen(b, l):
                pass

    def fn_gen(self, b, l):
        k, nc = self.k, self.nc
        need_ctx = (l < DEPTH - 1)
        if True:
            U = k.sb("U", [128, 18, 256], BF16)
            k.dma("sp", U[:, :, :], self.ztm[:, 640:896].rearrange("(kt p) c -> p kt c", p=128), r=[self.ztm], w=[U])
            Wfn = k.sb("Wfn", [128, 2, 256], BF16)
            k.dma("pool", Wfn[:, :, :], self.w_fnet[l, :, :].rearrange("(ct p) o -> p ct o", p=128), r=[self.w_fnet], w=[Wfn])
            CS = [k.sb("CS%d" % i, [128, 2, 16, 512], BF16) for i in range(2)]
            AB = k.sb("AB", [128, 4, 512], BF16)
            Fs = k.sb("Fs", [128, 2, 512], BF16)
            OD = [k.sb("OD%d" % i, [128, 512], BF16) for i in range(2)]
            jobs = []
            if need_ctx:
                jobs.append((0, 256, 0, 2, self.dftc, 0, 1.0 / math.sqrt(256 * 64)))
            for nb in range(4):
                jobs.append((256 + nb * 512, 512, 2, 16, self.dftn, nb * 512, 1.0 / math.sqrt(2048 * 64)))
            def load_cs(ji):
                if ji < len(jobs):
                    t0, n, kt0, nkt, tab, c0, fscale = jobs[ji]
                    for cs_i in range(2):
                        k.dma("sp", CS[ji % 2][:, cs_i, :nkt, :n], tab[cs_i, :, c0:c0 + n].rearrange("(kt p) c -> p kt c", p=128), r=[tab], w=[CS[ji % 2]])
            load_cs(0)
            load_cs(1)
            yield
            for ji, (t0, n, kt0, nkt, tab, c0, fscale) in enumerate(jobs):
                cs = CS[ji % 2]
                if ji >= 1:
                    load_cs(ji + 1)
                for cs_i in range(2):
                    for ct in range(2):
                        bk = self.bank()
                        yield from self.mm_g(self.pap(bk, n), [self.psb[bk]],
                                             [(U[:, kt0 + kt, ct * 128:(ct + 1) * 128], cs[:, cs_i, kt, :n]) for kt in range(nkt)], [U, cs], every=4)
                        idx = cs_i * 2 + ct
                        if False:
                            pass
                        else:
                            k.op("dve", lambda bk=bk, idx=idx: nc.vector.tensor_copy(out=AB[:, idx, :n], in_=self.pap(bk, n)), r=[self.psb[bk]], w=[AB])
                        yield
                for ct in range(2):
                    bk = self.bank()
                    self.mm(self.pap(bk, n), [self.psb[bk]], [(self.CC, AB[:, ct, :n]), (self.SCN, AB[:, 2 + ct, :n])], [AB, self.C])
                    k.op("dve", lambda bk=bk, ct=ct: nc.vector.tensor_scalar(out=Fs[:, ct, :n], in0=self.pap(bk, n), scalar1=fscale, scalar2=None, op0=ALU.mult), r=[self.psb[bk]], w=[Fs])
                for ot in range(2):
                    bk = self.bank()
                    self.mm(self.pap(bk, n), [self.psb[bk]], [(Wfn[:, ct, ot * 128:(ot + 1) * 128], Fs[:, ct, :n]) for ct in range(2)], [Wfn, Fs])
                    od = OD[ot]
                    k.op("dve", lambda bk=bk, ot=ot, od=od: nc.vector.tensor_scalar(out=od[:, :n], in0=self.pap(bk, n), scalar1=self.bfn_sb[:, l, ot:ot + 1], scalar2=None, op0=ALU.add),
                         r=[self.psb[bk], self.bfn_sb], w=[od])
                    k.dma("sp", self.mixT[768 + ot * 128:768 + (ot + 1) * 128, t0:t0 + n], od[:, :n], r=[od], w=[self.mixT])
                yield

    def ph_out_mlp(self, b, l, xsrc):
        k, nc = self.k, self.nc
        need_ctx = (l < DEPTH - 1)
        last = not need_ctx
        blks = TBLK if need_ctx else TBLK[1:]
        with k.scope():
            H2 = k.sb("H2", [128, 8, T], BF16)
            WB = k.sb("WB", [128, 8192], BF16)
            W1 = [Buf("W1v%d" % i, WB[:, i * 2048:(i + 1) * 2048].rearrange("p (kt c) -> p kt c", c=256)) for i in range(4)]
            W2 = [Buf("W2v%d" % i, WB[:, i * 4096:(i + 1) * 4096].rearrange("p (ft c) -> p ft c", c=128)) for i in range(2)]

            def load_w1(fp):
                if fp < 16:
                    k.dma("pool", W1[fp % 4][:, :, :], self.w_mlp1[l, :, fp * 256:(fp + 1) * 256].rearrange("(kt p) c -> p kt c", p=128), r=[self.w_mlp1], w=[W1[fp % 4]])
            with k.scope():
                Wout = k.sb("Wout", [128, 8, D], BF16)
                k.dma("sp", Wout[:, 0:4, :], self.wout_bf[l, :, 0:4, :], r=[self.wout_bf], w=[Wout])
                k.dma("act", Wout[:, 4:8, :], self.wout_bf[l, :, 4:8, :], r=[self.wout_bf], w=[Wout])
                for fp in range(4):
                    load_w1(fp)
                Xs = [k.sb("X%d" % i, [128, 8, 512]) for i in range(3)]
                Ms = [k.sb("M%d" % i, [128, 8, 512], BF16) for i in range(3)]
                Ys = [k.sb("Y%d" % i, [128, 8, 512]) for i in range(2)]
                W = {"SQ": k.sb("SQ", [128, 8, 512], BF16), "RS": k.sb("RS", [128, 512]), "TMP": k.sb("TMP", [128, 512]),
                     "XN": k.sb("XN", [128, 8, 512])}

                def loads3(bi):
                    if bi < len(blks):
                        t0, n, mj = blks[bi]
                        k.dma("sp", Xs[bi % 3][:, :, :n], self.xsl(xsrc, t0, n), r=[xsrc[0]], w=[Xs[bi % 3]])
                        k.dma("act", Ms[bi % 3][:, :, :n], self.mixT[:, t0:t0 + n].rearrange("(kt p) t -> p kt t", p=128), r=[self.mixT], w=[Ms[bi % 3]])

                def stage_mm(bi):
                    t0, n, mj = blks[bi]
                    X, M, Y = Xs[bi % 3], Ms[bi % 3], Ys[bi % 2]
                    loads3(bi + 1)
                    for ot in range(8):
                        bk = self.bank()
                        self.mm(self.pap(bk, n), [self.psb[bk]],
                                [(Wout[:, kt, ot * 128:(ot + 1) * 128], M[:, kt, :n]) for kt in range(8)], [Wout, M])
                        if ot % 2 == 0:
                            k.op("act", lambda bk=bk, ot=ot: nc.scalar.copy(out=Y[:, ot, :n], in_=self.pap(bk, n)), r=[self.psb[bk]], w=[Y])
                        else:
                            k.op("dve", lambda bk=bk, ot=ot: nc.vector.tensor_copy(out=Y[:, ot, :n], in_=self.pap(bk, n)), r=[self.psb[bk]], w=[Y])
                        yield

                def stage_post(bi):
                    t0, n, mj = blks[bi]
                    j = b if mj is None else mj
                    X, Y = Xs[bi % 3], Ys[bi % 2]
                    yield from self.r_post_g(Y, X, n, l, 2, j, W)
                    k.dma("sp", self.xA[:, t0:t0 + n].rearrange("(kt p) t -> p kt t", p=128), X[:, :, :n], r=[X], w=[self.xA])
                    yield from self.r_norm_g(X, n, l, 3, 4, j, H2, slice(t0, t0 + n), W, split=True)

                loads3(0)
                for _ in stage_mm(0):
                    pass
                for bi in range(len(blks)):
                    nxt = stage_mm(bi + 1) if bi + 1 < len(blks) else None
                    if nxt is None:
                        for _ in stage_post(bi):
                            pass
                    else:
                        self.weave(nxt, stage_post(bi), 4)
            with k.scope():
                HID = k.sb("HID", [128, 32, T], BF16)
                with k.scope():
                    RT = [k.sb("RT%d" % i, [128, 512]) for i in range(2)]
                    ri = 0
                    for fp in range(16):
                        w1 = W1[fp % 4]
                        if fp >= 1:
                            load_w1(fp + 3)
                        for fs in range(2):
                            f = fp * 2 + fs
                            for (t0, n, mj) in blks:
                                bk = self.bank()
                                self.mm(self.pap(bk, n), [self.psb[bk]],
                                        [(w1[:, kt, fs * 128:(fs + 1) * 128], H2[:, kt, t0:t0 + n]) for kt in range(8)], [w1, H2])
                                rt = RT[ri % 2]
                                ri += 1
                                k.op("act", lambda bk=bk, rt=rt, n=n: nc.scalar.activation(out=rt[:, :n], in_=self.pap(bk, n), func=AF.Relu), r=[self.psb[bk]], w=[rt])
                                k.op("dve", lambda rt=rt, f=f, t0=t0, n=n: nc.vector.tensor_tensor(out=HID[:, f, t0:t0 + n], in0=rt[:, :n], in1=rt[:, :n], op=ALU.mult), r=[rt], w=[HID])
                with k.scope():
                    YS = [k.sb("YS%d" % i, [128, 512]) for i in range(2)]
                    yi = 0
                    for ot in range(8):
                        w2 = W2[ot % 2]
                        k.dma("pool", w2[:, :, :], self.w_mlp2[l, :, ot * 128:(ot + 1) * 128].rearrange("(ft p) c -> p ft c", p=128), r=[self.w_mlp2], w=[w2])
                        for (t0, n, mj) in blks:
                            bk = self.bank()
                            self.mm(self.pap(bk, n), [self.psb[bk]], [(w2[:, f, :], HID[:, f, t0:t0 + n]) for f in range(32)], [w2, HID])
                            ys = YS[yi % 2]
                            if yi % 2 == 0:
                                k.op("act", lambda bk=bk, ys=ys, n=n: nc.scalar.copy(out=ys[:, :n], in_=self.pap(bk, n)), r=[self.psb[bk]], w=[ys])
                            else:
                                k.op("dve", lambda bk=bk, ys=ys, n=n: nc.vector.tensor_copy(out=ys[:, :n], in_=self.pap(bk, n)), r=[self.psb[bk]], w=[ys])
                            yi += 1
                            k.dma("sp", self.yscr[ot * 128:(ot + 1) * 128, t0:t0 + n], ys[:, :n], r=[ys], w=[self.yscr])
        if not last:
            return
        with k.scope():
            nb_ = len(blks)
            Xs = [k.sb("X%d" % i, [128, 8, 512]) for i in range(nb_)]
            Ys = [k.sb("Y%d" % i, [128, 8, 512]) for i in range(nb_)]
            Ws = [{"SQ": k.sb("SQ", [128, 8, 512], BF16), "RS": k.sb("RS", [128, 512]), "TMP": k.sb("TMP", [128, 512]),
                   "XN": k.sb("XN", [128, 8, 512])} for _ in range(2)]
            for bi, (t0, n, mj) in enumerate(blks):
                k.dma("sp", Xs[bi][:, :, :n], self.xA[:, t0:t0 + n].rearrange("(kt p) t -> p kt t", p=128), r=[self.xA], w=[Xs[bi]])
                k.dma("act", Ys[bi][:, :, :n], self.yscr[:, t0:t0 + n].rearrange("(kt p) t -> p kt t", p=128), r=[self.yscr], w=[Ys[bi]])

            def chain(bi):
                t0, n, mj = blks[bi]
                j = b if mj is None else mj
                yield from self.r_post_g(Ys[bi], Xs[bi], n, l, 5, j, Ws[bi % 2])
                dst = self.out[b, :, t0 - CTX:t0 - CTX + n].rearrange("(kt p) t -> p kt t", p=128)
                k.dma("sp", dst, Xs[bi][:, :, :n], r=[Xs[bi]], w=[self.out])
            for bi in range(0, nb_, 2):
                self.weave(chain(bi), chain(bi + 1) if bi + 1 < nb_ else None, 1)
            if self.dbg and "d_x1" in self.dbg and b == 0 and l == 0:
                self.dump("d_x1", self.xB, [D, T], F32)


OFF = dict(q=0, k=256, v=384, s5=512, nq=768, nk=1024, nv=1280, fn=1536)
SWAP64 = np.concatenate([np.arange(16, 32), np.arange(0, 16), np.arange(48, 64), np.arange(32, 48)])


def _bf16(a):
    import ml_dtypes
    return np.asarray(a, dtype=np.float32).astype(ml_dtypes.bfloat16)


def host_consts():
    c = {}
    pos = np.arange(S)
    row, colp = pos // 64, pos % 64
    inv = 10000.0 ** (-np.arange(16, dtype=np.float32) / 16.0)
    cos = np.ones((64, T), np.float32)
    sin = np.zeros((64, T), np.float32)
    for d in range(64):
        p = row if d < 32 else colp
        ang = p.astype(np.float32) * inv[d % 16]
        cos[d, CTX:] = np.cos(ang)
        sgn = -1.0 if (d % 32) < 16 else 1.0
        sin[d, CTX:] = sgn * np.sin(ang)
    c["rope"] = np.stack([np.concatenate([cos, cos], 0), np.concatenate([sin, sin], 0)]).astype(np.float32)
    eye = np.eye(128, dtype=np.float32)
    bones = np.kron(np.eye(2, dtype=np.float32), np.ones((64, 64), np.float32))
    cc = np.arange(64)
    ang = 2 * np.pi * ((cc[:, None] * cc[None, :]) % 64) / 64.0
    Cc = np.kron(np.eye(2), np.cos(ang))
    Sc = np.kron(np.eye(2), np.sin(ang))
    c["cbf"] = _bf16(np.stack([eye, eye[::-1], np.ones((128, 128)), bones, Cc, -Sc], 1))
    for nm, N in (("dftn", S), ("dftc", CTX)):
        n = np.arange(N, dtype=np.int64)
        a = 2 * np.pi * ((n[:, None] * n[None, :]) % N).astype(np.float64) / N
        c[nm] = _bf16(np.stack([np.cos(a), np.sin(a)]))
    return c


def host_prep(inp, core):
    f = lambda a: np.ascontiguousarray(np.asarray(a, dtype=np.float32))
    m = {}
    bs = slice(core * NB, (core + 1) * NB)
    x, ctx = f(inp["x"])[bs], f(inp["ctx"])[bs]
    m["xin"] = np.ascontiguousarray(np.concatenate([ctx, x], axis=1).transpose(0, 2, 1))
    cv = np.concatenate([f(inp["c"])[bs], f(inp["c_ctx"])[None]], 0)
    m["cT"] = np.ascontiguousarray(cv.reshape(3, 8, 128).transpose(2, 1, 0))
    return m


def host_nbias(rel_bias):
    rb = np.asarray(rel_bias, np.float32)
    out = np.empty((DEPTH, 4, 20, 128, 512), np.float32)
    a_l = np.arange(2)[:, None, None, None]
    kk = np.arange(64)[None, :, None, None]
    r_l = np.arange(8)[None, None, :, None]
    jj = np.arange(64)[None, None, None, :]
    cs = np.clip(jj - 8, 0, 48)
    for qb in range(4):
        for m in Prog.NA_TILES[qb]:
            if qb == 2:
                continue
            a = 2 * m + a_l
            r = 8 * qb + r_l
            rs = np.clip(r - 4, 0, 24)
            valid = (a >= rs) & (a < rs + 8) & (kk >= cs) & (kk < cs + 16)
            ia = np.clip(a - r + 7, 0, 14) + 0 * kk + 0 * jj
            ik = np.clip(kk - jj + 15, 0, 30) + 0 * a_l + 0 * r_l
            valid = np.broadcast_to(valid, (2, 64, 8, 64))
            g = rb[:, :, np.broadcast_to(ia, (2, 64, 8, 64)), np.broadcast_to(ik, (2, 64, 8, 64))]
            g = np.where(valid[None, None], g, np.float32(-30000.0))
            out[:, :, Prog.na_tidx(qb, m)] = g.reshape(DEPTH, 4, 128, 512)
    return out


def host_s5(inp):
    f = lambda a: np.asarray(a, dtype=np.float32)
    L = DEPTH
    m = {}
    are, aim, ldt = f(inp["s5_a_re"]), f(inp["s5_a_im"]), f(inp["s5_log_dt"])
    p3 = np.stack([are, aim, np.broadcast_to(ldt[..., None], are.shape)], -1)
    p3 = p3.reshape(L, 32, 64, 3).transpose(0, 2, 1, 3)
    m["s5p"] = np.ascontiguousarray(np.concatenate([p3, p3], 1))
    cre, cim = f(inp["s5_c_re"]), f(inp["s5_c_im"])
    cc = np.stack([cre, cim], 3).reshape(L, 32, 2, 16, 64).transpose(0, 4, 1, 2, 3)
    m["s5c"] = np.ascontiguousarray(np.concatenate([cc, cc], 1))
    bre, bim = f(inp["s5_b_re"]), f(inp["s5_b_im"])
    bb = np.stack([bre, bim], 4).reshape(L, 32, 64, 2, 16).transpose(0, 2, 1, 3, 4)
    m["s5b"] = np.ascontiguousarray(np.concatenate([bb, bb], 1))
    dsk = f(inp["s5_d"]).reshape(L, 16, 16)
    m["s5d"] = np.ascontiguousarray(np.tile(dsk.transpose(0, 2, 1), (1, 8, 1)))
    cf = np.zeros((128, 392), np.float32)
    cf[:64, 0] = 1.0
    cf[64:, 1] = 1.0
    cf[:64, 2] = -1.0
    cf[64:, 3] = -1.0
    cf[:, 4] = np.pi / 2
    cf[:, 5] = EPS
    cf[:, 6] = 1.0
    jj = np.arange(128) // 16
    cf[:, 8:136] = (jj[:, None] <= jj[None, :])
    cf[:, 136:264] = (jj[:, None] >= jj[None, :])
    cf[:, 264:392] = np.eye(128)
    m["cf32"] = cf
    m["w_glu"] = np.ascontiguousarray(f(inp["w_s5_glu"]))
    return m

def host_shared(inp):
    f = lambda a: np.ascontiguousarray(np.asarray(a, dtype=np.float32))
    m = dict(host_consts())
    m["w_ada"] = f(inp["w_ada"])
    m["bada"] = np.ascontiguousarray(f(inp["b_ada"]).reshape(DEPTH, 48, 128).transpose(0, 2, 1))
    g4 = np.stack([f(inp[n]) for n in ("g_pre_mix", "g_post_mix", "g_pre_mlp", "g_post_mlp")], 1)
    m["gvec"] = np.ascontiguousarray(g4.reshape(DEPTH, 4, 8, 128).transpose(0, 3, 1, 2))
    w_in = f(inp["w_in"])
    sw4 = np.concatenate([SWAP64 + 64 * h for h in range(4)])
    sw2 = np.concatenate([SWAP64 + 64 * h for h in range(2)])
    cols_fm = np.concatenate([OFF["q"] + np.arange(256), OFF["q"] + sw4, OFF["k"] + np.arange(128), OFF["k"] + sw2,
                              OFF["nq"] + np.arange(256), OFF["nk"] + np.arange(256)])
    cols_tm = np.concatenate([OFF["v"] + np.arange(128), OFF["nv"] + np.arange(256), OFF["s5"] + np.arange(256), OFF["fn"] + np.arange(256)])
    m["w_fm"] = np.ascontiguousarray(w_in[:, :, cols_fm])
    m["w_tm"] = np.ascontiguousarray(w_in[:, :, cols_tm])
    gq, gk = f(inp["g_q_attn"]), f(inp["g_k_attn"])
    m["gqk"] = np.ascontiguousarray(np.stack([np.tile(gq, (1, 2)), np.tile(gq[:, SWAP64], (1, 2)),
                                              np.tile(gk, (1, 2)), np.tile(gk[:, SWAP64], (1, 2))], 2))
    m["nbias"] = host_nbias(inp["na_rel_bias"])
    m.update(host_s5(inp))
    m["w_fnet"] = f(inp["w_fnet"])
    m["bfn"] = np.ascontiguousarray(f(inp["b_fnet"]).reshape(DEPTH, 2, 128).transpose(0, 2, 1))
    m["w_out"] = f(inp["w_out"])
    m["w_mlp1"] = f(inp["w_mlp1"])
    m["w_mlp2"] = f(inp["w_mlp2"])
    return m


def run_prog(prog, inp, cores):
    shared = host_shared(inp)
    in_maps = []
    for c in cores:
        m = dict(shared)
        m.update(host_prep(inp, c))
        in_maps.append(m)
    return run_bass_kernel_spmd(prog.nc, in_maps, core_ids=list(range(len(cores))))


def kernel(**inputs):
    prog = Prog()
    res = run_prog(prog, inputs, list(range(8)))
    out = np.empty((8 * NB, S, D), np.float32)
    for c in range(8):
        o = res.results[c]["outT"]
        for b in range(NB):
            out[c * NB + b] = o[b].T
    return out
```
